# Optimizing a Trainium2 kernel written in Bass

```python
import jax
import jax.numpy as jnp
from jax import lax
import numpy as np

D_MODEL = 2048
BATCH = 32
SEQ = 256
DEPTH = 2
DEC_BATCH = 4
DEC_SEQ = 4096
PAST_LEN = 512

GRID_W = 64
N_AB_LAYERS = (DEPTH + 1) // 2
N_C_LAYERS = DEPTH // 2
RMS_EPS = 1e-6

RET_HEADS = 8
RET_DK = 64
RET_DV = 128
RET_CHUNK = 128

RWKV_HEADS = 16
RWKV_N = 64
RWKV_W = RWKV_HEADS * RWKV_N
W_RANK = 64
A_RANK = 64
G_RANK = 128
DECAY_SCALE = 0.606531
RWKV_LN_EPS = 64e-5
L2_EPS = 1e-12

ATT_HEADS = 16
ATT_KV_HEADS = 4
ATT_HD = 128
ATT_GROUP = ATT_HEADS // ATT_KV_HEADS
ROPE_AXIS_DIM = ATT_HD // 2
ROPE_THETA = 10000.0
Q_BLOCK = 128

FFN_HIDDEN = ((8 * D_MODEL + 3 * 256 - 1) // (3 * 256)) * 256

A_QK = RET_HEADS * RET_DK
A_V = RET_HEADS * RET_DV
A_IN = 2 * A_QK + 2 * A_V
B_IN = 3 * RWKV_W + G_RANK + 2 * W_RANK + 2 * A_RANK
AB_IN = A_IN + B_IN
AB_MIX = A_V + RWKV_W
KV_W = ATT_KV_HEADS * ATT_HD
C_MIX = ATT_HEADS * ATT_HD
C_IN = C_MIX + 2 * KV_W

kernel_name = 'hybrid_retention_rwkv7_gqa_diffusion_step'


def rms_norm(x, g=None, eps=RMS_EPS):
    xf = x.astype(jnp.float32)
    y = xf * lax.rsqrt(jnp.mean(xf * xf, axis=-1, keepdims=True) + eps)
    if g is not None:
        y = y * g.astype(jnp.float32)
    return y.astype(x.dtype)


def group_norm(x, w, b, eps):
    H, N = x.shape[-2:]
    xf = x.astype(jnp.float32)
    mu = jnp.mean(xf, axis=-1, keepdims=True)
    var = jnp.mean(jnp.square(xf - mu), axis=-1, keepdims=True)
    y = (xf - mu) * lax.rsqrt(var + eps)
    return (y * w.reshape(H, N).astype(jnp.float32) + b.reshape(H, N).astype(jnp.float32)).astype(x.dtype)


def adaln(cond, w, b):
    m = jax.nn.silu(cond) @ w + b
    return jnp.split(m[:, None, :], 6, axis=-1)


def flip_seq(t):
    return jnp.flip(t, axis=1)


def centred_conv3(x, w):
    xp = jnp.pad(x, ((0, 0), (1, 1), (0, 0)))
    return xp[:, :-2] * w[0] + xp[:, 1:-1] * w[1] + xp[:, 2:] * w[2]


def retention_scan(q, k, v, log_gamma, s0):
    B, L, H, DK = q.shape
    DV = v.shape[-1]
    C = RET_CHUNK
    nc = L // C
    idx = jnp.arange(C, dtype=jnp.float32)
    diff = idx[:, None] - idx[None, :]
    lg = log_gamma.astype(jnp.float32)
    decay_mask = jnp.where(diff >= 0, jnp.exp(lg[:, None, None] * jnp.maximum(diff, 0.0)), 0.0)
    xi = jnp.exp(lg[None, :] * (idx[:, None] + 1.0))[None, :, :, None]
    zeta = jnp.exp(lg[None, :] * (C - 1.0 - idx[:, None]))[None, :, :, None]
    gamma_c = jnp.exp(lg * C)[None, :, None, None]

    def to_chunks(t):
        return t.reshape(B, nc, C, H, t.shape[-1]).transpose(1, 0, 2, 3, 4).astype(jnp.float32)

    def step(s, inp):
        qi, ki, vi = inp
        scores = jnp.einsum('bnhd,bmhd->bhnm', qi, ki) * decay_mask
        inner = jnp.einsum('bhnm,bmhe->bnhe', scores, vi)
        cross = jnp.einsum('bnhd,bhde->bnhe', qi, s) * xi
        s_new = gamma_c * s + jnp.einsum('bmhd,bmhe->bhde', ki * zeta, vi)
        return s_new, inner + cross

    s_fin, out = lax.scan(step, s0.astype(jnp.float32), (to_chunks(q), to_chunks(k), to_chunks(v)))
    out = out.transpose(1, 0, 2, 3, 4).reshape(B, L, H, DV).astype(v.dtype)
    return out, s_fin


def rwkv7_scan(r, w, k, v, kk, a, s0):
    def step(s, inp):
        rt, wt, kt, vt, kkt, at = inp
        removed = jnp.einsum('bhij,bhj->bhi', s, -kkt)
        s = s * wt[:, :, None, :] + removed[..., None] * (kkt * at)[:, :, None, :] + vt[..., None] * kt[:, :, None, :]
        y = jnp.einsum('bhij,bhj->bhi', s, rt)
        return s, y

    xs = tuple(t.astype(jnp.float32).transpose(1, 0, 2, 3) for t in (r, w, k, v, kk, a))
    s_fin, y = lax.scan(step, s0.astype(jnp.float32), xs)
    return y.transpose(1, 0, 2, 3).astype(v.dtype), s_fin


def ab_mixer(h, j, P, init):
    B, L, _ = h.shape
    proj = h @ P['ab_w_in'][j]
    pa, pb = proj[..., :A_IN], proj[..., A_IN:]

    q, k, v, g = jnp.split(pa, [A_QK, 2 * A_QK, 2 * A_QK + A_V], axis=-1)
    q = q.reshape(B, L, RET_HEADS, RET_DK)
    k = k.reshape(B, L, RET_HEADS, RET_DK) * (RET_DK ** -0.5)
    v = v.reshape(B, L, RET_HEADS, RET_DV)
    log_gamma = -jnp.exp(P['ret_log_decay'][j].astype(jnp.float32))
    o_f, s_rf = retention_scan(q, k, v, log_gamma[0], init[0])
    o_b, s_rb = retention_scan(flip_seq(q), flip_seq(k), flip_seq(v), log_gamma[1], init[1])
    o_ret = rms_norm(o_f + flip_seq(o_b)).reshape(B, L, A_V) * jax.nn.silu(g)

    pb = centred_conv3(pb, P['rwkv_conv_w'][j])
    r, kb, vb, gc, wc, ac = jnp.split(pb, [RWKV_W, 2 * RWKV_W, 3 * RWKV_W, 3 * RWKV_W + G_RANK,
                                           3 * RWKV_W + G_RANK + 2 * W_RANK], axis=-1)

    def heads(t):
        return t.reshape(B, L, RWKV_HEADS, RWKV_N)

    r, kb, vb = heads(r), heads(kb), heads(vb)
    wc = wc.reshape(B, L, 2, W_RANK)
    ac = ac.reshape(B, L, 2, A_RANK)
    kkf = (kb * P['rwkv_k_k'][j].reshape(RWKV_HEADS, RWKV_N)).astype(jnp.float32)
    kk = (kkf / jnp.maximum(jnp.sqrt(jnp.sum(kkf * kkf, axis=-1, keepdims=True)), L2_EPS)).astype(kb.dtype)
    k_a = P['rwkv_k_a'][j].reshape(RWKV_HEADS, RWKV_N)
    r_k = P['rwkv_r_k'][j]
    ys, finals, bonuses = [], [], []
    for d in range(2):
        wd = heads(jnp.exp(-DECAY_SCALE * jax.nn.sigmoid(
            (P['rwkv_w0'][j, d] + jnp.tanh(wc[:, :, d]) @ P['rwkv_w_up'][j, d]).astype(jnp.float32))))
        ad = heads(jax.nn.sigmoid(P['rwkv_a0'][j, d] + ac[:, :, d] @ P['rwkv_a_up'][j, d]))
        kd = kb * (1.0 + (ad - 1.0) * k_a)
        seqs = (r, wd, kd, vb, kk, ad)
        if d == 1:
            seqs = tuple(flip_seq(t) for t in seqs)
        y, s_fin = rwkv7_scan(*seqs, init[2 + d])
        if d == 1:
            y = flip_seq(y)
        ys.append(y)
        finals.append(s_fin)
        bonuses.append(jnp.sum(r * kd * r_k, axis=-1, keepdims=True) * vb)
    y = group_norm(ys[0] + ys[1], P['rwkv_ln_w'][j], P['rwkv_ln_b'][j], RWKV_LN_EPS) + bonuses[0] + bonuses[1]
    g_b = jax.nn.sigmoid(gc) @ P['rwkv_g_up'][j]
    o_rwkv = y.reshape(B, L, RWKV_W) * g_b

    out = jnp.concatenate([o_ret, o_rwkv], axis=-1) @ P['ab_w_out'][j]
    return out, (s_rf, s_rb, finals[0], finals[1])


def gqa_qkv(h, w_in, q_g, k_g):
    B, L, _ = h.shape
    q, k, v = jnp.split(h @ w_in, [C_MIX, C_MIX + KV_W], axis=-1)
    q = rms_norm(q.reshape(B, L, ATT_HEADS, ATT_HD), q_g)
    k = rms_norm(k.reshape(B, L, ATT_KV_HEADS, ATT_HD), k_g)
    v = v.reshape(B, L, ATT_KV_HEADS, ATT_HD)
    return q, k, v


def axial_rope(x):
    L = x.shape[1]
    rows = L // GRID_W
    row = jnp.repeat(jnp.arange(rows, dtype=jnp.float32), GRID_W)
    col = jnp.tile(jnp.arange(GRID_W, dtype=jnp.float32), rows)
    freqs = jnp.power(ROPE_THETA, -jnp.arange(0, ROPE_AXIS_DIM, 2, dtype=jnp.float32) / ROPE_AXIS_DIM)

    def rotate(xa, pos):
        ang = pos[:, None] * freqs[None, :]
        cos = jnp.cos(ang)[None, :, None, :]
        sin = jnp.sin(ang)[None, :, None, :]
        x1, x2 = jnp.split(xa.astype(jnp.float32), 2, axis=-1)
        return jnp.concatenate([x1 * cos - x2 * sin, x2 * cos + x1 * sin], axis=-1)

    out = jnp.concatenate([rotate(x[..., :ROPE_AXIS_DIM], row), rotate(x[..., ROPE_AXIS_DIM:], col)], axis=-1)
    return out.astype(x.dtype)


def block_attention(q, k, v):
    B, Lq = q.shape[:2]
    nb = Lq // Q_BLOCK
    qb = q.reshape(B, nb, Q_BLOCK, ATT_KV_HEADS, ATT_GROUP, ATT_HD).transpose(1, 0, 2, 3, 4, 5)
    kf = k.astype(jnp.float32)
    vf = v.astype(jnp.float32)
    scale = ATT_HD ** -0.5

    def one_block(qi):
        s = jnp.einsum('bqhgd,bkhd->bhgqk', qi.astype(jnp.float32), kf) * scale
        p = jax.nn.softmax(s, axis=-1)
        return jnp.einsum('bhgqk,bkhd->bqhgd', p, vf).astype(q.dtype)

    o = lax.map(one_block, qb)
    return o.transpose(1, 0, 2, 3, 4, 5).reshape(B, Lq, C_MIX)


def swiglu(h, wg, wu, wd):
    return (jax.nn.silu(h @ wg) * (h @ wu)) @ wd


def trunk(x, cond, P, cache):
    ctx_mode = cache is None
    B = x.shape[0]
    new = {'ret_f': [], 'ret_b': [], 'rwkv_f': [], 'rwkv_b': [], 'k': [], 'v': []}
    for i in range(DEPTH):
        j = i // 2
        sh_m, sc_m, g_m, sh_f, sc_f, g_f = adaln(cond, P['ada_w'][i], P['ada_b'][i])
        ng = P['norm_g'][i]
        h = rms_norm(x, ng[0]) * (1.0 + sc_m) + sh_m
        if i % 2 == 0:
            if ctx_mode:
                z_ret = jnp.zeros((B, RET_HEADS, RET_DK, RET_DV), jnp.float32)
                z_rwkv = jnp.zeros((B, RWKV_HEADS, RWKV_N, RWKV_N), jnp.float32)
                init = (z_ret, z_ret, z_rwkv, z_rwkv)
            else:
                init = (cache['ret_f'][:, j], cache['ret_b'][:, j], cache['rwkv_f'][:, j], cache['rwkv_b'][:, j])
            y, fin = ab_mixer(h, j, P, init)
            if ctx_mode:
                for name, s in zip(('ret_f', 'ret_b', 'rwkv_f', 'rwkv_b'), fin):
                    new[name].append(s)
        else:
            q, k, v = gqa_qkv(h, P['c_w_in'][j], P['c_q_norm'][j], P['c_k_norm'][j])
            if ctx_mode:
                o = block_attention(q, k, v)
                new['k'].append(k)
                new['v'].append(v)
            else:
                keys = jnp.concatenate([cache['k'][:, j].astype(k.dtype), axial_rope(k)], axis=1)
                vals = jnp.concatenate([cache['v'][:, j].astype(v.dtype), v], axis=1)
                o = block_attention(axial_rope(q), keys, vals)
            y = o @ P['c_w_out'][j]
        x = x + g_m * rms_norm(y, ng[1])
        h = rms_norm(x, ng[2]) * (1.0 + sc_f) + sh_f
        x = x + g_f * rms_norm(swiglu(h, P['ffn_w_gate'][i], P['ffn_w_up'][i], P['ffn_w_down'][i]), ng[3])
    return x, new


def setup_inputs(seed: int = 0) -> dict:
    key = jax.random.key(seed)
    ks = iter(jax.random.split(key, 40))

    def nrm(shape, s=1.0):
        return s * jax.random.normal(next(ks), shape, jnp.float32)

    D = D_MODEL
    ret_base = jnp.log(-jnp.log(1.0 - jnp.exp2(-5.0 - jnp.arange(RET_HEADS, dtype=jnp.float32))))
    conv_base = jnp.array([[0.25], [0.5], [0.25]], jnp.float32)
    return {
        'x_prompt': nrm((BATCH, SEQ, D)),
        'x_sample': nrm((DEC_BATCH, DEC_SEQ, D)),
        'state_ret_fwd': nrm((DEC_BATCH, N_AB_LAYERS, RET_HEADS, RET_DK, RET_DV)),
        'state_ret_bwd': nrm((DEC_BATCH, N_AB_LAYERS, RET_HEADS, RET_DK, RET_DV)),
        'state_rwkv_fwd': nrm((DEC_BATCH, N_AB_LAYERS, RWKV_HEADS, RWKV_N, RWKV_N)),
        'state_rwkv_bwd': nrm((DEC_BATCH, N_AB_LAYERS, RWKV_HEADS, RWKV_N, RWKV_N)),
        'cache_k': nrm((DEC_BATCH, N_C_LAYERS, PAST_LEN, ATT_KV_HEADS, ATT_HD)),
        'cache_v': nrm((DEC_BATCH, N_C_LAYERS, PAST_LEN, ATT_KV_HEADS, ATT_HD)),
        'c': nrm((DEC_BATCH, D)),
        'c_ctx': nrm((D,)),
        'ada_w': nrm((DEPTH, D, 6 * D), 0.5 * D ** -0.5),
        'ada_b': nrm((DEPTH, 6 * D), 0.01),
        'norm_g': 1.0 + nrm((DEPTH, 4, D), 0.05),
        'ffn_w_gate': nrm((DEPTH, D, FFN_HIDDEN), D ** -0.5),
        'ffn_w_up': nrm((DEPTH, D, FFN_HIDDEN), D ** -0.5),
        'ffn_w_down': nrm((DEPTH, FFN_HIDDEN, D), FFN_HIDDEN ** -0.5),
        'ab_w_in': nrm((N_AB_LAYERS, D, AB_IN), D ** -0.5),
        'ab_w_out': nrm((N_AB_LAYERS, AB_MIX, D), AB_MIX ** -0.5),
        'ret_log_decay': ret_base + nrm((N_AB_LAYERS, 2, RET_HEADS), 0.05),
        'rwkv_conv_w': conv_base + nrm((N_AB_LAYERS, 3, B_IN), 0.1),
        'rwkv_w0': nrm((N_AB_LAYERS, 2, RWKV_W), 1.5) - 0.5,
        'rwkv_w_up': nrm((N_AB_LAYERS, 2, W_RANK, RWKV_W), 0.1),
        'rwkv_a0': nrm((N_AB_LAYERS, 2, RWKV_W), 0.5),
        'rwkv_a_up': nrm((N_AB_LAYERS, 2, A_RANK, RWKV_W), 0.1),
        'rwkv_g_up': nrm((N_AB_LAYERS, G_RANK, RWKV_W), G_RANK ** -0.5),
        'rwkv_k_k': 1.0 + nrm((N_AB_LAYERS, RWKV_W), 0.1),
        'rwkv_k_a': 1.0 + nrm((N_AB_LAYERS, RWKV_W), 0.1),
        'rwkv_r_k': nrm((N_AB_LAYERS, RWKV_HEADS, RWKV_N), 0.1),
        'rwkv_ln_w': 1.0 + nrm((N_AB_LAYERS, RWKV_W), 0.05),
        'rwkv_ln_b': nrm((N_AB_LAYERS, RWKV_W), 0.01),
        'c_w_in': nrm((N_C_LAYERS, D, C_IN), D ** -0.5),
        'c_w_out': nrm((N_C_LAYERS, C_MIX, D), C_MIX ** -0.5),
        'c_q_norm': 1.0 + nrm((N_C_LAYERS, ATT_HD), 0.05),
        'c_k_norm': 1.0 + nrm((N_C_LAYERS, ATT_HD), 0.05),
    }


def reference(x_prompt, x_sample, state_ret_fwd, state_ret_bwd, state_rwkv_fwd, state_rwkv_bwd, cache_k, cache_v,
              c, c_ctx, ada_w, ada_b, norm_g, ffn_w_gate, ffn_w_up, ffn_w_down, ab_w_in, ab_w_out, ret_log_decay,
              rwkv_conv_w, rwkv_w0, rwkv_w_up, rwkv_a0, rwkv_a_up, rwkv_g_up, rwkv_k_k, rwkv_k_a, rwkv_r_k,
              rwkv_ln_w, rwkv_ln_b, c_w_in, c_w_out, c_q_norm, c_k_norm):
    P = {
        'ada_w': ada_w, 'ada_b': ada_b, 'norm_g': norm_g,
        'ffn_w_gate': ffn_w_gate, 'ffn_w_up': ffn_w_up, 'ffn_w_down': ffn_w_down,
        'ab_w_in': ab_w_in, 'ab_w_out': ab_w_out, 'ret_log_decay': ret_log_decay,
        'rwkv_conv_w': rwkv_conv_w, 'rwkv_w0': rwkv_w0, 'rwkv_w_up': rwkv_w_up, 'rwkv_a0': rwkv_a0,
        'rwkv_a_up': rwkv_a_up, 'rwkv_g_up': rwkv_g_up, 'rwkv_k_k': rwkv_k_k, 'rwkv_k_a': rwkv_k_a,
        'rwkv_r_k': rwkv_r_k, 'rwkv_ln_w': rwkv_ln_w, 'rwkv_ln_b': rwkv_ln_b,
        'c_w_in': c_w_in, 'c_w_out': c_w_out, 'c_q_norm': c_q_norm, 'c_k_norm': c_k_norm,
    }
    y_prompt, st = trunk(x_prompt, c_ctx[None, :], P, None)
    cache = {'ret_f': state_ret_fwd, 'ret_b': state_ret_bwd, 'rwkv_f': state_rwkv_fwd, 'rwkv_b': state_rwkv_bwd,
             'k': cache_k, 'v': cache_v}
    y_sample, _ = trunk(x_sample, c, P, cache)
    dt = x_prompt.dtype
    new_ret_f = jnp.stack(st['ret_f'], axis=1).astype(dt)
    new_ret_b = jnp.stack(st['ret_b'], axis=1).astype(dt)
    new_rwkv_f = jnp.stack(st['rwkv_f'], axis=1).astype(dt)
    new_rwkv_b = jnp.stack(st['rwkv_b'], axis=1).astype(dt)
    new_k = jnp.stack(st['k'], axis=1).astype(dt)
    new_v = jnp.stack(st['v'], axis=1).astype(dt)
    return (y_prompt, y_sample, new_ret_f, new_ret_b, new_rwkv_f, new_rwkv_b, new_k, new_v)
```

```python
import contextlib
import numpy as np
import concourse.bass as bass
import concourse.mybir as mybir
from concourse.bass_utils import run_bass_kernel_spmd

F32 = mybir.dt.float32
BF16 = mybir.dt.bfloat16
AF = mybir.ActivationFunctionType
ALU = mybir.AluOpType
AX = mybir.AxisListType

D = 2048
KC = 16
FFN = 5632
FC = 44
RMS_EPS = 1e-6
A_IN = 3072
B_IN = 3456
AB_IN = 6528


class Cfg:
    def __init__(self, NP=4, LP=256, LS=4096, PL=512, stages=99, debug=False):
        self.NP, self.LP, self.LS, self.PL = NP, LP, LS, PL
        self.TP = NP * LP
        self.TT = self.TP + LS
        self.stages = stages
        self.debug = debug
        self.zf32 = True


class Res:
    __slots__ = ("name", "w", "r", "multi", "ws")

    def __init__(self, name, multi=False):
        self.name = name
        self.w = None
        self.r = []
        self.multi = multi
        self.ws = {}


class Q:
    def __init__(self, name, eng, sem, ring):
        self.name, self.eng, self.sem, self.ring = name, eng, sem, ring
        self.cnt = 0
        self.seen = {}
        self.ring_tot = [0] * len(ring)
        self.ring_i = 0
        self.ring_limit = min(2, len(ring)) if name == "pool" else len(ring)


class Sy:
    def __init__(self, nc, es):
        self.nc = nc
        self.q = {}
        for name, eng, nring in (("pe", nc.tensor, 0), ("dve", nc.vector, 0), ("act", nc.scalar, 6),
                                 ("pool", nc.gpsimd, 6), ("sp", nc.sync, 12)):
            sem = es.enter_context(nc.semaphore("c_" + name))
            ring = [es.enter_context(nc.semaphore(f"d_{name}{i}")) for i in range(nring)]
            self.q[name] = Q(name, eng, sem, ring)
        self.semid = {}

    def _sid(self, sem):
        return id(sem)

    def _wait(self, q, evs, same_ok):
        need = {}
        for ev in evs:
            if ev is None:
                continue
            sem, val = ev
            if sem is q.sem and same_ok:
                continue
            k = id(sem)
            if k not in need or need[k][1] < val:
                need[k] = (sem, val)
        for k, (sem, val) in need.items():
            if q.seen.get(k, 0) < val:
                q.eng.wait_ge(sem, val)
                q.seen[k] = val

    def _deps(self, reads, writes):
        evs = []
        for r in reads:
            evs.append(r.w)
            if r.multi:
                evs.extend(r.ws.values())
        for w in writes:
            if not w.multi:
                evs.append(w.w)
            evs.extend(w.r)
        return evs

    def _commit(self, ev, reads, writes):
        for r in reads:
            r.r.append(ev)
            if len(r.r) > 24:
                r.r = r.r[-24:]
        for w in writes:
            if w.multi:
                w.ws[id(ev[0])] = ev
            else:
                w.w = ev
            w.r = []

    def op(self, qn, fn, reads=(), writes=()):
        q = self.q[qn]
        self._wait(q, self._deps(reads, writes), same_ok=(qn == "pe"))
        inst = fn(q.eng)
        q.cnt += 1
        inst.then_inc(q.sem, 1)
        ev = (q.sem, q.cnt)
        self._commit(ev, reads, writes)
        return ev

    def dma(self, qn, out, in_, reads=(), writes=(), **kw):
        q = self.q[qn]
        i = q.ring_i
        i = i % q.ring_limit
        q.ring_i = (i + 1) % q.ring_limit
        sem = q.ring[i]
        evs = self._deps(reads, writes)
        if q.ring_tot[i]:
            evs.append((sem, q.ring_tot[i]))
        self._wait(q, evs, same_ok=False)
        q.eng.dma_start(out=out, in_=in_, **kw).then_inc(sem, 16)
        q.ring_tot[i] += 16
        ev = (sem, q.ring_tot[i])
        self._commit(ev, reads, writes)
        return ev

    def barrier(self):
        evs = []
        for qq in self.q.values():
            for i, sem in enumerate(qq.ring):
                if qq.ring_tot[i]:
                    evs.append((sem, qq.ring_tot[i]))
            if qq.cnt:
                evs.append((qq.sem, qq.cnt))
        for q in self.q.values():
            self._wait(q, evs, same_ok=True)

    def finish(self):
        q = self.q["sp"]
        evs = []
        for qq in self.q.values():
            for i, sem in enumerate(qq.ring):
                if qq.ring_tot[i]:
                    evs.append((sem, qq.ring_tot[i]))
            if qq.cnt and qq is not q:
                evs.append((qq.sem, qq.cnt))
        self._wait(q, evs, same_ok=True)


class Ring:
    def __init__(self, tiles, name):
        self.t = tiles
        self.r = [Res(f"{name}{i}") for i in range(len(tiles))]
        self.i = 0

    def next(self):
        i = self.i
        self.i = (i + 1) % len(self.t)
        return self.t[i], self.r[i]


class B:
    def __init__(self, cfg):
        self.cfg = cfg
        self.nc = bass.Bass("TRN2", target_bir_lowering=False)
        self.es = contextlib.ExitStack()
        self.sy = Sy(self.nc, self.es)
        self.dbg_outs = []
        self.ins = {}
        self.outs = {}

    def inp(self, name, shape):
        t = self.nc.dram_tensor(name, list(shape), F32, kind="ExternalInput").ap()
        self.ins[name] = t
        return t

    def outp(self, name, shape):
        t = self.nc.dram_tensor(name, list(shape), F32, kind="ExternalOutput").ap()
        self.outs[name] = t
        return t

    def scr(self, name, shape, dt=F32):
        if self.cfg.debug and dt == F32:
            t = self.nc.dram_tensor(name, list(shape), dt, kind="ExternalOutput").ap()
            self.dbg_outs.append(name)
        else:
            t = self.nc.dram_tensor(name, list(shape), dt, kind="Internal").ap()
        return t

    def sb(self, st, name, shape, dt=F32):
        self._uid = getattr(self, "_uid", 0) + 1
        return st.enter_context(self.nc.sbuf_tensor(f"{name}_u{self._uid}", list(shape), dt))

    def ps(self, st, name, shape, dt=F32):
        return st.enter_context(self.nc.psum_tensor(name, list(shape), dt))

    def build(self):
        cfg = self.cfg
        nc, sy = self.nc, self.sy
        TT, TP, LS = cfg.TT, cfg.TP, cfg.LS
        x_all = self.inp("x_all", [TT, D])
        condT = self.inp("condT", [128, KC, 2])
        ada_w = self.inp("ada_w", [2, D, 6 * D])
        ada_bT = self.inp("ada_bT", [2, 128, 96])
        norm_gT = self.inp("norm_gT", [2, 128, 4, KC])
        ab_w_in = self.inp("ab_w_in", [D, AB_IN])
        ret_ld_bc = self.inp("ret_ld_bc", [128, 16])
        st_ret = self.inp("st_ret", [2, 8, 64, 128])
        out_ret = self.outp("out_ret", [2, cfg.NP, 8, 64, 128])
        oT = self.scr("oT", [D, TT])
        P_ = {}
        for n_, sh in (("convw", [128, 27, 3]), ("w0T", [128, 2, 8]), ("a0T", [128, 2, 8]), ("k_kT", [128, 8]), ("k_aT", [128, 8]), ("r_kT", [128, 8]),
                       ("lnw_bc", [128, 8, 64]), ("lnb_bc", [128, 8, 64]), ("w_up", [128, 1024]), ("a_up", [128, 1024]), ("g_up", [128, 1024])):
            P_[n_] = self.inp(n_, sh)
        st_rwkv = self.inp("st_rwkv", [2, 16, 64, 64])
        out_rwkv = self.outp("out_rwkv", [2, cfg.NP, 16, 64, 64])
        if cfg.stages < 4:
            _real_inp = self.inp
            self.inp = lambda name, shape: None
        ab_w_out = self.inp("ab_w_out", [D, D]); c_w_in = self.inp("c_w_in", [D, 3072]); c_w_out = self.inp("c_w_out", [D, D])
        ffn_g = self.inp("ffn_w_gate", [2, D, FFN]); ffn_u = self.inp("ffn_w_up", [2, D, FFN]); ffn_d = self.inp("ffn_w_down", [2, FFN, D])
        qkn_bc = self.inp("qkn_bc", [128, 2, 128]); rope = self.inp("rope", [LS, 2, 64])
        cache_k = self.inp("cache_k", [cfg.PL, 4, 128]); cache_v = self.inp("cache_v", [cfg.PL, 4, 128])
        if cfg.stages < 4:
            self.inp = _real_inp
            ffn_g = ffn_u = ffn_d = [None, None]
        out_k = self.outp("out_k", [cfg.NP, cfg.LP, 512]); out_v = self.outp("out_v", [cfg.NP, cfg.LP, 512])
        y_all = self.outp("y_all", [TT, D])
        xa = self.scr("xa", [TT, D]); xb = self.scr("xb", [TT, D]); yscr = self.scr("yscr", [512, D])
        QT = self.scr("QT", [16, 128, TT]); KT = self.scr("KT", [4, 128, TT]); Vs = self.scr("Vs", [TT, 512])
        kvg = self.scr("kvg", [TT, 2560])
        qkT = self.scr("qkT", [1024, TT])
        pbT = self.scr("pbT", [B_IN, TT])

        with contextlib.ExitStack() as st0:
            self.modT = self.sb(st0, "modT", [128, 2, 2, 6, KC])
            self.modT_r = Res("modT")
            self.G = self.sb(st0, "Gcols", [128, 2, 2, 4, KC])
            self.G_r = Res("G")
            self.gate = self.sb(st0, "gatecols", [128, 2, 2, 2, KC])
            self.gate_r = Res("gate")
            self.make_consts(st0)
            self.psum = Ring([self.ps(st0, f"ps{i}", [128, 512]) for i in range(3)], "ps")
            self.psa = [self.ps(st0, f"psa{i}", [128, 512]) for i in range(4)]
            self.psa_r = [Res(f"psa{i}") for i in range(4)]
            self.psx = self.ps(st0, "psx", [128, 512])
            self.psx_r = Res("psx")
            if cfg.stages >= 1:
                mats = [("ab_w_in", ab_w_in, D, AB_IN)]
                if cfg.stages >= 4:
                    mats += [("ab_w_out", ab_w_out, D, D), ("c_w_in", c_w_in, D, 3072), ("c_w_out", c_w_out, D, D)]
                    for l_ in range(2):
                        mats += [(f"ffn_g{l_}", ffn_g[l_], D, FFN), (f"ffn_u{l_}", ffn_u[l_], D, FFN), (f"ffn_d{l_}", ffn_d[l_], FFN, D)]
                wb = self.precast_all(mats)
                ab_w_in = wb[0]
                if cfg.stages >= 4:
                    ab_w_out, c_w_in, c_w_out = wb[1], wb[2], wb[3]
                    ffn_g = [wb[4], wb[7]]; ffn_u = [wb[5], wb[8]]; ffn_d = [wb[6], wb[9]]
            self.phase0(condT, ada_w, ada_bT, norm_gT)
            if cfg.debug:
                dbg = self.outp("dbg_G", [128, 2 * 2 * 4 * KC])
                sy.dma("sp", dbg[:, :], self.G[:].rearrange("p a b c d -> p (a b c d)"), reads=[self.G_r])
                dbg2 = self.outp("dbg_gate", [128, 2 * 2 * 2 * KC])
                sy.dma("sp", dbg2[:, :], self.gate[:].rearrange("p a b c d -> p (a b c d)"), reads=[self.gate_r])
            if cfg.stages in (0.51, 0.52, 0.53):
                with contextlib.ExitStack() as stx:
                    tmp = self.sb(stx, "tmpx", [128, 128]); tmp_r = Res("tmpx")
                    if cfg.stages == 0.51:
                        sy.op("dve", lambda e: e.tensor_scalar(out=tmp[:], in0=self.ident_f[:], scalar1=2.0, scalar2=None, op0=ALU.mult), reads=[self.ident_f_r], writes=[tmp_r])
                    elif cfg.stages == 0.52:
                        sy.op("dve", lambda e: e.tensor_scalar(out=tmp[:], in0=self.ident_f[:], scalar1=self.G[:, 0, 1, 0, 3:4], scalar2=None, op0=ALU.mult), reads=[self.ident_f_r, self.G_r], writes=[tmp_r])
                    else:
                        sy.op("dve", lambda e: e.tensor_scalar(out=tmp[:], in0=self.ident_f[:], scalar1=self.gate[:, 0, 1, 0, 3:4], scalar2=None, op0=ALU.mult), reads=[self.ident_f_r, self.gate_r], writes=[tmp_r])
                    dbg3 = self.outp("dbg_tmp", [128, 128])
                    sy.dma("sp", dbg3[:, :], tmp[:], reads=[tmp_r])
                    sy.barrier()
            if cfg.stages in (0.5, 0.54, 0.55):
                with contextlib.ExitStack() as stx:
                    gbc = self.sb(stx, "gbcx", [128, D]); gbc_r = Res("gbcx")
                    tmp = self.sb(stx, "tmpx", [128, 128]); tmp_r = Res("tmpx")
                    ones = self.sb(stx, "onesx", [128, 128]); ones_r = Res("onesx")
                    sy.op("dve", lambda e: e.memset(ones[:], 1.0), writes=[ones_r])
                    self.gate_bc_build(gbc, gbc_r, 0, 1, 0, tmp, tmp_r, ones, ones_r)
                    dbg3 = self.outp("dbg_gbc", [128, D])
                    sy.dma("sp", dbg3[:, :], gbc[:], reads=[gbc_r])
                    sy.barrier()
            if cfg.stages >= 1:
                self.layer0_inproj(x_all, ab_w_in, kvg, qkT, pbT)
            if cfg.stages >= 2:
                self.retention(kvg, qkT, oT, ret_ld_bc, st_ret, out_ret)
            if cfg.stages >= 3:
                self.rwkv(pbT, oT, P_, st_rwkv, out_rwkv)
            if cfg.stages >= 4:
                self.mix_out_and_ffn(0, oT, ab_w_out, ffn_g[0], ffn_u[0], ffn_d[0], x_all, None, xa, xb, None, yscr)
                x1_r = Res("x1", multi=True); self.oT_r = Res("oT2", multi=True)
            if cfg.stages >= 5:
                self.layer1(xb, x1_r, c_w_in, qkn_bc, rope, cache_k, cache_v, QT, KT, Vs, oT, out_k, out_v)
            if cfg.stages >= 6:
                self.mix_out_and_ffn(1, oT, c_w_out, ffn_g[1], ffn_u[1], ffn_d[1], xb, x1_r, xa, None, y_all, yscr)
        sy.finish()
        self.es.close()
        return nc

    def make_consts(self, st):
        nc, sy = self.nc, self.sy
        idf = self.sb(st, "ident_f", [128, 128])
        self.ident_f, self.ident_f_r = idf, Res("ident_f")
        self.ident_bf = self.sb(st, "ident_bf", [128, 128], BF16)
        self.ident_r = Res("ident_bf")
        sy.op("pool", lambda e: e.memset(idf[:], 1.0), writes=[self.ident_f_r])
        sy.op("pool", lambda e: e.affine_select(out=idf[:], in_=idf[:], pattern=[[1, 128]], compare_op=ALU.is_equal,
                                                fill=0.0, base=0, channel_multiplier=-1),
              reads=[self.ident_f_r], writes=[self.ident_f_r])
        sy.op("dve", lambda e: e.tensor_copy(out=self.ident_bf[:], in_=idf[:]), reads=[self.ident_f_r], writes=[self.ident_r])

    def precast_all(self, mats):
        sy = self.sy
        outs = []
        sy.q["pool"].ring_limit = 4
        with contextlib.ExitStack() as st:
            ring = Ring([self.sb(st, f"pc{i}", [128, 22, 512], BF16) for i in range(4)], "pc")
            for (name, W, R_, N_) in mats:
                Wb = self.scr(name + "_bf", [R_, N_], BF16)
                outs.append(Wb)
                wr = Res(name + "_bf", multi=True)
                KR = R_ // 128
                for n0 in range(0, N_, 512):
                    nc_ = min(512, N_ - n0)
                    for k0 in range(0, KR, 22):
                        kk_ = min(22, KR - k0)
                        t, t_r = ring.next()
                        sy.dma("pool", t[:, 0:kk_, 0:nc_], W[k0 * 128:(k0 + kk_) * 128, n0:n0 + nc_].rearrange("(kc p) n -> p kc n", p=128), writes=[t_r])
                        sy.dma("sp", Wb[k0 * 128:(k0 + kk_) * 128, n0:n0 + nc_].rearrange("(kc p) n -> p kc n", p=128), t[:, 0:kk_, 0:nc_], reads=[t_r], writes=[wr])
            sy.barrier()
        sy.q["pool"].ring_limit = 2
        return outs

    def phase0(self, condT, ada_w, ada_bT, norm_gT):
        nc, sy = self.nc, self.sy
        with contextlib.ExitStack() as st:
            cT = self.sb(st, "cT", [128, KC, 2])
            cT_r = Res("cT")
            sT = self.sb(st, "sT", [128, KC, 2])
            sT_r = Res("sT")
            bT = self.sb(st, "bT", [128, 2, 96])
            bT_r = Res("bT")
            gT = self.sb(st, "gT", [128, 2, 4, KC])
            gT_r = Res("gT")
            wr = Ring([self.sb(st, f"aw{i}", [128, KC, 512]) for i in range(2)], "aw")
            sy.dma("sp", cT[:], condT[:, :, :], writes=[cT_r])
            for l in range(2):
                sy.dma("sp", bT[:, l, :], ada_bT[l, :, :], writes=[bT_r])
                sy.dma("sp", gT[:, l, :, :], norm_gT[l, :, :, :], writes=[gT_r])
            sy.op("act", lambda e: e.activation(out=sT[:], in_=cT[:], func=AF.Silu), reads=[cT_r], writes=[sT_r])
            for l in range(2):
                for nch in range(24):
                    wt, wt_r = wr.next()
                    src = ada_w[l, :, nch * 512:(nch + 1) * 512].rearrange("(kc p) n -> p kc n", p=128)
                    sy.dma("sp" if nch % 2 == 0 else "act", wt[:], src, writes=[wt_r])
                    pt, pt_r = self.psum.next()
                    for sub in range(4):
                        for kc in range(KC):
                            sy.op("pe", lambda e, sub=sub, kc=kc: e.matmul(
                                pt[:, sub * 2:sub * 2 + 2], lhsT=wt[:, kc, sub * 128:(sub + 1) * 128],
                                rhs=sT[:, kc, :], start=(kc == 0), stop=(kc == KC - 1)),
                                reads=[wt_r, sT_r], writes=[pt_r])
                    for sub in range(4):
                        idx = nch * 4 + sub
                        vec, kq = idx // 16, idx % 16
                        for g in range(2):
                            sy.op("dve", lambda e, sub=sub, g=g, vec=vec, kq=kq, idx=idx: e.tensor_tensor(
                                out=self.modT[:, l, g, vec, kq:kq + 1], in0=pt[:, sub * 2 + g:sub * 2 + g + 1],
                                in1=bT[:, l, idx:idx + 1], op=ALU.add),
                                reads=[pt_r, bT_r], writes=[self.modT_r])
            for l in range(2):
                for g in range(2):
                    m = self.modT
                    sy.op("dve", lambda e, l=l, g=g: e.scalar_tensor_tensor(
                        out=self.G[:, l, g, 0, :], in0=m[:, l, g, 1, :], scalar=1.0, in1=gT[:, l, 0, :],
                        op0=ALU.add, op1=ALU.mult), reads=[self.modT_r, gT_r], writes=[self.G_r])
                    sy.op("dve", lambda e, l=l, g=g: e.tensor_copy(out=self.G[:, l, g, 1, :], in_=m[:, l, g, 0, :]),
                          reads=[self.modT_r], writes=[self.G_r])
                    sy.op("dve", lambda e, l=l, g=g: e.scalar_tensor_tensor(
                        out=self.G[:, l, g, 2, :], in0=m[:, l, g, 4, :], scalar=1.0, in1=gT[:, l, 2, :],
                        op0=ALU.add, op1=ALU.mult), reads=[self.modT_r, gT_r], writes=[self.G_r])
                    sy.op("dve", lambda e, l=l, g=g: e.tensor_copy(out=self.G[:, l, g, 3, :], in_=m[:, l, g, 3, :]),
                          reads=[self.modT_r], writes=[self.G_r])
                    sy.op("dve", lambda e, l=l, g=g: e.tensor_tensor(
                        out=self.gate[:, l, g, 0, :], in0=m[:, l, g, 2, :], in1=gT[:, l, 1, :], op=ALU.mult),
                        reads=[self.modT_r, gT_r], writes=[self.gate_r])
                    sy.op("dve", lambda e, l=l, g=g: e.tensor_tensor(
                        out=self.gate[:, l, g, 1, :], in0=m[:, l, g, 5, :], in1=gT[:, l, 3, :], op=ALU.mult),
                        reads=[self.modT_r, gT_r], writes=[self.gate_r])
            sy.barrier()

    def make_hT(self, st, x_rows, T, l, g, which, hT, hT_r, ident_bf, ident_r, src_r=None):
        nc, sy = self.nc, self.sy
        gi, si = (0, 1) if which == 0 else (2, 3)
        for i in range(T // 128):
            xt, xt_r = self.xring.next()
            sy.dma("sp", xt[:], x_rows[i * 128:(i + 1) * 128, :], reads=[src_r] if src_r else [], writes=[xt_r])
            xb, xb_r = self.xbring.next()
            ss, ss_r = self.ssring.next()
            sy.op("act", lambda e: e.activation(out=xb[:], in_=xt[:], func=AF.Square, accum_out=ss[:, 0:1]),
                  reads=[xt_r], writes=[xb_r, ss_r])
            sy.op("dve", lambda e: e.tensor_scalar(out=ss[:, 1:2], in0=ss[:, 0:1], scalar1=1.0 / D, scalar2=RMS_EPS,
                                                   op0=ALU.mult, op1=ALU.add), reads=[ss_r], writes=[ss_r])
            sy.op("act", lambda e: e.activation(out=ss[:, 2:3], in_=ss[:, 1:2], func=AF.Sqrt), reads=[ss_r], writes=[ss_r])
            sy.op("dve", lambda e: e.reciprocal(out=ss[:, 3:4], in_=ss[:, 2:3]), reads=[ss_r], writes=[ss_r])
            sy.op("dve", lambda e: e.tensor_scalar(out=xb[:], in0=xt[:], scalar1=ss[:, 3:4], scalar2=None, op0=ALU.mult),
                  reads=[xt_r, ss_r], writes=[xb_r])
            for half in range(2):
                pt, pt_r = self.psum.next()
                ptb = pt[:].bitcast(BF16)
                for k8 in range(8):
                    kc = half * 8 + k8
                    sy.op("pe", lambda e, kc=kc, k8=k8: e.transpose(ptb[:, k8 * 128:(k8 + 1) * 128],
                                                                    xb[:, kc * 128:(kc + 1) * 128], ident_bf[:]),
                          reads=[xb_r, ident_r], writes=[pt_r])
                for k8 in range(8):
                    kc = half * 8 + k8
                    eng = "dve" if k8 % 2 == 0 else "pool_no"
                    sy.op("dve" if k8 % 2 == 0 else "act", (lambda e, kc=kc, k8=k8: e.tensor_scalar(
                        out=hT[:, kc, i * 128:(i + 1) * 128], in0=ptb[:, k8 * 128:(k8 + 1) * 128],
                        scalar1=self.G[:, l, g, gi, kc:kc + 1], scalar2=self.G[:, l, g, si, kc:kc + 1],
                        op0=ALU.mult, op1=ALU.add)) if k8 % 2 == 0 else (lambda e, kc=kc, k8=k8: e.activation(
                        out=hT[:, kc, i * 128:(i + 1) * 128], in_=ptb[:, k8 * 128:(k8 + 1) * 128], func=AF.Identity,
                        scale=self.G[:, l, g, gi, kc:kc + 1], bias=self.G[:, l, g, si, kc:kc + 1])),
                        reads=[pt_r, self.G_r], writes=[hT_r])

    def wload(self, W, n0, ncols=512, K=KC, k0=0):
        wt, wt_r = self.wring.next()
        src = W[k0 * 128:(k0 + K) * 128, n0:n0 + ncols].rearrange("(kc p) n -> p kc n", p=128)
        self.sy.dma("pool", wt[:, 0:K, 0:ncols], src, writes=[wt_r])
        return wt, wt_r

    def blocks(self):
        cfg = self.cfg
        out = [(i, min(512, cfg.TP - i), 0) for i in range(0, cfg.TP, 512)]
        TB = min(512, cfg.LS)
        for i in range(cfg.LS // TB):
            out.append((cfg.TP + i * TB, TB, 1))
        return out

    def layer0_inproj(self, x_all, ab_w_in, kvg, qkT, pbT):
        nc, sy, cfg = self.nc, self.sy, self.cfg
        sy.q["pool"].ring_limit = 5
        self.kvg_r, self.qkT_r, self.pbT_r = Res("kvg", multi=True), Res("qkT", multi=True), Res("pbT", multi=True)
        with contextlib.ExitStack() as st:
            TBmax = 512
            hT = self.sb(st, "hT", [128, KC, TBmax], BF16)
            hT_r = Res("hT")
            self.setup_dense_bufs(st)
            ev = Ring([self.sb(st, f"ev{i}", [128, 512]) for i in range(4)], "ev")
            for (t0, T, g) in self.blocks():
                self.make_hT(st, x_all[t0:t0 + T, :], T, 0, g, 0, hT, hT_r, self.ident_bf, self.ident_r)
                for c5 in range(5):
                    wt, wt_r = self.wload(ab_w_in, 512 + c5 * 512)
                    for i in range(T // 128):
                        pt, pt_r = self.psum.next()
                        for kc in range(KC):
                            sy.op("pe", lambda e, kc=kc, i=i: e.matmul(pt[:], lhsT=hT[:, kc, i * 128:(i + 1) * 128], rhs=wt[:, kc, :],
                                                                     start=(kc == 0), stop=(kc == KC - 1)),
                                  reads=[hT_r, wt_r], writes=[pt_r])
                        et, et_r = ev.next()
                        sy.op("act" if i % 2 else "dve", (lambda e: e.activation(out=et[:], in_=pt[:], func=AF.Copy)) if i % 2 else
                              (lambda e: e.tensor_copy(out=et[:], in_=pt[:])), reads=[pt_r], writes=[et_r])
                        sy.dma("sp", kvg[t0 + i * 128:t0 + (i + 1) * 128, c5 * 512:(c5 + 1) * 512], et[:], reads=[et_r], writes=[self.kvg_r])
                fm_chunks = [(n0, qkT, n0) for n0 in range(0, 1024, 512)] + \
                            [(3072 + j * 512, pbT, j * 512) for j in range(7)]
                for (n0, dst, d0) in fm_chunks:
                    ncols = min(512, AB_IN - n0)
                    wt, wt_r = self.wload(ab_w_in, n0, ncols)
                    for sub in range(ncols // 128):
                        for tb in range(0, T, 512):
                            tw = min(512, T - tb)
                            pt, pt_r = self.psum.next()
                            for kc in range(KC):
                                sy.op("pe", lambda e, kc=kc, sub=sub, tb=tb, tw=tw: e.matmul(
                                    pt[:, 0:tw], lhsT=wt[:, kc, sub * 128:(sub + 1) * 128], rhs=hT[:, kc, tb:tb + tw],
                                    start=(kc == 0), stop=(kc == KC - 1)), reads=[hT_r, wt_r], writes=[pt_r])
                            et, et_r = ev.next()
                            sy.op("act" if sub % 2 else "dve", (lambda e, tw=tw: e.activation(out=et[:, 0:tw], in_=pt[:, 0:tw], func=AF.Copy)) if sub % 2 else
                                  (lambda e, tw=tw: e.tensor_copy(out=et[:, 0:tw], in_=pt[:, 0:tw])), reads=[pt_r], writes=[et_r])
                            sy.dma("sp", dst[d0 + sub * 128:d0 + (sub + 1) * 128, t0 + tb:t0 + tb + tw], et[:, 0:tw],
                                   reads=[et_r], writes=[self.pbT_r if dst is pbT else self.qkT_r])
            sy.barrier()
        sy.q["pool"].ring_limit = 2

    def seqs(self):
        cfg = self.cfg
        out = [(i * cfg.LP, cfg.LP, 0, i) for i in range(cfg.NP)]
        out.append((cfg.TP, cfg.LS, 1, 0))
        return out

    def retention(self, kvg, qkT, oT, ret_ld_bc, st_ret, out_ret):
        nc, sy, cfg = self.nc, self.sy, self.cfg
        I32 = mybir.dt.int32
        C = 128
        Lmax = max(cfg.LP, cfg.LS)
        NCmax = Lmax // C
        with contextlib.ExitStack() as st:
            ii = self.sb(st, "r_ii", [128, 128], I32)
            diff = self.sb(st, "r_diff", [128, 128])
            ndiff = self.sb(st, "r_ndiff", [128, 128])
            n1 = self.sb(st, "r_n1", [128, 128])
            cn = self.sb(st, "r_cn", [128, 128])
            pc = self.sb(st, "r_pc", [128, 4])
            lg = self.sb(st, "r_lg", [128, 16])
            cst_r = Res("r_const")
            maskT = self.sb(st, "r_maskT", [128, 16, 128])
            xiT = self.sb(st, "r_xiT", [128, 16, 128])
            zeta = self.sb(st, "r_zeta", [128, 16])
            gcc = self.sb(st, "r_gc", [128, 16])
            sy.op("pool", lambda e: e.iota(ii[:], pattern=[[1, 128]], base=0, channel_multiplier=-1), writes=[cst_r])
            sy.op("dve", lambda e: e.tensor_copy(out=diff[:], in_=ii[:]), reads=[cst_r], writes=[cst_r])
            sy.op("dve", lambda e: e.tensor_scalar(out=ndiff[:], in0=diff[:], scalar1=-1.0, scalar2=None, op0=ALU.mult), reads=[cst_r], writes=[cst_r])
            sy.op("pool", lambda e: e.iota(ii[:], pattern=[[1, 128]], base=1, channel_multiplier=0), reads=[cst_r], writes=[cst_r])
            sy.op("dve", lambda e: e.tensor_copy(out=n1[:], in_=ii[:]), reads=[cst_r], writes=[cst_r])
            sy.op("dve", lambda e: e.tensor_scalar(out=cn[:], in0=n1[:], scalar1=-1.0, scalar2=float(C + 1), op0=ALU.mult, op1=ALU.add), reads=[cst_r], writes=[cst_r])
            sy.op("pool", lambda e: e.iota(ii[:, 0:1], pattern=[[1, 1]], base=0, channel_multiplier=1), reads=[cst_r], writes=[cst_r])
            sy.op("dve", lambda e: e.tensor_copy(out=pc[:, 1:2], in_=ii[:, 0:1]), reads=[cst_r], writes=[cst_r])
            sy.op("dve", lambda e: e.tensor_scalar(out=pc[:, 0:1], in0=pc[:, 1:2], scalar1=-1.0, scalar2=float(C - 1), op0=ALU.mult, op1=ALU.add), reads=[cst_r], writes=[cst_r])
            sy.op("dve", lambda e: e.memset(pc[:, 2:3], float(C)), reads=[cst_r], writes=[cst_r])
            sy.dma("sp", lg[:], ret_ld_bc[:, :], reads=[cst_r], writes=[cst_r])
            sy.op("act", lambda e: e.activation(out=lg[:], in_=lg[:], func=AF.Exp), reads=[cst_r], writes=[cst_r])
            sy.op("dve", lambda e: e.tensor_scalar(out=lg[:], in0=lg[:], scalar1=-1.0, scalar2=None, op0=ALU.mult), reads=[cst_r], writes=[cst_r])
            for d in range(2):
                for h in range(8):
                    c = d * 8 + h
                    src = diff if d == 0 else ndiff
                    sy.op("act", lambda e, c=c, src=src: e.activation(out=maskT[:, c, :], in_=src[:], func=AF.Exp, scale=lg[:, c:c + 1]),
                          reads=[cst_r], writes=[cst_r])
                    sy.op("pool", lambda e, c=c, d=d: e.affine_select(out=maskT[:, c, :], in_=maskT[:, c, :], pattern=[[1 if d == 0 else -1, 128]],
                                                                    compare_op=ALU.is_ge, fill=0.0, base=0, channel_multiplier=(-1 if d == 0 else 1)),
                          reads=[cst_r], writes=[cst_r])
                    sy.op("dve", lambda e, c=c: e.tensor_scalar(out=maskT[:, c, :], in0=maskT[:, c, :], scalar1=0.125, scalar2=None, op0=ALU.mult),
                          reads=[cst_r], writes=[cst_r])
                    sy.op("act", lambda e, c=c, d=d: e.activation(out=xiT[:, c, :], in_=(n1 if d == 0 else cn)[:], func=AF.Exp, scale=lg[:, c:c + 1]),
                          reads=[cst_r], writes=[cst_r])
                    sy.op("act", lambda e, c=c, d=d: e.activation(out=zeta[:, c:c + 1], in_=pc[:, 0:1] if d == 0 else pc[:, 1:2], func=AF.Exp, scale=lg[:, c:c + 1]),
                          reads=[cst_r], writes=[cst_r])
                    sy.op("act", lambda e, c=c: e.activation(out=gcc[:, c:c + 1], in_=pc[:, 2:3], func=AF.Exp, scale=lg[:, c:c + 1]),
                          reads=[cst_r], writes=[cst_r])
            sy.op("dve", lambda e: e.tensor_scalar(out=zeta[:], in0=zeta[:], scalar1=0.125, scalar2=None, op0=ALU.mult), reads=[cst_r], writes=[cst_r])
            qT = Ring([self.sb(st, f"r_qT{i}", [64, Lmax], BF16) for i in range(2)], "r_qT")
            kT = Ring([self.sb(st, f"r_kT{i}", [64, Lmax], BF16) for i in range(2)], "r_kT")
            ktm = Ring([self.sb(st, f"r_ktm{i}", [128, NCmax, 64], BF16) for i in range(2)], "r_ktm")
            vtm = Ring([self.sb(st, f"r_vtm{i}", [128, NCmax, 128], BF16) for i in range(2)], "r_vtm")
            gtm = Ring([self.sb(st, f"r_gtm{i}", [128, NCmax, 128]) for i in range(2)], "r_gtm")
            oacc = Ring([self.sb(st, f"r_oacc{i}", [128, NCmax, 128]) for i in range(2)], "r_oacc")
            kz = Ring([self.sb(st, f"r_kz{i}", [128, NCmax, 64], BF16) for i in range(2)], "r_kz")
            qx = Ring([self.sb(st, f"r_qx{i}", [64, 128], BF16) for i in range(3)], "r_qx")
            sTs = Ring([self.sb(st, f"r_sT{i}", [128, 128], BF16) for i in range(3)], "r_sT")
            S = Ring([self.sb(st, f"r_S{i}", [64, 128]) for i in range(2)], "r_S")
            Sb = Ring([self.sb(st, f"r_Sb{i}", [64, 128], BF16) for i in range(3)], "r_Sb")
            ssq = Ring([self.sb(st, f"r_ssq{i}", [128, NCmax, 2]) for i in range(2)], "r_ssq")
            junk = Ring([self.sb(st, f"r_junk{i}", [128, 128]) for i in range(2)], "r_junk")
            oTt = Ring([self.sb(st, f"r_oTt{i}", [128, 128]) for i in range(3)], "r_oTt")
            self.oT_r = Res("oT", multi=True)
            out_r = Res("out_ret", multi=True)
            for (t0, L, g, si) in self.seqs():
                NC_ = L // C
                for h in range(8):
                    q_t, q_r = qT.next(); k_t, k_r = kT.next(); kt_t, kt_r = ktm.next(); v_t, v_r = vtm.next()
                    g_t, g_r = gtm.next(); o_t, o_r = oacc.next()
                    sy.dma("pool", q_t[:, 0:L], qkT[h * 64:(h + 1) * 64, t0:t0 + L], reads=[self.qkT_r], writes=[q_r])
                    sy.dma("pool", k_t[:, 0:L], qkT[512 + h * 64:512 + (h + 1) * 64, t0:t0 + L], reads=[self.qkT_r], writes=[k_r])
                    sy.dma("pool", kt_t[:, 0:NC_, :], kvg[t0:t0 + L, h * 64:(h + 1) * 64].rearrange("(c p) d -> p c d", p=128),
                           reads=[self.kvg_r], writes=[kt_r])
                    sy.dma("pool", v_t[:, 0:NC_, :], kvg[t0:t0 + L, 512 + h * 128:512 + (h + 1) * 128].rearrange("(c p) d -> p c d", p=128),
                           reads=[self.kvg_r], writes=[v_r])
                    sy.dma("sp", g_t[:, 0:NC_, :], kvg[t0:t0 + L, 1536 + h * 128:1536 + (h + 1) * 128].rearrange("(c p) d -> p c d", p=128),
                           reads=[self.kvg_r], writes=[g_r])
                    for d in range(2):
                        c = d * 8 + h
                        kz_t, kz_r = kz.next()
                        sy.op("dve", lambda e, c=c: e.tensor_scalar(out=kz_t[:, 0:NC_, :], in0=kt_t[:, 0:NC_, :], scalar1=zeta[:, c:c + 1], scalar2=None, op0=ALU.mult),
                              reads=[kt_r, cst_r], writes=[kz_r])
                        S_t, S_r = S.next()
                        if g == 0:
                            sy.op("dve", lambda e: e.memset(S_t[:], 0.0), writes=[S_r])
                        else:
                            sy.dma("sp", S_t[:], st_ret[d, h, :, :], writes=[S_r])
                        Sb_t, Sb_r = Sb.next()
                        sy.op("act", lambda e: e.activation(out=Sb_t[:], in_=S_t[:], func=AF.Copy), reads=[S_r], writes=[Sb_r])
                        for ci in (range(NC_) if d == 0 else range(NC_ - 1, -1, -1)):
                            cs = slice(ci * C, (ci + 1) * C)
                            pt, pt_r = self.psum.next()
                            sy.op("pe", lambda e, cs=cs: e.matmul(pt[:, 0:128], lhsT=k_t[:, cs], rhs=q_t[:, cs], start=True, stop=True),
                                  reads=[k_r, q_r], writes=[pt_r])
                            sT_t, sT_r = sTs.next()
                            sy.op("dve", lambda e, c=c: e.tensor_tensor(out=sT_t[:], in0=pt[:, 0:128], in1=maskT[:, c, :], op=ALU.mult),
                                  reads=[pt_r, cst_r], writes=[sT_r])
                            qx_t, qx_r = qx.next()
                            sy.op("pool", lambda e, cs=cs, c=c: e.tensor_tensor(out=qx_t[:], in0=q_t[:, cs], in1=xiT[0:64, c, :], op=ALU.mult),
                                  reads=[q_r, cst_r], writes=[qx_r])
                            po, po_r = self.psum.next()
                            sy.op("pe", lambda e, ci=ci: e.matmul(po[:, 0:128], lhsT=sT_t[:], rhs=v_t[:, ci, :], start=True, stop=False),
                                  reads=[sT_r, v_r], writes=[po_r])
                            sy.op("pe", lambda e: e.matmul(po[:, 0:128], lhsT=qx_t[:], rhs=Sb_t[:], start=False, stop=True),
                                  reads=[qx_r, Sb_r], writes=[po_r])
                            if d == 0:
                                sy.op("act", lambda e, ci=ci: e.activation(out=o_t[:, ci, :], in_=po[:, 0:128], func=AF.Copy), reads=[po_r], writes=[o_r])
                            else:
                                sy.op("dve", lambda e, ci=ci: e.tensor_tensor(out=o_t[:, ci, :], in0=po[:, 0:128], in1=o_t[:, ci, :], op=ALU.add),
                                      reads=[po_r, o_r], writes=[o_r])
                            pS, pS_r = self.psum.next()
                            sy.op("pe", lambda e, ci=ci: e.matmul(pS[0:64, 0:128], lhsT=kz_t[:, ci, :], rhs=v_t[:, ci, :], start=True, stop=True),
                                  reads=[kz_r, v_r], writes=[pS_r])
                            sy.op("dve", lambda e, c=c: e.scalar_tensor_tensor(out=S_t[:], in0=S_t[:], scalar=gcc[0:64, c:c + 1], in1=pS[0:64, 0:128],
                                                                              op0=ALU.mult, op1=ALU.add), reads=[pS_r, S_r, cst_r], writes=[S_r])
                            Sb_t, Sb_r = Sb.next()
                            sy.op("act", lambda e, Sb_t=Sb_t: e.activation(out=Sb_t[:], in_=S_t[:], func=AF.Copy), reads=[S_r], writes=[Sb_r])
                        if g == 0:
                            sy.dma("sp", out_ret[d, si, h, :, :], S_t[:], reads=[S_r], writes=[out_r])
                    sq_t, sq_r = ssq.next()
                    for ci in range(NC_):
                        j_t, j_r = junk.next()
                        sy.op("act", lambda e, ci=ci, j_t=j_t: e.activation(out=j_t[:], in_=o_t[:, ci, :], func=AF.Square, accum_out=sq_t[:, ci, 0:1]),
                              reads=[o_r], writes=[j_r, sq_r])
                    sy.op("dve", lambda e: e.tensor_scalar(out=sq_t[:, 0:NC_, 1:2], in0=sq_t[:, 0:NC_, 0:1], scalar1=1.0 / 128, scalar2=RMS_EPS, op0=ALU.mult, op1=ALU.add),
                          reads=[sq_r], writes=[sq_r])
                    sy.op("act", lambda e: e.activation(out=sq_t[:, 0:NC_, 1:2], in_=sq_t[:, 0:NC_, 1:2], func=AF.Sqrt), reads=[sq_r], writes=[sq_r])
                    sy.op("dve", lambda e: e.reciprocal(out=sq_t[:, 0:NC_, 1:2], in_=sq_t[:, 0:NC_, 1:2]), reads=[sq_r], writes=[sq_r])
                    sy.op("act", lambda e: e.activation(out=g_t[:, 0:NC_, :], in_=g_t[:, 0:NC_, :], func=AF.Silu), reads=[g_r], writes=[g_r])
                    for ci in range(NC_):
                        sy.op("dve", lambda e, ci=ci: e.scalar_tensor_tensor(out=o_t[:, ci, :], in0=o_t[:, ci, :], scalar=sq_t[:, ci, 1:2], in1=g_t[:, ci, :],
                                                                          op0=ALU.mult, op1=ALU.mult), reads=[o_r, sq_r, g_r], writes=[o_r])
                        pt, pt_r = self.psum.next()
                        sy.op("pe", lambda e, ci=ci: e.transpose(pt[:, 0:128], o_t[:, ci, :], self.ident_f[:]), reads=[o_r, self.ident_f_r], writes=[pt_r])
                        ot_t, ot_r = oTt.next()
                        sy.op("act", lambda e, ot_t=ot_t: e.activation(out=ot_t[:], in_=pt[:, 0:128], func=AF.Copy), reads=[pt_r], writes=[ot_r])
                        sy.dma("sp", oT[h * 128:(h + 1) * 128, t0 + ci * C:t0 + (ci + 1) * C], ot_t[:], reads=[ot_r], writes=[self.oT_r])
            sy.barrier()

    def rwkv(self, pbT, oT, P_, st_rwkv, out_rwkv):
        nc, sy, cfg = self.nc, self.sy, self.cfg
        C = 64
        LAM = 0.606531
        SEGmax = min(max(cfg.LP, cfg.LS), 512)
        NCHm = SEGmax // C
        Lmax = max(cfg.LP, cfg.LS)
        NCL = Lmax // C
        ZDT = F32 if cfg.zf32 else BF16
        with contextlib.ExitStack() as st:
            cst_r = Res("w_const")
            convw = self.sb(st, "w_convw", [128, 27, 3]); w0T = self.sb(st, "w_w0T", [128, 2, 8]); a0T = self.sb(st, "w_a0T", [128, 2, 8])
            kkT = self.sb(st, "w_kkT", [128, 8]); kaT = self.sb(st, "w_kaT", [128, 8]); rkT = self.sb(st, "w_rkT", [128, 8])
            lnw = self.sb(st, "w_lnw", [128, 8, 64]); lnb = self.sb(st, "w_lnb", [128, 8, 64])
            wup = self.sb(st, "w_wup", [128, 1024], BF16); aup = self.sb(st, "w_aup", [128, 1024], BF16); gup = self.sb(st, "w_gup", [128, 1024], BF16)
            for t_, n_ in ((convw, "convw"), (w0T, "w0T"), (a0T, "a0T"), (kkT, "k_kT"), (kaT, "k_aT"), (rkT, "r_kT"), (lnw, "lnw_bc"), (lnb, "lnb_bc")):
                sy.dma("sp", t_[:], P_[n_][tuple(slice(None) for _ in t_.shape)], writes=[cst_r])
            for t_, n_ in ((wup, "w_up"), (aup, "a_up"), (gup, "g_up")):
                sy.dma("pool", t_[:], P_[n_][:, :], writes=[cst_r])
            E = self.sb(st, "w_E", [128, 64], BF16)
            sy.op("dve", lambda e: e.tensor_tensor(out=E[:], in0=self.ident_f[:, 0:64], in1=self.ident_f[:, 64:128], op=ALU.add),
                  reads=[self.ident_f_r], writes=[cst_r])
            ones_c = self.sb(st, "w_ones", [128, 2], BF16)
            sy.op("dve", lambda e: e.memset(ones_c[:], 1.0), writes=[cst_r])
            bd1 = self.sb(st, "w_bd1", [128, 128])
            sy.op("dve", lambda e: e.memset(bd1[:], 0.0), writes=[cst_r])
            sy.op("dve", lambda e: e.memset(bd1[0:64, 0:64], 1.0), reads=[cst_r], writes=[cst_r])
            sy.op("dve", lambda e: e.memset(bd1[64:128, 64:128], 1.0), reads=[cst_r], writes=[cst_r])
            mask1 = self.sb(st, "w_mask1", [128, 2, 4, 128]); maskQ = self.sb(st, "w_maskQ", [128, 2, 128])
            sy.op("dve", lambda e: e.memset(mask1[:], 1.0), writes=[cst_r])
            sy.op("dve", lambda e: e.memset(maskQ[:], 1.0), reads=[cst_r], writes=[cst_r])
            for d in range(2):
                sg_ = 1 if d == 0 else -1
                for k4 in range(4):
                    strict = (k4 % 2 == 0)
                    sy.op("pool", lambda e, d=d, k4=k4, strict=strict, sg_=sg_: e.affine_select(
                        out=mask1[:, d, k4, :], in_=mask1[:, d, k4, :], pattern=[[sg_, 128]],
                        compare_op=(ALU.is_gt if strict else ALU.is_ge), fill=0.0, base=0, channel_multiplier=-sg_),
                        reads=[cst_r], writes=[cst_r])
                sy.op("pool", lambda e, d=d, sg_=sg_: e.affine_select(out=maskQ[:, d, :], in_=maskQ[:, d, :], pattern=[[-sg_, 128]],
                                                                    compare_op=ALU.is_gt, fill=0.0, base=0, channel_multiplier=sg_),
                      reads=[cst_r], writes=[cst_r])
            rst = self.sb(st, "w_rst", [128, SEGmax])
            sy.op("dve", lambda e: e.memset(rst[:], 1.0), writes=[cst_r])
            sy.op("dve", lambda e: e.memset(rst[:].rearrange("p (c t) -> p c t", t=C)[:, :, 0:1], 0.0), reads=[cst_r], writes=[cst_r])
            def ring(name, shape, n, dt=F32):
                return Ring([self.sb(st, f"{name}{i}", shape, dt) for i in range(n)], name)
            raw = ring("w_raw", [128, SEGmax + 2], 3)
            f_r, f_kb, f_vb, f_kk = ring("w_r", [128, SEGmax], 1), ring("w_kb", [128, SEGmax], 1), ring("w_vb", [128, SEGmax], 1), ring("w_kk", [128, SEGmax], 1)
            f_t1, f_t2, f_t3 = ring("w_t1", [128, SEGmax], 1), ring("w_t2", [128, SEGmax], 1), ring("w_t3", [128, SEGmax], 1)
            f_a, f_b, f_kd, f_sg, f_cs, f_ex = (ring("w_a", [128, SEGmax], 1), ring("w_b", [128, SEGmax], 1), ring("w_kd", [128, SEGmax], 1),
                                                ring("w_sg", [128, SEGmax], 1), ring("w_cs", [128, SEGmax], 1), ring("w_ex", [128, SEGmax], 1))
            f_e = [ring(f"w_e{i}", [128, SEGmax], 1) for i in range(4)]
            twc = ring("w_twc", [128, SEGmax], 1, BF16); acb = ring("w_acb", [128, SEGmax], 1, BF16)
            WCt = ring("w_WC", [128, NCHm], 2)
            AR = ring("w_AR", [128, NCHm, 2, 128], 1, BF16); BK = ring("w_BK", [128, NCHm, 2, 128], 1, BF16)
            BKh = ring("w_BKh", [128, NCHm, 2, 128], 1, BF16); Vbd = ring("w_Vbd", [128, NCHm, 128], 2, BF16)
            PRd = ring("w_PRd", [128, NCHm, 128], 2, BF16)
            for rg in (AR, BK, BKh, Vbd, PRd):
                for t_, r_ in zip(rg.t, rg.r):
                    sy.op("pool", lambda e, t_=t_: e.memset(t_[:], 0.0), writes=[r_])
            MM = ring("w_MM", [128, NCHm, 4, 128], 2, BF16)
            Zs = ring("w_Zs", [128, NCHm, 128], 2, BF16)
            Vt = ring("w_Vt", [128, NCHm, 64], 2, BF16); Vf = ring("w_Vf", [128, NCL, 64], 1)
            BKtm = ring("w_BKtm", [128, NCHm, 2, 128], 2, BF16)
            Pq = ring("w_Pq", [128, 2, 128], 4, ZDT); Zt = ring("w_Zt", [128, 128], 4, ZDT)
            T = ring("w_T", [128, 64], 2); Tb = ring("w_Tb", [128, 64], 3, BF16)
            RH = ring("w_RH", [128, 64], 3, BF16); Ut = ring("w_Ut", [128, 64], 3, BF16)
            yacc = ring("w_yacc", [128, NCL, 64], 1); bsum = ring("w_bsum", [128, NCL], 1)
            sgc = ring("w_sgc", [128, Lmax // C, 2, C], 1, BF16)
            gn = ring("w_gn", [128, NCL, 4], 1); ysq = ring("w_ysq", [128, NCL, 64], 1)
            ybd = ring("w_ybd", [128, 128], 3, BF16)
            orow = ring("w_orow", [128, Lmax], 1)
            tin = ring("w_tin", [64, 128], 2); tout = ring("w_tout", [64, 128], 2)
            for t_, r_ in zip(ybd.t, ybd.r):
                sy.op("pool", lambda e, t_=t_: e.memset(t_[:], 0.0), writes=[r_])
            out_r = Res("out_rwkv", multi=True)

            def conv(dst, dst_r, blk, t0, L, s0, SEG):
                rw, rw_r = raw.next()
                lo = max(s0 - 1, 0); hi = min(s0 + SEG + 1, L)
                if s0 == 0:
                    sy.op("dve", lambda e: e.memset(rw[:, 0:1], 0.0), writes=[rw_r])
                if s0 + SEG == L:
                    sy.op("dve", lambda e: e.memset(rw[:, SEG + 1:SEG + 2], 0.0), writes=[rw_r])
                sy.dma("sp", rw[:, lo - (s0 - 1):hi - (s0 - 1)], pbT[blk * 128:(blk + 1) * 128, t0 + lo:t0 + hi], reads=[self.pbT_r], writes=[rw_r])
                sy.op("act", lambda e: e.activation(out=dst[:, 0:SEG], in_=rw[:, 0:SEG], func=AF.Copy, scale=convw[:, blk, 0:1]),
                      reads=[rw_r, cst_r], writes=[dst_r])
                sy.op("dve", lambda e: e.scalar_tensor_tensor(out=dst[:, 0:SEG], in0=rw[:, 1:SEG + 1], scalar=convw[:, blk, 1:2], in1=dst[:, 0:SEG],
                                                              op0=ALU.mult, op1=ALU.add), reads=[rw_r, cst_r, dst_r], writes=[dst_r])
                sy.op("dve", lambda e: e.scalar_tensor_tensor(out=dst[:, 0:SEG], in0=rw[:, 2:SEG + 2], scalar=convw[:, blk, 2:3], in1=dst[:, 0:SEG],
                                                              op0=ALU.mult, op1=ALU.add), reads=[rw_r, cst_r, dst_r], writes=[dst_r])

            def bdw(eng, dst, dst_r, k, a, a_r, b, b_r, SEG, NCH):
                for hh in range(2):
                    ps_ = slice(hh * 64, (hh + 1) * 64)
                    o = dst[ps_, 0:NCH, k, ps_] if k is not None else dst[ps_, 0:NCH, ps_]
                    i0 = a[ps_, 0:SEG].rearrange("p (c t) -> p c t", t=C)
                    if b is None:
                        sy.op(eng, lambda e, o=o, i0=i0: e.tensor_copy(out=o, in_=i0), reads=[a_r], writes=[dst_r])
                    else:
                        i1 = b[ps_, 0:SEG].rearrange("p (c t) -> p c t", t=C)
                        sy.op(eng, lambda e, o=o, i0=i0, i1=i1: e.tensor_tensor(out=o, in0=i0, in1=i1, op=ALU.mult), reads=[a_r, b_r], writes=[dst_r])

            for (t0, L, g, si) in self.seqs():
                SEG = min(L, SEGmax); NSEG = L // SEG; NCH = SEG // C; NCS = L // C
                sg_t, sg_r = sgc.next()
                for sgi in range(NSEG):
                    tt, tt_r = f_t1.next()
                    conv(tt, tt_r, 24, t0, L, sgi * SEG, SEG)
                    for dup in range(2):
                        sy.op("act", lambda e, dup=dup, sgi=sgi: e.activation(out=sg_t[:, sgi * NCH:(sgi + 1) * NCH, dup, :],
                                                                           in_=tt[:, 0:SEG].rearrange("p (c t) -> p c t", t=C), func=AF.Sigmoid),
                              reads=[tt_r], writes=[sg_r])
                for cb in range(8):
                    ya, ya_r = yacc.next(); bs, bs_r = bsum.next(); vf, vf_r = Vf.next()
                    for d in range(2):
                        T_t, T_r = T.next()
                        if g == 0:
                            sy.op("dve", lambda e: e.memset(T_t[:], 0.0), writes=[T_r])
                        else:
                            ti, ti_r = tin.next()
                            for hh in range(2):
                                sy.dma("sp", ti[:, hh * 64:(hh + 1) * 64], st_rwkv[d, 2 * cb + hh, :, :], writes=[ti_r])
                            pt, pt_r = self.psum.next()
                            sy.op("pe", lambda e: e.transpose(pt[:, 0:64], ti[:], self.ident_f[0:64, 0:64]), reads=[ti_r, self.ident_f_r], writes=[pt_r])
                            sy.op("dve", lambda e: e.tensor_copy(out=T_t[:], in_=pt[:, 0:64]), reads=[pt_r], writes=[T_r])
                        Tb_t, Tb_r = Tb.next()
                        sy.op("act", lambda e: e.activation(out=Tb_t[:], in_=T_t[:], func=AF.Copy), reads=[T_r], writes=[Tb_r])
                        for sgi in (range(NSEG) if d == 0 else range(NSEG - 1, -1, -1)):
                            s0 = sgi * SEG
                            r_, r_r = f_r.next(); kb, kb_r = f_kb.next(); vb, vb_r = f_vb.next(); kk, kk_r = f_kk.next()
                            t1, t1_r = f_t1.next(); t2, t2_r = f_t2.next(); t3, t3_r = f_t3.next()
                            conv(r_, r_r, cb, t0, L, s0, SEG); conv(kb, kb_r, 8 + cb, t0, L, s0, SEG); conv(vb, vb_r, 16 + cb, t0, L, s0, SEG)
                            conv(t1, t1_r, 25, t0, L, s0, SEG); conv(t2, t2_r, 26, t0, L, s0, SEG)
                            tw, tw_r = twc.next(); ab_, ab_r = acb.next()
                            sy.op("act", lambda e: e.activation(out=tw[:, 0:SEG], in_=t1[:, 0:SEG], func=AF.Tanh), reads=[t1_r], writes=[tw_r])
                            sy.op("dve", lambda e: e.tensor_copy(out=ab_[:, 0:SEG], in_=t2[:, 0:SEG]), reads=[t2_r], writes=[ab_r])
                            sy.op("dve", lambda e: e.tensor_scalar(out=kk[:, 0:SEG], in0=kb[:, 0:SEG], scalar1=kkT[:, cb:cb + 1], scalar2=None, op0=ALU.mult),
                                  reads=[kb_r, cst_r], writes=[kk_r])
                            sy.op("pool", lambda e: e.tensor_tensor(out=t3[:, 0:SEG], in0=kk[:, 0:SEG], in1=kk[:, 0:SEG], op=ALU.mult), reads=[kk_r], writes=[t3_r])
                            pt, pt_r = self.psum.next()
                            sy.op("pe", lambda e: e.matmul(pt[:, 0:SEG], lhsT=bd1[:], rhs=t3[:, 0:SEG], start=True, stop=True), reads=[t3_r, cst_r], writes=[pt_r])
                            sy.op("act", lambda e: e.activation(out=t3[:, 0:SEG], in_=pt[:, 0:SEG], func=AF.Sqrt), reads=[pt_r], writes=[t3_r])
                            sy.op("dve", lambda e: e.tensor_scalar(out=t3[:, 0:SEG], in0=t3[:, 0:SEG], scalar1=1e-12, scalar2=None, op0=ALU.max), reads=[t3_r], writes=[t3_r])
                            sy.op("dve", lambda e: e.reciprocal(out=t3[:, 0:SEG], in_=t3[:, 0:SEG]), reads=[t3_r], writes=[t3_r])
                            sy.op("dve", lambda e: e.tensor_tensor(out=kk[:, 0:SEG], in0=kk[:, 0:SEG], in1=t3[:, 0:SEG], op=ALU.mult), reads=[kk_r, t3_r], writes=[kk_r])
                            a_, a_r = f_a.next(); sg, sg_r2 = f_sg.next()
                            dsl = slice(d * 64, (d + 1) * 64)
                            pt, pt_r = self.psum.next()
                            sy.op("pe", lambda e: e.matmul(pt[:, 0:SEG], lhsT=wup[dsl, cb * 128:(cb + 1) * 128], rhs=tw[dsl, 0:SEG], start=True, stop=True),
                                  reads=[tw_r, cst_r], writes=[pt_r])
                            sy.op("act", lambda e: e.activation(out=sg[:, 0:SEG], in_=pt[:, 0:SEG], func=AF.Sigmoid, bias=w0T[:, d, cb:cb + 1]), reads=[pt_r, cst_r], writes=[sg_r2])
                            pt, pt_r = self.psum.next()
                            sy.op("pe", lambda e: e.matmul(pt[:, 0:SEG], lhsT=aup[dsl, cb * 128:(cb + 1) * 128], rhs=ab_[dsl, 0:SEG], start=True, stop=True),
                                  reads=[ab_r, cst_r], writes=[pt_r])
                            sy.op("act", lambda e: e.activation(out=a_[:, 0:SEG], in_=pt[:, 0:SEG], func=AF.Sigmoid, bias=a0T[:, d, cb:cb + 1]), reads=[pt_r, cst_r], writes=[a_r])
                            b_, b_r = f_b.next(); kd, kd_r = f_kd.next()
                            sy.op("pool", lambda e: e.tensor_tensor(out=b_[:, 0:SEG], in0=kk[:, 0:SEG], in1=a_[:, 0:SEG], op=ALU.mult), reads=[kk_r, a_r], writes=[b_r])
                            sy.op("dve", lambda e: e.tensor_scalar(out=kd[:, 0:SEG], in0=a_[:, 0:SEG], scalar1=-1.0, scalar2=kaT[:, cb:cb + 1], op0=ALU.add, op1=ALU.mult),
                                  reads=[a_r, cst_r], writes=[kd_r])
                            sy.op("dve", lambda e: e.scalar_tensor_tensor(out=kd[:, 0:SEG], in0=kd[:, 0:SEG], scalar=1.0, in1=kb[:, 0:SEG], op0=ALU.add, op1=ALU.mult),
                                  reads=[kd_r, kb_r], writes=[kd_r])
                            cs, cs_r = f_cs.next(); ex, ex_r = f_ex.next()
                            sy.op("dve", lambda e: e.tensor_tensor_scan(out=cs[:, 0:SEG], data0=rst[:, 0:SEG], data1=sg[:, 0:SEG], initial=0.0, op0=ALU.mult, op1=ALU.add),
                                  reads=[sg_r2, cst_r], writes=[cs_r])
                            c3 = cs[:, 0:SEG].rearrange("p (c t) -> p c t", t=C)
                            tot = c3[:, :, C - 1:C].broadcast_to([128, NCH, C])
                            sy.op("dve", lambda e: e.tensor_tensor(out=ex[:, 0:SEG].rearrange("p (c t) -> p c t", t=C), in0=tot, in1=c3, op=ALU.subtract), reads=[cs_r], writes=[ex_r])
                            WC_t, WC_r = WCt.next()
                            sy.op("act", lambda e: e.activation(out=WC_t[:, 0:NCH], in_=c3[:, :, C - 1], func=AF.Exp, scale=-LAM), reads=[cs_r], writes=[WC_r])
                            if d == 1:
                                sy.op("dve", lambda e: e.tensor_tensor(out=t1[:, 0:SEG], in0=cs[:, 0:SEG], in1=sg[:, 0:SEG], op=ALU.subtract), reads=[cs_r, sg_r2, t1_r], writes=[t1_r])
                                sy.op("dve", lambda e: e.tensor_tensor(out=cs[:, 0:SEG], in0=ex[:, 0:SEG], in1=sg[:, 0:SEG], op=ALU.add), reads=[ex_r, sg_r2, t1_r], writes=[cs_r])
                                tmi, tmi_r = t1, t1_r
                                exc, exc_r = ex, ex_r
                            else:
                                sy.op("dve", lambda e: e.tensor_tensor(out=t1[:, 0:SEG], in0=cs[:, 0:SEG], in1=sg[:, 0:SEG], op=ALU.subtract), reads=[cs_r, sg_r2, t1_r], writes=[t1_r])
                                tmi, tmi_r = ex, ex_r
                                exc, exc_r = t1, t1_r
                            eW, eW_r = f_e[0].next(); eWx, eWx_r = f_e[1].next(); eWi, eWi_r = f_e[2].next(); eWC, eWC_r = f_e[3].next()
                            sy.op("act", lambda e: e.activation(out=eW[:, 0:SEG], in_=cs[:, 0:SEG], func=AF.Exp, scale=-LAM), reads=[cs_r], writes=[eW_r])
                            sy.op("act", lambda e: e.activation(out=eWx[:, 0:SEG], in_=exc[:, 0:SEG], func=AF.Exp, scale=-LAM), reads=[exc_r], writes=[eWx_r])
                            sy.op("act", lambda e: e.activation(out=eWi[:, 0:SEG], in_=cs[:, 0:SEG], func=AF.Exp, scale=LAM), reads=[cs_r], writes=[eWi_r])
                            sy.op("act", lambda e: e.activation(out=eWC[:, 0:SEG], in_=tmi[:, 0:SEG], func=AF.Exp, scale=-LAM), reads=[tmi_r], writes=[eWC_r])
                            AR_t, AR_r = AR.next(); BK_t, BK_r = BK.next(); BKh_t, BKh_r = BKh.next(); V_t, V_r = Vbd.next(); PR_t, PR_r = PRd.next()
                            bdw("dve", AR_t, AR_r, 0, kk, kk_r, eWx, eWx_r, SEG, NCH)
                            bdw("pool", AR_t, AR_r, 1, r_, r_r, eW, eW_r, SEG, NCH)
                            bdw("dve", BK_t, BK_r, 0, b_, b_r, eWi, eWi_r, SEG, NCH)
                            bdw("pool", BK_t, BK_r, 1, kd, kd_r, eWi, eWi_r, SEG, NCH)
                            bdw("dve", BKh_t, BKh_r, 0, b_, b_r, eWC, eWC_r, SEG, NCH)
                            bdw("pool", BKh_t, BKh_r, 1, kd, kd_r, eWC, eWC_r, SEG, NCH)
                            bdw("dve", V_t, V_r, None, vb, vb_r, None, None, SEG, NCH)
                            sy.op("dve", lambda e: e.scalar_tensor_tensor(out=t2[:, 0:SEG], in0=r_[:, 0:SEG], scalar=rkT[:, cb:cb + 1], in1=kd[:, 0:SEG], op0=ALU.mult, op1=ALU.mult),
                                  reads=[r_r, kd_r, cst_r, t2_r], writes=[t2_r])
                            bdw("pool", PR_t, PR_r, None, t2, t2_r, None, None, SEG, NCH)
                            MM_t, MM_r = MM.next(); Z_t, Z_r = Zs.next(); Vt_t, Vt_r = Vt.next(); BKtm_t, BKtm_r = BKtm.next()
                            pb_, pb_r = self.psx, self.psx_r
                            for ci in range(NCH):
                                gci = sgi * NCH + ci
                                p1, p1_r = self.psum.next()
                                arflat = AR_t[:, ci, :, :].rearrange("p a b -> p (a b)")
                                sy.op("pe", lambda e, ci=ci, arflat=arflat: e.matmul(p1[:, 0:256], lhsT=BK_t[:, ci, 0, :], rhs=arflat, start=True, stop=True), reads=[BK_r, AR_r], writes=[p1_r])
                                sy.op("pe", lambda e, ci=ci, arflat=arflat: e.matmul(p1[:, 256:512], lhsT=BK_t[:, ci, 1, :], rhs=arflat, start=True, stop=True), reads=[BK_r, AR_r], writes=[p1_r])
                                sy.op("dve", lambda e, ci=ci: e.tensor_tensor(out=MM_t[:, ci, :, :].rearrange("p a b -> p (a b)"), in0=p1[:, :],
                                                                            in1=mask1[:, d, :, :].rearrange("p a b -> p (a b)"), op=ALU.mult), reads=[p1_r, cst_r], writes=[MM_r])
                                p2, p2_r = self.psum.next()
                                sy.op("pe", lambda e, ci=ci: e.matmul(p2[:, 0:128], lhsT=AR_t[:, ci, 0, :], rhs=BK_t[:, ci, 0, :], start=True, stop=True), reads=[BK_r, AR_r], writes=[p2_r])
                                pq, pq_r = Pq.next()
                                sy.op("dve", lambda e, ci=ci, pq=pq: e.tensor_tensor(out=pq[:, 0, :], in0=p1[:, 0:128], in1=mask1[:, d, 0, :], op=ALU.mult), reads=[p1_r, cst_r], writes=[pq_r])
                                sy.op("dve", lambda e, pq=pq: e.tensor_tensor(out=pq[:, 1, :], in0=p2[:, 0:128], in1=maskQ[:, d, :], op=ALU.mult), reads=[p2_r, cst_r], writes=[pq_r])
                                z, z_r = Zt.next()
                                sy.op("pool", lambda e, z=z, pq=pq: e.tensor_tensor(out=z[:], in0=self.ident_f[:], in1=pq[:, 0, :], op=ALU.subtract), reads=[pq_r, self.ident_f_r], writes=[z_r])
                                for lev in range(5):
                                    pp, pp_r = self.psum.next()
                                    sy.op("pe", lambda e, pq=pq: e.matmul(pp[:, 0:128], lhsT=pq[:, 1, :], rhs=pq[:, 0, :], start=True, stop=True), reads=[pq_r], writes=[pp_r])
                                    sy.op("pe", lambda e, pq=pq: e.matmul(pp[:, 128:256], lhsT=pq[:, 0, :], rhs=pq[:, 1, :], start=True, stop=True), reads=[pq_r], writes=[pp_r])
                                    pq, pq_r = Pq.next()
                                    sy.op("act", lambda e, pq=pq, pp=pp: e.activation(out=pq[:].rearrange("p a b -> p (a b)"), in_=pp[:, 0:256], func=AF.Copy), reads=[pp_r], writes=[pq_r])
                                    pz, pz_r = self.psum.next()
                                    sy.op("pe", lambda e, pq=pq, z=z: e.matmul(pz[:, 0:128], lhsT=pq[:, 1, :], rhs=z[:], start=True, stop=True), reads=[pq_r, z_r], writes=[pz_r])
                                    zn, zn_r = Zt.next()
                                    if lev == 4:
                                        sy.op("dve", lambda e, z=z, pz=pz, ci=ci: e.tensor_tensor(out=Z_t[:, ci, :], in0=pz[:, 0:128], in1=z[:], op=ALU.add), reads=[pz_r, z_r], writes=[Z_r])
                                    else:
                                        sy.op("dve", lambda e, z=z, pz=pz, zn=zn: e.tensor_tensor(out=zn[:], in0=pz[:, 0:128], in1=z[:], op=ALU.add), reads=[pz_r, z_r], writes=[zn_r])
                                        z, z_r = zn, zn_r
                                pv, pv_r = self.psum.next()
                                sy.op("pe", lambda e, ci=ci: e.matmul(pv[:, 0:64], lhsT=V_t[:, ci, :], rhs=E[:], start=True, stop=True), reads=[V_r, cst_r], writes=[pv_r])
                                sy.op("act", lambda e, ci=ci: e.activation(out=Vt_t[:, ci, :], in_=pv[:, 0:64], func=AF.Copy), reads=[pv_r], writes=[Vt_r])
                                if d == 0:
                                    sy.op("dve", lambda e, gci=gci: e.tensor_copy(out=vf[:, gci, :], in_=pv[:, 0:64]), reads=[pv_r], writes=[vf_r])
                                pbk, pbk_r = self.psum.next()
                                pbkb = pbk[:].bitcast(BF16)
                                for k2 in range(2):
                                    sy.op("pe", lambda e, ci=ci, k2=k2: e.transpose(pbkb[:, k2 * 128:(k2 + 1) * 128], BKh_t[:, ci, k2, :], self.ident_bf[:]), reads=[BKh_r, self.ident_r], writes=[pbk_r])
                                sy.op("act", lambda e, ci=ci: e.activation(out=BKtm_t[:, ci, :, :].rearrange("p a b -> p (a b)"), in_=pbkb[:, 0:256], func=AF.Copy), reads=[pbk_r], writes=[BKtm_r])
                                sy.op("pe", lambda e, ci=ci: e.matmul(pb_[:, 2 * ci:2 * ci + 2], lhsT=PR_t[:, ci, :], rhs=ones_c[:], start=True, stop=True), reads=[PR_r, cst_r], writes=[pb_r])
                            pbv = pb_[:, 0:2 * NCH].rearrange("p (c two) -> p c two", two=2)[:, :, 0]
                            bsl = bs[:, sgi * NCH:(sgi + 1) * NCH]
                            if d == 0:
                                sy.op("dve", lambda e: e.tensor_copy(out=bsl, in_=pbv), reads=[pb_r], writes=[bs_r])
                            else:
                                sy.op("dve", lambda e: e.tensor_tensor(out=bsl, in0=pbv, in1=bsl, op=ALU.add), reads=[pb_r, bs_r], writes=[bs_r])
                            for ci in (range(NCH) if d == 0 else range(NCH - 1, -1, -1)):
                                gci = sgi * NCH + ci
                                pr, pr_r = self.psum.next()
                                sy.op("pe", lambda e, ci=ci, Tb_t=Tb_t: e.matmul(pr[:, 0:64], lhsT=AR_t[:, ci, 0, :], rhs=Tb_t[:], start=True, stop=False), reads=[AR_r, Tb_r], writes=[pr_r])
                                sy.op("pe", lambda e, ci=ci: e.matmul(pr[:, 0:64], lhsT=MM_t[:, ci, 2, :], rhs=Vt_t[:, ci, :], start=False, stop=True), reads=[MM_r, Vt_r], writes=[pr_r])
                                rh, rh_r = RH.next()
                                sy.op("dve", lambda e, rh=rh, pr=pr: e.tensor_scalar(out=rh[:], in0=pr[:, 0:64], scalar1=-1.0, scalar2=None, op0=ALU.mult), reads=[pr_r], writes=[rh_r])
                                pu, pu_r = self.psum.next()
                                sy.op("pe", lambda e, ci=ci, rh=rh: e.matmul(pu[:, 0:64], lhsT=Z_t[:, ci, :], rhs=rh[:], start=True, stop=True), reads=[Z_r, rh_r], writes=[pu_r])
                                u, u_r = Ut.next()
                                sy.op("act", lambda e, u=u, pu=pu: e.activation(out=u[:], in_=pu[:, 0:64], func=AF.Copy), reads=[pu_r], writes=[u_r])
                                py, py_r = self.psum.next()
                                sy.op("pe", lambda e, ci=ci, Tb_t=Tb_t: e.matmul(py[:, 0:64], lhsT=AR_t[:, ci, 1, :], rhs=Tb_t[:], start=True, stop=False), reads=[AR_r, Tb_r], writes=[py_r])
                                sy.op("pe", lambda e, ci=ci, u=u: e.matmul(py[:, 0:64], lhsT=MM_t[:, ci, 1, :], rhs=u[:], start=False, stop=False), reads=[MM_r, u_r], writes=[py_r])
                                sy.op("pe", lambda e, ci=ci: e.matmul(py[:, 0:64], lhsT=MM_t[:, ci, 3, :], rhs=Vt_t[:, ci, :], start=False, stop=True), reads=[MM_r, Vt_r], writes=[py_r])
                                if d == 0:
                                    sy.op("act", lambda e, gci=gci, py=py: e.activation(out=ya[:, gci, :], in_=py[:, 0:64], func=AF.Copy), reads=[py_r], writes=[ya_r])
                                else:
                                    sy.op("dve", lambda e, gci=gci, py=py: e.tensor_tensor(out=ya[:, gci, :], in0=py[:, 0:64], in1=ya[:, gci, :], op=ALU.add), reads=[py_r, ya_r], writes=[ya_r])
                                pT, pT_r = self.psum.next()
                                sy.op("pe", lambda e, ci=ci, u=u: e.matmul(pT[:, 0:64], lhsT=BKtm_t[:, ci, 0, :], rhs=u[:], start=True, stop=False), reads=[BKtm_r, u_r], writes=[pT_r])
                                sy.op("pe", lambda e, ci=ci: e.matmul(pT[:, 0:64], lhsT=BKtm_t[:, ci, 1, :], rhs=Vt_t[:, ci, :], start=False, stop=True), reads=[BKtm_r, Vt_r], writes=[pT_r])
                                sy.op("dve", lambda e, ci=ci, pT=pT: e.scalar_tensor_tensor(out=T_t[:], in0=T_t[:], scalar=WC_t[:, ci:ci + 1], in1=pT[:, 0:64], op0=ALU.mult, op1=ALU.add),
                                      reads=[pT_r, T_r, WC_r], writes=[T_r])
                                Tb_t, Tb_r = Tb.next()
                                sy.op("act", lambda e, Tb_t=Tb_t: e.activation(out=Tb_t[:], in_=T_t[:], func=AF.Copy), reads=[T_r], writes=[Tb_r])
                        if g == 0:
                            pt, pt_r = self.psum.next()
                            sy.op("pe", lambda e: e.transpose(pt[0:64, 0:128], T_t[:], self.ident_f[:]), reads=[T_r, self.ident_f_r], writes=[pt_r])
                            to, to_r = tout.next()
                            sy.op("dve", lambda e: e.tensor_copy(out=to[:], in_=pt[0:64, 0:128]), reads=[pt_r], writes=[to_r])
                            for hh in range(2):
                                sy.dma("sp", out_rwkv[d, si, 2 * cb + hh, :, :], to[:, hh * 64:(hh + 1) * 64], reads=[to_r], writes=[out_r])
                    gn_t, gn_r = gn.next(); yq, yq_r = ysq.next(); orw, orw_r = orow.next()
                    yv = ya[:, 0:NCS, :]
                    sy.op("dve", lambda e: e.tensor_reduce(out=gn_t[:, 0:NCS, 0], in_=yv, axis=AX.X, op=ALU.add), reads=[ya_r], writes=[gn_r])
                    sy.op("pool", lambda e: e.tensor_tensor(out=yq[:, 0:NCS, :], in0=yv, in1=yv, op=ALU.mult), reads=[ya_r], writes=[yq_r])
                    sy.op("dve", lambda e: e.tensor_reduce(out=gn_t[:, 0:NCS, 1], in_=yq[:, 0:NCS, :], axis=AX.X, op=ALU.add), reads=[yq_r, gn_r], writes=[gn_r])
                    sy.op("dve", lambda e: e.tensor_scalar(out=gn_t[:, 0:NCS, 0], in0=gn_t[:, 0:NCS, 0], scalar1=1.0 / 64, scalar2=None, op0=ALU.mult), reads=[gn_r], writes=[gn_r])
                    sy.op("dve", lambda e: e.tensor_tensor(out=gn_t[:, 0:NCS, 2], in0=gn_t[:, 0:NCS, 0], in1=gn_t[:, 0:NCS, 0], op=ALU.mult), reads=[gn_r], writes=[gn_r])
                    sy.op("dve", lambda e: e.scalar_tensor_tensor(out=gn_t[:, 0:NCS, 1], in0=gn_t[:, 0:NCS, 1], scalar=1.0 / 64, in1=gn_t[:, 0:NCS, 2], op0=ALU.mult, op1=ALU.subtract),
                          reads=[gn_r], writes=[gn_r])
                    sy.op("dve", lambda e: e.tensor_scalar(out=gn_t[:, 0:NCS, 1], in0=gn_t[:, 0:NCS, 1], scalar1=64e-5, scalar2=None, op0=ALU.add), reads=[gn_r], writes=[gn_r])
                    sy.op("act", lambda e: e.activation(out=gn_t[:, 0:NCS, 1], in_=gn_t[:, 0:NCS, 1], func=AF.Sqrt), reads=[gn_r], writes=[gn_r])
                    sy.op("dve", lambda e: e.reciprocal(out=gn_t[:, 0:NCS, 1], in_=gn_t[:, 0:NCS, 1]), reads=[gn_r], writes=[gn_r])
                    bc = lambda col: gn_t[:, 0:NCS, col:col + 1].broadcast_to([128, NCS, 64])
                    sy.op("dve", lambda e: e.tensor_tensor(out=yv, in0=yv, in1=bc(0), op=ALU.subtract), reads=[ya_r, gn_r], writes=[ya_r])
                    sy.op("dve", lambda e: e.tensor_tensor(out=yv, in0=yv, in1=bc(1), op=ALU.mult), reads=[ya_r, gn_r], writes=[ya_r])
                    sy.op("dve", lambda e: e.tensor_tensor(out=yv, in0=yv, in1=lnw[:, cb:cb + 1, :].broadcast_to([128, NCS, 64]), op=ALU.mult), reads=[ya_r, cst_r], writes=[ya_r])
                    sy.op("dve", lambda e: e.tensor_tensor(out=yv, in0=yv, in1=lnb[:, cb:cb + 1, :].broadcast_to([128, NCS, 64]), op=ALU.add), reads=[ya_r, cst_r], writes=[ya_r])
                    sy.op("pool", lambda e: e.tensor_tensor(out=yq[:, 0:NCS, :], in0=vf[:, 0:NCS, :], in1=bs[:, 0:NCS].unsqueeze(2).broadcast_to([128, NCS, 64]), op=ALU.mult),
                          reads=[vf_r, bs_r, yq_r], writes=[yq_r])
                    sy.op("dve", lambda e: e.tensor_tensor(out=yv, in0=yv, in1=yq[:, 0:NCS, :], op=ALU.add), reads=[ya_r, yq_r], writes=[ya_r])
                    for gci in range(NCS):
                        pg, pg_r = self.psum.next()
                        sy.op("pe", lambda e, gci=gci: e.matmul(pg[:, 0:128], lhsT=sg_t[:, gci, :, :].rearrange("p a b -> p (a b)"), rhs=gup[:, cb * 128:(cb + 1) * 128], start=True, stop=True),
                              reads=[sg_r, cst_r], writes=[pg_r])
                        yb, yb_r = ybd.next()
                        for hh in range(2):
                            ps_ = slice(hh * 64, (hh + 1) * 64)
                            sy.op("dve", lambda e, gci=gci, ps_=ps_, yb=yb, pg=pg: e.tensor_tensor(out=yb[ps_, ps_], in0=pg[ps_, ps_], in1=ya[ps_, gci, :], op=ALU.mult),
                                  reads=[pg_r, ya_r], writes=[yb_r])
                        po, po_r = self.psum.next()
                        sy.op("pe", lambda e, yb=yb: e.matmul(po[:, 0:64], lhsT=yb[:], rhs=E[:], start=True, stop=True), reads=[yb_r, cst_r], writes=[po_r])
                        sy.op("act", lambda e, gci=gci, po=po: e.activation(out=orw[:, gci * C:(gci + 1) * C], in_=po[:, 0:64], func=AF.Copy), reads=[po_r], writes=[orw_r])
                    sy.dma("sp", oT[1024 + cb * 128:1024 + (cb + 1) * 128, t0:t0 + L], orw[:, 0:L], reads=[orw_r], writes=[self.oT_r])
            sy.barrier()

    def gate_bc_build(self, gbc, gbc_r, l, g, which, tmp, tmp_r, ones, ones_r):
        sy = self.sy
        for kc in range(KC):
            sy.op("dve", lambda e, kc=kc: e.tensor_scalar(out=tmp[:], in0=self.ident_f[:], scalar1=self.gate[:, l, g, which, kc:kc + 1], scalar2=None, op0=ALU.mult),
                  reads=[self.ident_f_r, self.gate_r, tmp_r], writes=[tmp_r])
            if self.cfg.stages in (4.121, 0.54):
                continue
            pt, pt_r = self.psum.next()
            sy.op("pe", lambda e: e.matmul(pt[:, 0:128], lhsT=ones[:], rhs=tmp[:], start=True, stop=True), reads=[tmp_r, ones_r], writes=[pt_r])
            if self.cfg.stages in (4.122, 0.55):
                continue
            sy.op("dve", lambda e, kc=kc: e.tensor_copy(out=gbc[:, kc * 128:(kc + 1) * 128], in_=pt[:, 0:128]), reads=[pt_r], writes=[gbc_r])

    def dense_residual(self, st, lT, lT_r, KCx, W, T, t0, x_src, x_dst, x_dst_r, x_src_r, l, g, which, yscr, B_):
        sy = self.sy
        ntl = T // 128
        ssq, ssq_r = B_["ssq"].next()
        yscr_r = B_["yscr_r"]
        for nch in range(4):
            wts = []
            for k0 in range(0, KCx, 22):
                kk_ = min(22, KCx - k0)
                wt, wt_r = B_["wbig"].next()
                src = W[k0 * 128:(k0 + kk_) * 128, nch * 512:(nch + 1) * 512].rearrange("(kc p) n -> p kc n", p=128)
                sy.dma("pool", wt[:, 0:kk_, :], src, writes=[wt_r])
                wts.append((wt, wt_r, k0, kk_))
            for i in range(ntl):
                pt, pt_r = self.psum.next()
                for (wt, wt_r, k0, kk_) in wts:
                    for kc in range(kk_):
                        sy.op("pe", lambda e, kc=kc, k0=k0, wt=wt, i=i: e.matmul(pt[:], lhsT=lT[:, k0 + kc, i * 128:(i + 1) * 128], rhs=wt[:, kc, :],
                                                                             start=(k0 + kc == 0), stop=(k0 + kc == KCx - 1)), reads=[lT_r, wt_r], writes=[pt_r])
                et, et_r = B_["ev"].next()
                sy.op("dve", lambda e, et=et: e.tensor_copy(out=et[:], in_=pt[:]), reads=[pt_r], writes=[et_r])
                jk, jk_r = B_["junk"].next()
                sy.op("act", lambda e, jk=jk, i=i, nch=nch, et=et: e.activation(out=jk[:], in_=et[:], func=AF.Square, accum_out=ssq[:, i, nch:nch + 1]), reads=[et_r], writes=[jk_r, ssq_r])
                sy.dma("sp", yscr[i * 128:(i + 1) * 128, nch * 512:(nch + 1) * 512], et[:], reads=[et_r], writes=[yscr_r])
        if self.cfg.stages == 4.11:
            return
        gbc, gbc_r = B_["gbc"], B_["gbc_r"]
        self.gate_bc_build(gbc, gbc_r, l, g, which, B_["tmp"], B_["tmp_r"], B_["ones"], B_["ones_r"])
        if self.cfg.stages in (4.12, 4.121, 4.122):
            return
        for i in range(ntl):
            yt, yt_r = self.xring.next(); xt, xt_r = self.xring.next()
            sy.dma("sp", yt[:], yscr[i * 128:(i + 1) * 128, :], reads=[yscr_r], writes=[yt_r])
            sy.dma("act", xt[:], x_src[t0 + i * 128:t0 + (i + 1) * 128, :], reads=[x_src_r] if x_src_r else [], writes=[xt_r])
            ss, ss_r = self.ssring.next()
            sy.op("dve", lambda e, i=i: e.tensor_reduce(out=ss[:, 0:1], in_=ssq[:, i, :], axis=AX.X, op=ALU.add), reads=[ssq_r], writes=[ss_r])
            sy.op("dve", lambda e: e.tensor_scalar(out=ss[:, 1:2], in0=ss[:, 0:1], scalar1=1.0 / D, scalar2=RMS_EPS, op0=ALU.mult, op1=ALU.add), reads=[ss_r], writes=[ss_r])
            sy.op("act", lambda e: e.activation(out=ss[:, 2:3], in_=ss[:, 1:2], func=AF.Sqrt), reads=[ss_r], writes=[ss_r])
            sy.op("dve", lambda e: e.reciprocal(out=ss[:, 3:4], in_=ss[:, 2:3]), reads=[ss_r], writes=[ss_r])
            sy.op("dve", lambda e: e.scalar_tensor_tensor(out=yt[:], in0=yt[:], scalar=ss[:, 3:4], in1=gbc[:], op0=ALU.mult, op1=ALU.mult), reads=[yt_r, ss_r, gbc_r], writes=[yt_r])
            sy.op("pool", lambda e: e.tensor_tensor(out=xt[:], in0=xt[:], in1=yt[:], op=ALU.add), reads=[yt_r, xt_r], writes=[xt_r])
            sy.dma("sp", x_dst[t0 + i * 128:t0 + (i + 1) * 128, :], xt[:], reads=[xt_r], writes=[x_dst_r])

    def dense_bufs(self, st, yscr):
        B_ = {}
        B_["wbig"] = Ring([self.sb(st, f"wbig{i}", [128, 22, 512], BF16) for i in range(4)], "wbig")
        B_["ev"] = Ring([self.sb(st, f"dev{i}", [128, 512]) for i in range(3)], "dev")
        B_["junk"] = Ring([self.sb(st, f"djunk{i}", [128, 512], BF16) for i in range(2)], "djunk")
        B_["ssq"] = Ring([self.sb(st, f"dssq{i}", [128, 4, 4]) for i in range(2)], "dssq")
        B_["gbc"] = self.sb(st, "gbc", [128, D]); B_["gbc_r"] = Res("gbc")
        B_["tmp"] = self.sb(st, "gtmp", [128, 128]); B_["tmp_r"] = Res("gtmp")
        B_["ones"] = self.sb(st, "gones", [128, 128]); B_["ones_r"] = Res("gones")
        self.sy.op("dve", lambda e: e.memset(B_["ones"][:], 1.0), writes=[B_["ones_r"]])
        B_["yscr_r"] = Res("yscr", multi=True)
        self.xring = Ring([self.sb(st, f"xt{i}", [128, D]) for i in range(3)], "xt")
        self.xbring = Ring([self.sb(st, f"xb{i}", [128, D], BF16) for i in range(2)], "xb")
        self.ssring = Ring([self.sb(st, f"ss{i}", [128, 4]) for i in range(4)], "ss")
        return B_

    def mix_out_and_ffn(self, l, oT, W_out, Wg, Wu, Wd, x_in, x_in_r, x_mid, x_out, x_out_final, yscr):
        sy, cfg = self.sy, self.cfg
        sy.q["pool"].ring_limit = 5
        with contextlib.ExitStack() as st:
            B_ = self.dense_bufs(st, yscr)
            TBm = 512
            hT = self.sb(st, "hT2", [128, KC, TBm], BF16); hT_r = Res("hT2")
            actT = self.sb(st, "actT", [128, FC, TBm], BF16); actT_r = Res("actT")
            sil = Ring([self.sb(st, f"sil{i}", [128, 512]) for i in range(2)], "sil")
            x_mid_r = Res("x_mid", multi=True); x_out_r = Res("x_out", multi=True)
            for (t0, T, g) in self.blocks():
                sy.dma("pool", hT[:, :, 0:T], oT[:, t0:t0 + T].rearrange("(kc p) t -> p kc t", p=128), reads=[self.oT_r, hT_r], writes=[hT_r])
                if cfg.stages == 4.05:
                    break
                self.dense_residual(st, hT, hT_r, KC, W_out, T, t0, x_in, x_mid, x_mid_r, x_in_r, l, g, 0, yscr, B_)
                if cfg.stages in (4.1, 4.11, 4.12, 4.121, 4.122):
                    break
                self.make_hT(st, x_mid[t0:t0 + T, :], T, l, g, 1, hT, hT_r, self.ident_bf, self.ident_r, src_r=x_mid_r)
                for fc4 in range(11):
                    wg, wg_r = B_["wbig"].next()
                    sy.dma("pool", wg[:, 0:KC, :], Wg[:, fc4 * 512:(fc4 + 1) * 512].rearrange("(kc p) n -> p kc n", p=128), writes=[wg_r])
                    wu, wu_r = B_["wbig"].next()
                    sy.dma("pool", wu[:, 0:KC, :], Wu[:, fc4 * 512:(fc4 + 1) * 512].rearrange("(kc p) n -> p kc n", p=128), writes=[wu_r])
                    for sub in range(4):
                        fc = fc4 * 4 + sub
                        for tb in range(0, T, 512):
                            tw = min(512, T - tb)
                            pg, pg_r = self.psum.next(); pu, pu_r = self.psum.next()
                            for kc in range(KC):
                                sy.op("pe", lambda e, kc=kc, sub=sub, tb=tb, tw=tw: e.matmul(pg[:, 0:tw], lhsT=wg[:, kc, sub * 128:(sub + 1) * 128], rhs=hT[:, kc, tb:tb + tw],
                                                                                         start=(kc == 0), stop=(kc == KC - 1)), reads=[wg_r, hT_r], writes=[pg_r])
                            for kc in range(KC):
                                sy.op("pe", lambda e, kc=kc, sub=sub, tb=tb, tw=tw: e.matmul(pu[:, 0:tw], lhsT=wu[:, kc, sub * 128:(sub + 1) * 128], rhs=hT[:, kc, tb:tb + tw],
                                                                                         start=(kc == 0), stop=(kc == KC - 1)), reads=[wu_r, hT_r], writes=[pu_r])
                            s_t, s_r = sil.next()
                            sy.op("act", lambda e, tw=tw, s_t=s_t: e.activation(out=s_t[:, 0:tw], in_=pg[:, 0:tw], func=AF.Silu), reads=[pg_r], writes=[s_r])
                            sy.op("dve", lambda e, tw=tw, s_t=s_t, fc=fc, tb=tb: e.tensor_tensor(out=actT[:, fc, tb:tb + tw], in0=pu[:, 0:tw], in1=s_t[:, 0:tw], op=ALU.mult),
                                  reads=[pu_r, s_r], writes=[actT_r])
                if cfg.stages == 4.3:
                    break
                dst = x_out_final if x_out_final is not None else x_out
                self.dense_residual(st, actT, actT_r, FC, Wd, T, t0, x_mid, dst, x_out_r, x_mid_r, l, g, 1, yscr, B_)
            sy.barrier()
        sy.q["pool"].ring_limit = 2
        return x_out_r

    def layer1(self, x1, x1_r, c_w_in, qkn_bc, rope, cache_k, cache_v, QT, KT, Vs, oT, out_k, out_v):
        sy, cfg = self.sy, self.cfg
        QT_r, KT_r, Vs_r = Res("QT", multi=True), Res("KT", multi=True), Res("Vs", multi=True)
        okv_r = Res("okv", multi=True)
        sy.q["pool"].ring_limit = 5
        with contextlib.ExitStack() as st:
            self.setup_dense_bufs(st)
            hT = self.sb(st, "hT3", [128, KC, 512], BF16); hT_r = Res("hT3")
            qn = self.sb(st, "qn_bc", [128, 2, 128]); cst_r = Res("l1c")
            sy.dma("sp", qn[:], qkn_bc[:, :, :], writes=[cst_r])
            qt = Ring([self.sb(st, f"l1q{i}", [128, 4, 128]) for i in range(2)], "l1q")
            sq = Ring([self.sb(st, f"l1sq{i}", [128, 4, 128]) for i in range(2)], "l1sq")
            rs = Ring([self.sb(st, f"l1rs{i}", [128, 8]) for i in range(3)], "l1rs")
            cs_t = Ring([self.sb(st, f"l1cs{i}", [128, 2, 64]) for i in range(2)], "l1cs")
            rt = Ring([self.sb(st, f"l1rt{i}", [128, 4, 2, 2, 32]) for i in range(2)], "l1rt")
            r2 = Ring([self.sb(st, f"l1r2{i}", [128, 4, 2, 32]) for i in range(4)], "l1r2")
            qb = Ring([self.sb(st, f"l1qb{i}", [128, 4, 128], BF16) for i in range(2)], "l1qb")
            tq = Ring([self.sb(st, f"l1tq{i}", [128, 4, 128]) for i in range(3)], "l1tq")
            for (t0, T, g) in self.blocks():
                self.make_hT(st, x1[t0:t0 + T, :], T, 1, g, 0, hT, hT_r, self.ident_bf, self.ident_r, src_r=x1_r)
                for c6 in range(6):
                    wt, wt_r = self.wload(c_w_in, c6 * 512)
                    for i in range(T // 128):
                        tok = t0 + i * 128
                        pt, pt_r = self.psum.next()
                        for kc in range(KC):
                            sy.op("pe", lambda e, kc=kc, i=i: e.matmul(pt[:], lhsT=hT[:, kc, i * 128:(i + 1) * 128], rhs=wt[:, kc, :], start=(kc == 0), stop=(kc == KC - 1)),
                                  reads=[hT_r, wt_r], writes=[pt_r])
                        q_t, q_r = qt.next()
                        p3 = pt[:].rearrange("p (h d) -> p h d", d=128)
                        if c6 == 5:
                            sy.op("act", lambda e, q_t=q_t: e.activation(out=q_t[:].rearrange("p h d -> p (h d)"), in_=pt[:], func=AF.Copy), reads=[pt_r], writes=[q_r])
                            sy.dma("sp", Vs[tok:tok + 128, :], q_t[:].rearrange("p h d -> p (h d)"), reads=[q_r], writes=[Vs_r])
                            if g == 0:
                                si, tl = tok // cfg.LP, tok % cfg.LP
                                sy.dma("sp", out_v[si, tl:tl + 128, :], q_t[:].rearrange("p h d -> p (h d)"), reads=[q_r], writes=[okv_r])
                            continue
                        s_t, s_r = sq.next(); r_t, r_r = rs.next()
                        sy.op("act", lambda e, s_t=s_t: e.activation(out=s_t[:].rearrange("p h d -> p (h d)"), in_=pt[:], func=AF.Square), reads=[pt_r], writes=[s_r])
                        sy.op("dve", lambda e, s_t=s_t, r_t=r_t: e.tensor_reduce(out=r_t[:, 0:4], in_=s_t[:], axis=AX.X, op=ALU.add), reads=[s_r], writes=[r_r])
                        sy.op("dve", lambda e, r_t=r_t: e.tensor_scalar(out=r_t[:, 0:4], in0=r_t[:, 0:4], scalar1=1.0 / 128, scalar2=RMS_EPS, op0=ALU.mult, op1=ALU.add), reads=[r_r], writes=[r_r])
                        sy.op("act", lambda e, r_t=r_t: e.activation(out=r_t[:, 0:4], in_=r_t[:, 0:4], func=AF.Sqrt), reads=[r_r], writes=[r_r])
                        sy.op("dve", lambda e, r_t=r_t: e.reciprocal(out=r_t[:, 4:8], in_=r_t[:, 0:4]), reads=[r_r], writes=[r_r])
                        sy.op("dve", lambda e, q_t=q_t, r_t=r_t: e.tensor_tensor(out=q_t[:], in0=p3, in1=r_t[:, 4:8].unsqueeze(2).broadcast_to([128, 4, 128]), op=ALU.mult),
                              reads=[pt_r, r_r], writes=[q_r])
                        gi = 0 if c6 < 4 else 1
                        sy.op("pool", lambda e, q_t=q_t, gi=gi: e.tensor_tensor(out=q_t[:], in0=q_t[:], in1=qn[:, gi:gi + 1, :].broadcast_to([128, 4, 128]), op=ALU.mult),
                              reads=[q_r, cst_r], writes=[q_r])
                        if g == 0 and c6 == 4:
                            si, tl = tok // cfg.LP, tok % cfg.LP
                            sy.dma("sp", out_k[si, tl:tl + 128, :], q_t[:].rearrange("p h d -> p (h d)"), reads=[q_r], writes=[okv_r])
                        qb_t, qb_r = qb.next()
                        if g == 1:
                            c_t, c_r = cs_t.next()
                            sy.dma("act", c_t[:], rope[tok - cfg.TP:tok - cfg.TP + 128, :, :], writes=[c_r])
                            q5 = q_t[:].rearrange("p h (a f e) -> p h a f e", a=2, f=2)
                            cosb = c_t[:, 0, :].rearrange("p (a e) -> p a e", a=2).unsqueeze(1).broadcast_to([128, 4, 2, 32])
                            sinb = c_t[:, 1, :].rearrange("p (a e) -> p a e", a=2).unsqueeze(1).broadcast_to([128, 4, 2, 32])
                            x1v, x2v = q5[:, :, :, 0, :], q5[:, :, :, 1, :]
                            a1, a1_r = r2.next(); a2, a2_r = r2.next(); a3, a3_r = r2.next(); a4, a4_r = r2.next()
                            sy.op("dve", lambda e, a1=a1: e.tensor_tensor(out=a1[:], in0=x1v, in1=cosb, op=ALU.mult), reads=[q_r, c_r], writes=[a1_r])
                            sy.op("pool", lambda e, a2=a2: e.tensor_tensor(out=a2[:], in0=x2v, in1=sinb, op=ALU.mult), reads=[q_r, c_r], writes=[a2_r])
                            sy.op("dve", lambda e, a3=a3: e.tensor_tensor(out=a3[:], in0=x2v, in1=cosb, op=ALU.mult), reads=[q_r, c_r], writes=[a3_r])
                            sy.op("pool", lambda e, a4=a4: e.tensor_tensor(out=a4[:], in0=x1v, in1=sinb, op=ALU.mult), reads=[q_r, c_r], writes=[a4_r])
                            qb5 = qb_t[:].rearrange("p h (a f e) -> p h a f e", a=2, f=2)
                            sy.op("dve", lambda e, a1=a1, a2=a2: e.tensor_tensor(out=qb5[:, :, :, 0, :], in0=a1[:], in1=a2[:], op=ALU.subtract), reads=[a1_r, a2_r], writes=[qb_r])
                            sy.op("dve", lambda e, a3=a3, a4=a4: e.tensor_tensor(out=qb5[:, :, :, 1, :], in0=a3[:], in1=a4[:], op=ALU.add), reads=[a3_r, a4_r], writes=[qb_r])
                        else:
                            sy.op("dve", lambda e, qb_t=qb_t, q_t=q_t: e.tensor_copy(out=qb_t[:], in_=q_t[:]), reads=[q_r], writes=[qb_r])
                        pT, pT_r = self.psum.next()
                        pTb = pT[:].bitcast(BF16)
                        for hh in range(4):
                            sy.op("pe", lambda e, hh=hh, qb_t=qb_t: e.transpose(pTb[:, hh * 128:(hh + 1) * 128], qb_t[:, hh, :], self.ident_bf[:]), reads=[qb_r, self.ident_r], writes=[pT_r])
                        tq_t, tq_r = tq.next()
                        sy.op("act", lambda e, tq_t=tq_t: e.activation(out=tq_t[:].rearrange("p h d -> p (h d)"), in_=pTb[:, 0:512], func=AF.Copy), reads=[pT_r], writes=[tq_r])
                        if c6 < 4:
                            sy.dma("sp", QT[c6 * 4:(c6 + 1) * 4, :, tok:tok + 128].rearrange("h d t -> d h t"), tq_t[:], reads=[tq_r], writes=[QT_r])
                        else:
                            sy.dma("sp", KT[:, :, tok:tok + 128].rearrange("h d t -> d h t"), tq_t[:], reads=[tq_r], writes=[KT_r])
            sy.barrier()
        sy.q["pool"].ring_limit = 2
        with contextlib.ExitStack() as st:
            Lmax = max(cfg.LP, cfg.LS + cfg.PL)
            NKT = Lmax // 128
            KTs = self.sb(st, "a_KT", [128, Lmax], BF16); KTs_r = Res("a_KT")
            Va = self.sb(st, "a_Va", [128, NKT, 130], BF16); Va_r = Res("a_Va")
            ck = Ring([self.sb(st, f"a_ck{i}", [128, 128], BF16) for i in range(2)], "a_ck")
            Qc = Ring([self.sb(st, f"a_Qc{i}", [128, 4, 128], BF16) for i in range(2)], "a_Qc")
            Pt = Ring([self.sb(st, f"a_Pt{i}", [128, 512], BF16) for i in range(3)], "a_Pt")
            ob = Ring([self.sb(st, f"a_ob{i}", [128, 4, 128]) for i in range(2)], "a_ob")
            rc = Ring([self.sb(st, f"a_rc{i}", [128, 4]) for i in range(2)], "a_rc")
            ot = Ring([self.sb(st, f"a_ot{i}", [128, 4, 128]) for i in range(2)], "a_ot")
            sy.op("dve", lambda e: e.memset(Va[:], 1.0), writes=[Va_r])
            for (t0, L, g, si) in self.seqs():
                PLs = cfg.PL if g == 1 else 0
                nkt = (L + PLs) // 128
                for hk in range(4):
                    if g == 1:
                        for i in range(PLs // 128):
                            c_t, c_r = ck.next()
                            sy.dma("pool", c_t[:], cache_k[i * 128:(i + 1) * 128, hk, :], writes=[c_r])
                            pT, pT_r = self.psum.next()
                            pTb = pT[:].bitcast(BF16)
                            sy.op("pe", lambda e, c_t=c_t: e.transpose(pTb[:, 0:128], c_t[:], self.ident_bf[:]), reads=[c_r, self.ident_r], writes=[pT_r])
                            sy.op("act", lambda e, i=i: e.activation(out=KTs[:, i * 128:(i + 1) * 128], in_=pTb[:, 0:128], func=AF.Copy), reads=[pT_r], writes=[KTs_r])
                        sy.dma("pool", Va[:, 0:PLs // 128, 0:128], cache_v[:, hk, :].rearrange("(c p) d -> p c d", p=128), writes=[Va_r])
                    sy.dma("pool", KTs[:, PLs:PLs + L], KT[hk, :, t0:t0 + L], reads=[KT_r], writes=[KTs_r])
                    sy.dma("pool", Va[:, PLs // 128:nkt, 0:128], Vs[t0:t0 + L, hk * 128:(hk + 1) * 128].rearrange("(c p) d -> p c d", p=128), reads=[Vs_r], writes=[Va_r])
                    for qi in range(L // 128):
                        Q_t, Q_r = Qc.next()
                        sy.dma("pool", Q_t[:], QT[hk * 4:(hk + 1) * 4, :, t0 + qi * 128:t0 + (qi + 1) * 128].rearrange("h d t -> d h t"), reads=[QT_r], writes=[Q_r])
                        accs = [(self.psa[i], self.psa_r[i]) for i in range(4)]
                        for kt in range(nkt):
                            pS, pS_r = self.psum.next()
                            sy.op("pe", lambda e, kt=kt, Q_t=Q_t: e.matmul(pS[:], lhsT=KTs[:, kt * 128:(kt + 1) * 128], rhs=Q_t[:].rearrange("p h d -> p (h d)"), start=True, stop=True),
                                  reads=[KTs_r, Q_r], writes=[pS_r])
                            P_t, P_r = Pt.next()
                            sy.op("act", lambda e, P_t=P_t: e.activation(out=P_t[:], in_=pS[:], func=AF.Exp, scale=128 ** -0.5), reads=[pS_r], writes=[P_r])
                            for hq in range(4):
                                acc, acc_r = accs[hq]
                                sy.op("pe", lambda e, hq=hq, kt=kt, P_t=P_t, acc=acc: e.matmul(acc[:, 0:129], lhsT=P_t[:, hq * 128:(hq + 1) * 128], rhs=Va[:, kt, 0:129],
                                                                                           start=(kt == 0), stop=(kt == nkt - 1)), reads=[P_r, Va_r], writes=[acc_r])
                        o_t, o_r = ob.next(); r_t, r_r = rc.next()
                        for hq in range(4):
                            acc, acc_r = accs[hq]
                            c0 = 0
                            sy.op("dve", lambda e, hq=hq, acc=acc, c0=c0, r_t=r_t: e.reciprocal(out=r_t[:, hq:hq + 1], in_=acc[:, c0 + 128:c0 + 129]), reads=[acc_r, r_r], writes=[r_r])
                            sy.op("dve", lambda e, hq=hq, acc=acc, c0=c0, r_t=r_t, o_t=o_t: e.tensor_scalar(out=o_t[:, hq, :], in0=acc[:, c0:c0 + 128], scalar1=r_t[:, hq:hq + 1], scalar2=None, op0=ALU.mult),
                                  reads=[acc_r, r_r], writes=[o_r])
                        pT, pT_r = self.psum.next()
                        for hq in range(4):
                            sy.op("pe", lambda e, hq=hq, o_t=o_t: e.transpose(pT[:, hq * 128:(hq + 1) * 128], o_t[:, hq, :], self.ident_f[:]), reads=[o_r, self.ident_f_r], writes=[pT_r])
                        ot_t, ot_r = ot.next()
                        sy.op("act", lambda e, ot_t=ot_t: e.activation(out=ot_t[:].rearrange("p h d -> p (h d)"), in_=pT[:], func=AF.Copy), reads=[pT_r], writes=[ot_r])
                        sy.dma("sp", oT[hk * 512:(hk + 1) * 512, t0 + qi * 128:t0 + (qi + 1) * 128].rearrange("(h d) t -> d h t", d=128), ot_t[:], reads=[ot_r], writes=[self.oT_r])
            sy.barrier()

    def setup_dense_bufs(self, st):
        self.xring = Ring([self.sb(st, f"xt{i}", [128, D]) for i in range(2)], "xt")
        self.xbring = Ring([self.sb(st, f"xb{i}", [128, D], BF16) for i in range(2)], "xb")
        self.ssring = Ring([self.sb(st, f"ss{i}", [128, 4]) for i in range(4)], "ss")
        self.wring = Ring([self.sb(st, f"wt{i}", [128, KC, 512], BF16) for i in range(4)], "wt")


def rope_tables(LS):
    t = np.arange(LS)
    row = (t // 64).astype(np.float32); col = (t % 64).astype(np.float32)
    freqs = np.power(np.float32(10000.0), -np.arange(0, 64, 2, dtype=np.float32) / np.float32(64)).astype(np.float32)
    ang = np.concatenate([row[:, None] * freqs[None], col[:, None] * freqs[None]], axis=1).astype(np.float32)
    return np.ascontiguousarray(np.stack([np.cos(ang), np.sin(ang)], axis=1).astype(np.float32))


def build_inputs(cfg, core, I):
    NP, LP = cfg.NP, cfg.LP
    nb = I["x_sample"].shape[0]
    b = core % nb
    xp = I["x_prompt"][core * NP:(core + 1) * NP].reshape(NP * LP, D)
    xs = I["x_sample"][b]
    m = {}
    m["x_all"] = np.ascontiguousarray(np.concatenate([xp, xs], axis=0))
    ct = np.stack([I["c_ctx"].reshape(KC, 128).T, I["c"][b].reshape(KC, 128).T], axis=-1)
    m["condT"] = np.ascontiguousarray(ct)
    m["ada_w"] = I["ada_w"]
    m["ada_bT"] = np.ascontiguousarray(I["ada_b"].reshape(2, 96, 128).transpose(0, 2, 1))
    m["norm_gT"] = np.ascontiguousarray(I["norm_g"].reshape(2, 4, KC, 128).transpose(0, 3, 1, 2))
    m["ab_w_in"] = I["ab_w_in"][0]
    m["ret_ld_bc"] = np.ascontiguousarray(np.broadcast_to(I["ret_log_decay"][0].reshape(1, 16), (128, 16)))
    m["convw"] = np.ascontiguousarray(I["rwkv_conv_w"][0].T.reshape(27, 128, 3).transpose(1, 0, 2))
    f8 = lambda v: np.ascontiguousarray(np.asarray(v).reshape(8, 128).T)
    m["w0T"] = np.ascontiguousarray(np.stack([f8(I["rwkv_w0"][0, d]) for d in range(2)], axis=1))
    m["a0T"] = np.ascontiguousarray(np.stack([f8(I["rwkv_a0"][0, d]) for d in range(2)], axis=1))
    m["k_kT"] = f8(I["rwkv_k_k"][0]); m["k_aT"] = f8(I["rwkv_k_a"][0]); m["r_kT"] = f8(I["rwkv_r_k"][0])
    pairbc = lambda v: np.ascontiguousarray(np.broadcast_to(np.asarray(v).reshape(8, 2, 1, 64), (8, 2, 64, 64)).transpose(1, 2, 0, 3).reshape(128, 8, 64))
    m["lnw_bc"] = pairbc(I["rwkv_ln_w"][0]); m["lnb_bc"] = pairbc(I["rwkv_ln_b"][0])
    m["w_up"] = np.ascontiguousarray(I["rwkv_w_up"][0].reshape(128, 1024)); m["a_up"] = np.ascontiguousarray(I["rwkv_a_up"][0].reshape(128, 1024))
    m["g_up"] = I["rwkv_g_up"][0]
    m["st_rwkv"] = np.ascontiguousarray(np.stack([I["state_rwkv_fwd"][b, 0], I["state_rwkv_bwd"][b, 0]]))
    m["ab_w_out"] = I["ab_w_out"][0]; m["c_w_in"] = I["c_w_in"][0]; m["c_w_out"] = I["c_w_out"][0]
    m["ffn_w_gate"] = I["ffn_w_gate"]; m["ffn_w_up"] = I["ffn_w_up"]; m["ffn_w_down"] = I["ffn_w_down"]
    m["qkn_bc"] = np.ascontiguousarray(np.broadcast_to(np.stack([I["c_q_norm"][0], I["c_k_norm"][0]])[None], (128, 2, 128)))
    m["rope"] = rope_tables(cfg.LS)
    m["cache_k"] = I["cache_k"][b, 0]; m["cache_v"] = I["cache_v"][b, 0]
    m["st_ret"] = np.ascontiguousarray(np.stack([I["state_ret_fwd"][b, 0], I["state_ret_bwd"][b, 0]]))
    return m


def run(I, cfg, ncores=8):
    b = B(cfg)
    nc = b.build()
    in_maps = []
    for core in range(ncores):
        m = build_inputs(cfg, core, I)
        in_maps.append({k: v for k, v in m.items() if k in b.ins})
    res = run_bass_kernel_spmd(nc, in_maps, core_ids=list(range(ncores)))
    return res.results


def kernel(**I):
    cfg = Cfg()
    I = {k: np.asarray(v) for k, v in I.items()}
    r = run(I, cfg)
    NP, LP, TP = cfg.NP, cfg.LP, cfg.TP
    nb = I["x_sample"].shape[0]
    y_prompt = np.concatenate([r[c]["y_all"][:TP].reshape(NP, LP, D) for c in range(8)], 0)
    y_sample = np.stack([r[c]["y_all"][TP:] for c in range(nb)], 0)
    cat = lambda name, d: np.concatenate([r[c][name][d] for c in range(8)], 0)[:, None]
    new_k = np.concatenate([r[c]["out_k"].reshape(NP, LP, 4, 128) for c in range(8)], 0)[:, None]
    new_v = np.concatenate([r[c]["out_v"].reshape(NP, LP, 4, 128) for c in range(8)], 0)[:, None]
    outs = (y_prompt, y_sample, cat("out_ret", 0), cat("out_ret", 1), cat("out_rwkv", 0), cat("out_rwkv", 1), new_k, new_v)
    return tuple(np.ascontiguousarray(o, dtype=np.float32) for o in outs)
```

```python
import contextlib
import numpy as np
import concourse.bass as bass
import concourse.mybir as mybir
from concourse.bass_utils import run_bass_kernel_spmd

F32 = mybir.dt.float32
BF16 = mybir.dt.bfloat16
AF = mybir.ActivationFunctionType
ALU = mybir.AluOpType
AX = mybir.AxisListType

D = 2048
KC = 16
FFN = 5632
FC = 44
RMS_EPS = 1e-6
A_IN = 3072
B_IN = 3456
AB_IN = 6528


class Cfg:
    def __init__(self, NP=4, LP=256, LS=4096, PL=512, stages=99, debug=False):
        self.NP, self.LP, self.LS, self.PL = NP, LP, LS, PL
        self.TP = NP * LP
        self.TT = self.TP + LS
        self.stages = stages
        self.debug = debug
        self.zf32 = True


class Res:
    __slots__ = ("name", "w", "r", "multi", "ws")

    def __init__(self, name, multi=False):
        self.name = name
        self.w = None
        self.r = []
        self.multi = multi
        self.ws = {}


class Q:
    def __init__(self, name, eng, sem, ring):
        self.name, self.eng, self.sem, self.ring = name, eng, sem, ring
        self.cnt = 0
        self.seen = {}
        self.ring_tot = [0] * len(ring)
        self.ring_i = 0
        self.ring_limit = min(2, len(ring)) if name == "pool" else len(ring)


class Sy:
    def __init__(self, nc, es):
        self.nc = nc
        self.q = {}
        for name, eng, nring in (("pe", nc.tensor, 0), ("dve", nc.vector, 0), ("act", nc.scalar, 6),
                                 ("pool", nc.gpsimd, 6), ("sp", nc.sync, 12)):
            sem = es.enter_context(nc.semaphore("c_" + name))
            ring = [es.enter_context(nc.semaphore(f"d_{name}{i}")) for i in range(nring)]
            self.q[name] = Q(name, eng, sem, ring)
        self.semid = {}

    def _sid(self, sem):
        return id(sem)

    def _wait(self, q, evs, same_ok):
        need = {}
        for ev in evs:
            if ev is None:
                continue
            sem, val = ev
            if sem is q.sem and same_ok:
                continue
            k = id(sem)
            if k not in need or need[k][1] < val:
                need[k] = (sem, val)
        for k, (sem, val) in need.items():
            if q.seen.get(k, 0) < val:
                q.eng.wait_ge(sem, val)
                q.seen[k] = val

    def _deps(self, reads, writes):
        evs = []
        for r in reads:
            evs.append(r.w)
            if r.multi:
                evs.extend(r.ws.values())
        for w in writes:
            if not w.multi:
                evs.append(w.w)
            evs.extend(w.r)
        return evs

    def _commit(self, ev, reads, writes):
        for r in reads:
            r.r.append(ev)
            if len(r.r) > 24:
                r.r = r.r[-24:]
        for w in writes:
            if w.multi:
                w.ws[id(ev[0])] = ev
            else:
                w.w = ev
            w.r = []

    def op(self, qn, fn, reads=(), writes=()):
        q = self.q[qn]
        self._wait(q, self._deps(reads, writes), same_ok=(qn == "pe"))
        inst = fn(q.eng)
        q.cnt += 1
        inst.then_inc(q.sem, 1)
        ev = (q.sem, q.cnt)
        self._commit(ev, reads, writes)
        return ev

    def dma(self, qn, out, in_, reads=(), writes=(), **kw):
        q = self.q[qn]
        i = q.ring_i
        i = i % q.ring_limit
        q.ring_i = (i + 1) % q.ring_limit
        sem = q.ring[i]
        evs = self._deps(reads, writes)
        if q.ring_tot[i]:
            evs.append((sem, q.ring_tot[i]))
        self._wait(q, evs, same_ok=False)
        q.eng.dma_start(out=out, in_=in_, **kw).then_inc(sem, 16)
        q.ring_tot[i] += 16
        ev = (sem, q.ring_tot[i])
        self._commit(ev, reads, writes)
        return ev

    def barrier(self):
        evs = []
        for qq in self.q.values():
            for i, sem in enumerate(qq.ring):
                if qq.ring_tot[i]:
                    evs.append((sem, qq.ring_tot[i]))
            if qq.cnt:
                evs.append((qq.sem, qq.cnt))
        for q in self.q.values():
            self._wait(q, evs, same_ok=True)

    def finish(self):
        q = self.q["sp"]
        evs = []
        for qq in self.q.values():
            for i, sem in enumerate(qq.ring):
                if qq.ring_tot[i]:
                    evs.append((sem, qq.ring_tot[i]))
            if qq.cnt and qq is not q:
                evs.append((qq.sem, qq.cnt))
        self._wait(q, evs, same_ok=True)


class Ring:
    def __init__(self, tiles, name):
        self.t = tiles
        self.r = [Res(f"{name}{i}") for i in range(len(tiles))]
        self.i = 0

    def next(self):
        i = self.i
        self.i = (i + 1) % len(self.t)
        return self.t[i], self.r[i]


class B:
    def __init__(self, cfg):
        self.cfg = cfg
        self.nc = bass.Bass("TRN2", target_bir_lowering=False)
        self.es = contextlib.ExitStack()
        self.sy = Sy(self.nc, self.es)
        self.dbg_outs = []
        self.ins = {}
        self.outs = {}

    def inp(self, name, shape):
        t = self.nc.dram_tensor(name, list(shape), F32, kind="ExternalInput").ap()
        self.ins[name] = t
        return t

    def outp(self, name, shape):
        t = self.nc.dram_tensor(name, list(shape), F32, kind="ExternalOutput").ap()
        self.outs[name] = t
        return t

    def scr(self, name, shape, dt=F32):
        if self.cfg.debug and dt == F32:
            t = self.nc.dram_tensor(name, list(shape), dt, kind="ExternalOutput").ap()
            self.dbg_outs.append(name)
        else:
            t = self.nc.dram_tensor(name, list(shape), dt, kind="Internal").ap()
        return t

    def sb(self, st, name, shape, dt=F32):
        self._uid = getattr(self, "_uid", 0) + 1
        return st.enter_context(self.nc.sbuf_tensor(f"{name}_u{self._uid}", list(shape), dt))

    def ps(self, st, name, shape, dt=F32):
        return st.enter_context(self.nc.psum_tensor(name, list(shape), dt))

    def build(self):
        cfg = self.cfg
        nc, sy = self.nc, self.sy
        TT, TP, LS = cfg.TT, cfg.TP, cfg.LS
        x_all = self.inp("x_all", [TT, D])
        condT = self.inp("condT", [128, KC, 2])
        ada_w = self.inp("ada_w", [2, D, 6 * D])
        ada_bT = self.inp("ada_bT", [2, 128, 96])
        norm_gT = self.inp("norm_gT", [2, 128, 4, KC])
        ab_w_in = self.inp("ab_w_in", [D, AB_IN])
        ret_ld_bc = self.inp("ret_ld_bc", [128, 16])
        st_ret = self.inp("st_ret", [2, 8, 64, 128])
        out_ret = self.outp("out_ret", [2, cfg.NP, 8, 64, 128])
        oT = self.scr("oT", [D, TT])
        P_ = {}
        for n_, sh in (("convw", [128, 27, 3]), ("w0T", [128, 2, 8]), ("a0T", [128, 2, 8]), ("k_kT", [128, 8]), ("k_aT", [128, 8]), ("r_kT", [128, 8]),
                       ("lnw_bc", [128, 8, 64]), ("lnb_bc", [128, 8, 64]), ("w_up", [128, 1024]), ("a_up", [128, 1024]), ("g_up", [128, 1024])):
            P_[n_] = self.inp(n_, sh)
        st_rwkv = self.inp("st_rwkv", [2, 16, 64, 64])
        out_rwkv = self.outp("out_rwkv", [2, cfg.NP, 16, 64, 64])
        if cfg.stages < 4:
            _real_inp = self.inp
            self.inp = lambda name, shape: None
        ab_w_out = self.inp("ab_w_out", [D, D]); c_w_in = self.inp("c_w_in", [D, 3072]); c_w_out = self.inp("c_w_out", [D, D])
        ffn_g = self.inp("ffn_w_gate", [2, D, FFN]); ffn_u = self.inp("ffn_w_up", [2, D, FFN]); ffn_d = self.inp("ffn_w_down", [2, FFN, D])
        qkn_bc = self.inp("qkn_bc", [128, 2, 128]); rope = self.inp("rope", [LS, 2, 64])
        cache_k = self.inp("cache_k", [cfg.PL, 4, 128]); cache_v = self.inp("cache_v", [cfg.PL, 4, 128])
        if cfg.stages < 4:
            self.inp = _real_inp
            ffn_g = ffn_u = ffn_d = [None, None]
        out_k = self.outp("out_k", [cfg.NP, cfg.LP, 512]); out_v = self.outp("out_v", [cfg.NP, cfg.LP, 512])
        y_all = self.outp("y_all", [TT, D])
        xa = self.scr("xa", [TT, D]); xb = self.scr("xb", [TT, D]); yscr = self.scr("yscr", [512, D])
        QT = self.scr("QT", [16, 128, TT]); KT = self.scr("KT", [4, 128, TT]); Vs = self.scr("Vs", [TT, 512])
        kvg = self.scr("kvg", [TT, 2560])
        qkT = self.scr("qkT", [1024, TT])
        pbT = self.scr("pbT", [B_IN, TT])

        with contextlib.ExitStack() as st0:
            self.modT = self.sb(st0, "modT", [128, 2, 2, 6, KC])
            self.modT_r = Res("modT")
            self.G = self.sb(st0, "Gcols", [128, 2, 2, 4, KC])
            self.G_r = Res("G")
            self.gate = self.sb(st0, "gatecols", [128, 2, 2, 2, KC])
            self.gate_r = Res("gate")
            self.make_consts(st0)
            self.psum = Ring([self.ps(st0, f"ps{i}", [128, 512]) for i in range(3)], "ps")
            self.psa = [self.ps(st0, f"psa{i}", [128, 512]) for i in range(4)]
            self.psa_r = [Res(f"psa{i}") for i in range(4)]
            self.psum_small = self.psum
            big = Ring.__new__(Ring)
            big.t = list(self.psum.t) + list(self.psa); big.r = list(self.psum.r) + list(self.psa_r); big.i = 0
            self.psum_big = big
            self.psum = big
            self.psx = self.ps(st0, "psx", [128, 512])
            self.psx_r = Res("psx")
            self.phase0(condT, ada_w, ada_bT, norm_gT)
            if cfg.debug:
                dbg = self.outp("dbg_G", [128, 2 * 2 * 4 * KC])
                sy.dma("sp", dbg[:, :], self.G[:].rearrange("p a b c d -> p (a b c d)"), reads=[self.G_r])
                dbg2 = self.outp("dbg_gate", [128, 2 * 2 * 2 * KC])
                sy.dma("sp", dbg2[:, :], self.gate[:].rearrange("p a b c d -> p (a b c d)"), reads=[self.gate_r])
            if cfg.stages in (0.51, 0.52, 0.53):
                with contextlib.ExitStack() as stx:
                    tmp = self.sb(stx, "tmpx", [128, 128]); tmp_r = Res("tmpx")
                    if cfg.stages == 0.51:
                        sy.op("dve", lambda e: e.tensor_scalar(out=tmp[:], in0=self.ident_f[:], scalar1=2.0, scalar2=None, op0=ALU.mult), reads=[self.ident_f_r], writes=[tmp_r])
                    elif cfg.stages == 0.52:
                        sy.op("dve", lambda e: e.tensor_scalar(out=tmp[:], in0=self.ident_f[:], scalar1=self.G[:, 0, 1, 0, 3:4], scalar2=None, op0=ALU.mult), reads=[self.ident_f_r, self.G_r], writes=[tmp_r])
                    else:
                        sy.op("dve", lambda e: e.tensor_scalar(out=tmp[:], in0=self.ident_f[:], scalar1=self.gate[:, 0, 1, 0, 3:4], scalar2=None, op0=ALU.mult), reads=[self.ident_f_r, self.gate_r], writes=[tmp_r])
                    dbg3 = self.outp("dbg_tmp", [128, 128])
                    sy.dma("sp", dbg3[:, :], tmp[:], reads=[tmp_r])
                    sy.barrier()
            if cfg.stages in (0.5, 0.54, 0.55):
                with contextlib.ExitStack() as stx:
                    gbc = self.sb(stx, "gbcx", [128, D]); gbc_r = Res("gbcx")
                    tmp = self.sb(stx, "tmpx", [128, 128]); tmp_r = Res("tmpx")
                    ones = self.sb(stx, "onesx", [128, 128]); ones_r = Res("onesx")
                    sy.op("dve", lambda e: e.memset(ones[:], 1.0), writes=[ones_r])
                    self.gate_bc_build(gbc, gbc_r, 0, 1, 0, tmp, tmp_r, ones, ones_r)
                    dbg3 = self.outp("dbg_gbc", [128, D])
                    sy.dma("sp", dbg3[:, :], gbc[:], reads=[gbc_r])
                    sy.barrier()
            if cfg.stages >= 1:
                self.layer0_inproj(x_all, ab_w_in, kvg, qkT, pbT)
            if cfg.stages >= 2:
                self.retention(kvg, qkT, oT, ret_ld_bc, st_ret, out_ret)
            if cfg.stages >= 3:
                self.rwkv(pbT, oT, P_, st_rwkv, out_rwkv)
            if cfg.stages >= 4:
                self.mix_out_and_ffn(0, oT, ab_w_out, ffn_g[0], ffn_u[0], ffn_d[0], x_all, None, xa, xb, None, yscr)
                x1_r = Res("x1", multi=True); self.oT_r = Res("oT2", multi=True)
            if cfg.stages >= 5:
                self.layer1(xb, x1_r, c_w_in, qkn_bc, rope, cache_k, cache_v, QT, KT, Vs, oT, out_k, out_v)
            if cfg.stages >= 6:
                self.mix_out_and_ffn(1, oT, c_w_out, ffn_g[1], ffn_u[1], ffn_d[1], xb, x1_r, xa, None, y_all, yscr)
        sy.finish()
        self.es.close()
        return nc

    def make_consts(self, st):
        nc, sy = self.nc, self.sy
        idf = self.sb(st, "ident_f", [128, 128])
        self.ident_f, self.ident_f_r = idf, Res("ident_f")
        self.ident_bf = self.sb(st, "ident_bf", [128, 128], BF16)
        self.ident_r = Res("ident_bf")
        sy.op("pool", lambda e: e.memset(idf[:], 1.0), writes=[self.ident_f_r])
        sy.op("pool", lambda e: e.affine_select(out=idf[:], in_=idf[:], pattern=[[1, 128]], compare_op=ALU.is_equal,
                                                fill=0.0, base=0, channel_multiplier=-1),
              reads=[self.ident_f_r], writes=[self.ident_f_r])
        sy.op("dve", lambda e: e.tensor_copy(out=self.ident_bf[:], in_=idf[:]), reads=[self.ident_f_r], writes=[self.ident_r])

    def phase0(self, condT, ada_w, ada_bT, norm_gT):
        nc, sy = self.nc, self.sy
        with contextlib.ExitStack() as st:
            cT = self.sb(st, "cT", [128, KC, 2])
            cT_r = Res("cT")
            sT = self.sb(st, "sT", [128, KC, 2])
            sT_r = Res("sT")
            bT = self.sb(st, "bT", [128, 2, 96])
            bT_r = Res("bT")
            gT = self.sb(st, "gT", [128, 2, 4, KC])
            gT_r = Res("gT")
            wr = Ring([self.sb(st, f"aw{i}", [128, KC, 512]) for i in range(2)], "aw")
            sy.dma("sp", cT[:], condT[:, :, :], writes=[cT_r])
            for l in range(2):
                sy.dma("sp", bT[:, l, :], ada_bT[l, :, :], writes=[bT_r])
                sy.dma("sp", gT[:, l, :, :], norm_gT[l, :, :, :], writes=[gT_r])
            sy.op("act", lambda e: e.activation(out=sT[:], in_=cT[:], func=AF.Silu), reads=[cT_r], writes=[sT_r])
            for l in range(2):
                for nch in range(24):
                    wt, wt_r = wr.next()
                    src = ada_w[l, :, nch * 512:(nch + 1) * 512].rearrange("(kc p) n -> p kc n", p=128)
                    sy.dma("sp" if nch % 2 == 0 else "act", wt[:], src, writes=[wt_r])
                    pt, pt_r = self.psum.next()
                    for sub in range(4):
                        for kc in range(KC):
                            sy.op("pe", lambda e, sub=sub, kc=kc: e.matmul(
                                pt[:, sub * 2:sub * 2 + 2], lhsT=wt[:, kc, sub * 128:(sub + 1) * 128],
                                rhs=sT[:, kc, :], start=(kc == 0), stop=(kc == KC - 1)),
                                reads=[wt_r, sT_r], writes=[pt_r])
                    for sub in range(4):
                        idx = nch * 4 + sub
                        vec, kq = idx // 16, idx % 16
                        for g in range(2):
                            sy.op("dve", lambda e, sub=sub, g=g, vec=vec, kq=kq, idx=idx: e.tensor_tensor(
                                out=self.modT[:, l, g, vec, kq:kq + 1], in0=pt[:, sub * 2 + g:sub * 2 + g + 1],
                                in1=bT[:, l, idx:idx + 1], op=ALU.add),
                                reads=[pt_r, bT_r], writes=[self.modT_r])
            for l in range(2):
                for g in range(2):
                    m = self.modT
                    sy.op("dve", lambda e, l=l, g=g: e.scalar_tensor_tensor(
                        out=self.G[:, l, g, 0, :], in0=m[:, l, g, 1, :], scalar=1.0, in1=gT[:, l, 0, :],
                        op0=ALU.add, op1=ALU.mult), reads=[self.modT_r, gT_r], writes=[self.G_r])
                    sy.op("dve", lambda e, l=l, g=g: e.tensor_copy(out=self.G[:, l, g, 1, :], in_=m[:, l, g, 0, :]),
                          reads=[self.modT_r], writes=[self.G_r])
                    sy.op("dve", lambda e, l=l, g=g: e.scalar_tensor_tensor(
                        out=self.G[:, l, g, 2, :], in0=m[:, l, g, 4, :], scalar=1.0, in1=gT[:, l, 2, :],
                        op0=ALU.add, op1=ALU.mult), reads=[self.modT_r, gT_r], writes=[self.G_r])
                    sy.op("dve", lambda e, l=l, g=g: e.tensor_copy(out=self.G[:, l, g, 3, :], in_=m[:, l, g, 3, :]),
                          reads=[self.modT_r], writes=[self.G_r])
                    sy.op("dve", lambda e, l=l, g=g: e.tensor_tensor(
                        out=self.gate[:, l, g, 0, :], in0=m[:, l, g, 2, :], in1=gT[:, l, 1, :], op=ALU.mult),
                        reads=[self.modT_r, gT_r], writes=[self.gate_r])
                    sy.op("dve", lambda e, l=l, g=g: e.tensor_tensor(
                        out=self.gate[:, l, g, 1, :], in0=m[:, l, g, 5, :], in1=gT[:, l, 3, :], op=ALU.mult),
                        reads=[self.modT_r, gT_r], writes=[self.gate_r])
            sy.barrier()

    def make_hT(self, st, x_rows, T, l, g, which, hT, hT_r, ident_bf, ident_r, src_r=None):
        nc, sy = self.nc, self.sy
        gi, si = (0, 1) if which == 0 else (2, 3)
        for i in range(T // 128):
            xt, xt_r = self.xring.next()
            sy.dma("sp", xt[:], x_rows[i * 128:(i + 1) * 128, :], reads=[src_r] if src_r else [], writes=[xt_r])
            xb, xb_r = self.xbring.next()
            ss, ss_r = self.ssring.next()
            sy.op("act", lambda e: e.activation(out=xb[:], in_=xt[:], func=AF.Square, accum_out=ss[:, 0:1]),
                  reads=[xt_r], writes=[xb_r, ss_r])
            sy.op("dve", lambda e: e.tensor_scalar(out=ss[:, 1:2], in0=ss[:, 0:1], scalar1=1.0 / D, scalar2=RMS_EPS,
                                                   op0=ALU.mult, op1=ALU.add), reads=[ss_r], writes=[ss_r])
            sy.op("act", lambda e: e.activation(out=ss[:, 2:3], in_=ss[:, 1:2], func=AF.Sqrt), reads=[ss_r], writes=[ss_r])
            sy.op("dve", lambda e: e.reciprocal(out=ss[:, 3:4], in_=ss[:, 2:3]), reads=[ss_r], writes=[ss_r])
            sy.op("dve", lambda e: e.tensor_scalar(out=xb[:], in0=xt[:], scalar1=ss[:, 3:4], scalar2=None, op0=ALU.mult),
                  reads=[xt_r, ss_r], writes=[xb_r])
            for half in range(2):
                pt, pt_r = self.psum.next()
                ptb = pt[:].bitcast(BF16)
                for k8 in range(8):
                    kc = half * 8 + k8
                    sy.op("pe", lambda e, kc=kc, k8=k8: e.transpose(ptb[:, k8 * 128:(k8 + 1) * 128],
                                                                    xb[:, kc * 128:(kc + 1) * 128], ident_bf[:]),
                          reads=[xb_r, ident_r], writes=[pt_r])
                for k8 in range(8):
                    kc = half * 8 + k8
                    eng = "dve" if k8 % 2 == 0 else "pool_no"
                    sy.op("dve" if k8 % 2 == 0 else "act", (lambda e, kc=kc, k8=k8: e.tensor_scalar(
                        out=hT[:, kc, i * 128:(i + 1) * 128], in0=ptb[:, k8 * 128:(k8 + 1) * 128],
                        scalar1=self.G[:, l, g, gi, kc:kc + 1], scalar2=self.G[:, l, g, si, kc:kc + 1],
                        op0=ALU.mult, op1=ALU.add)) if k8 % 2 == 0 else (lambda e, kc=kc, k8=k8: e.activation(
                        out=hT[:, kc, i * 128:(i + 1) * 128], in_=ptb[:, k8 * 128:(k8 + 1) * 128], func=AF.Identity,
                        scale=self.G[:, l, g, gi, kc:kc + 1], bias=self.G[:, l, g, si, kc:kc + 1])),
                        reads=[pt_r, self.G_r], writes=[hT_r])

    def wload(self, W, n0, ncols=512, K=KC, k0=0):
        wt, wt_r = self.wring.next()
        src = W[k0 * 128:(k0 + K) * 128, n0:n0 + ncols].rearrange("(kc p) n -> p kc n", p=128)
        self.sy.dma("pool", wt[:, 0:K, 0:ncols], src, writes=[wt_r])
        return wt, wt_r

    def blocks(self):
        cfg = self.cfg
        out = [(i, min(512, cfg.TP - i), 0) for i in range(0, cfg.TP, 512)]
        TB = min(512, cfg.LS)
        for i in range(cfg.LS // TB):
            out.append((cfg.TP + i * TB, TB, 1))
        return out

    def layer0_inproj(self, x_all, ab_w_in, kvg, qkT, pbT):
        nc, sy, cfg = self.nc, self.sy, self.cfg
        sy.q["pool"].ring_limit = 5
        self.kvg_r, self.qkT_r, self.pbT_r = Res("kvg", multi=True), Res("qkT", multi=True), Res("pbT", multi=True)
        with contextlib.ExitStack() as st:
            TBmax = 512
            hT = self.sb(st, "hT", [128, KC, TBmax], BF16)
            hT_r = Res("hT")
            self.setup_dense_bufs(st)
            ev = Ring([self.sb(st, f"ev{i}", [128, 512]) for i in range(4)], "ev")
            for (t0, T, g) in self.blocks():
                self.make_hT(st, x_all[t0:t0 + T, :], T, 0, g, 0, hT, hT_r, self.ident_bf, self.ident_r)
                for c5 in range(5):
                    wt, wt_r = self.wload(ab_w_in, 512 + c5 * 512)
                    for i in range(T // 128):
                        pt, pt_r = self.psum.next()
                        for kc in range(KC):
                            sy.op("pe", lambda e, kc=kc, i=i: e.matmul(pt[:], lhsT=hT[:, kc, i * 128:(i + 1) * 128], rhs=wt[:, kc, :],
                                                                     start=(kc == 0), stop=(kc == KC - 1)),
                                  reads=[hT_r, wt_r], writes=[pt_r])
                        et, et_r = ev.next()
                        sy.op("act" if i % 2 else "dve", (lambda e: e.activation(out=et[:], in_=pt[:], func=AF.Copy)) if i % 2 else
                              (lambda e: e.tensor_copy(out=et[:], in_=pt[:])), reads=[pt_r], writes=[et_r])
                        sy.dma("sp", kvg[t0 + i * 128:t0 + (i + 1) * 128, c5 * 512:(c5 + 1) * 512], et[:], reads=[et_r], writes=[self.kvg_r])
                fm_chunks = [(n0, qkT, n0) for n0 in range(0, 1024, 512)] + \
                            [(3072 + j * 512, pbT, j * 512) for j in range(7)]
                for (n0, dst, d0) in fm_chunks:
                    ncols = min(512, AB_IN - n0)
                    wt, wt_r = self.wload(ab_w_in, n0, ncols)
                    for sub in range(ncols // 128):
                        for tb in range(0, T, 512):
                            tw = min(512, T - tb)
                            pt, pt_r = self.psum.next()
                            for kc in range(KC):
                                sy.op("pe", lambda e, kc=kc, sub=sub, tb=tb, tw=tw: e.matmul(
                                    pt[:, 0:tw], lhsT=wt[:, kc, sub * 128:(sub + 1) * 128], rhs=hT[:, kc, tb:tb + tw],
                                    start=(kc == 0), stop=(kc == KC - 1)), reads=[hT_r, wt_r], writes=[pt_r])
                            et, et_r = ev.next()
                            sy.op("act" if sub % 2 else "dve", (lambda e, tw=tw: e.activation(out=et[:, 0:tw], in_=pt[:, 0:tw], func=AF.Copy)) if sub % 2 else
                                  (lambda e, tw=tw: e.tensor_copy(out=et[:, 0:tw], in_=pt[:, 0:tw])), reads=[pt_r], writes=[et_r])
                            sy.dma("sp", dst[d0 + sub * 128:d0 + (sub + 1) * 128, t0 + tb:t0 + tb + tw], et[:, 0:tw],
                                   reads=[et_r], writes=[self.pbT_r if dst is pbT else self.qkT_r])
            sy.barrier()
        sy.q["pool"].ring_limit = 2

    def seqs(self):
        cfg = self.cfg
        out = [(i * cfg.LP, cfg.LP, 0, i) for i in range(cfg.NP)]
        out.append((cfg.TP, cfg.LS, 1, 0))
        return out

    def retention(self, kvg, qkT, oT, ret_ld_bc, st_ret, out_ret):
        nc, sy, cfg = self.nc, self.sy, self.cfg
        I32 = mybir.dt.int32
        C = 128
        Lmax = max(cfg.LP, cfg.LS)
        NCmax = Lmax // C
        with contextlib.ExitStack() as st:
            ii = self.sb(st, "r_ii", [128, 128], I32)
            diff = self.sb(st, "r_diff", [128, 128])
            ndiff = self.sb(st, "r_ndiff", [128, 128])
            n1 = self.sb(st, "r_n1", [128, 128])
            cn = self.sb(st, "r_cn", [128, 128])
            pc = self.sb(st, "r_pc", [128, 4])
            lg = self.sb(st, "r_lg", [128, 16])
            cst_r = Res("r_const")
            maskT = self.sb(st, "r_maskT", [128, 16, 128])
            xiT = self.sb(st, "r_xiT", [128, 16, 128])
            zeta = self.sb(st, "r_zeta", [128, 16])
            gcc = self.sb(st, "r_gc", [128, 16])
            sy.op("pool", lambda e: e.iota(ii[:], pattern=[[1, 128]], base=0, channel_multiplier=-1), writes=[cst_r])
            sy.op("dve", lambda e: e.tensor_copy(out=diff[:], in_=ii[:]), reads=[cst_r], writes=[cst_r])
            sy.op("dve", lambda e: e.tensor_scalar(out=ndiff[:], in0=diff[:], scalar1=-1.0, scalar2=None, op0=ALU.mult), reads=[cst_r], writes=[cst_r])
            sy.op("pool", lambda e: e.iota(ii[:], pattern=[[1, 128]], base=1, channel_multiplier=0), reads=[cst_r], writes=[cst_r])
            sy.op("dve", lambda e: e.tensor_copy(out=n1[:], in_=ii[:]), reads=[cst_r], writes=[cst_r])
            sy.op("dve", lambda e: e.tensor_scalar(out=cn[:], in0=n1[:], scalar1=-1.0, scalar2=float(C + 1), op0=ALU.mult, op1=ALU.add), reads=[cst_r], writes=[cst_r])
            sy.op("pool", lambda e: e.iota(ii[:, 0:1], pattern=[[1, 1]], base=0, channel_multiplier=1), reads=[cst_r], writes=[cst_r])
            sy.op("dve", lambda e: e.tensor_copy(out=pc[:, 1:2], in_=ii[:, 0:1]), reads=[cst_r], writes=[cst_r])
            sy.op("dve", lambda e: e.tensor_scalar(out=pc[:, 0:1], in0=pc[:, 1:2], scalar1=-1.0, scalar2=float(C - 1), op0=ALU.mult, op1=ALU.add), reads=[cst_r], writes=[cst_r])
            sy.op("dve", lambda e: e.memset(pc[:, 2:3], float(C)), reads=[cst_r], writes=[cst_r])
            sy.dma("sp", lg[:], ret_ld_bc[:, :], reads=[cst_r], writes=[cst_r])
            sy.op("act", lambda e: e.activation(out=lg[:], in_=lg[:], func=AF.Exp), reads=[cst_r], writes=[cst_r])
            sy.op("dve", lambda e: e.tensor_scalar(out=lg[:], in0=lg[:], scalar1=-1.0, scalar2=None, op0=ALU.mult), reads=[cst_r], writes=[cst_r])
            for d in range(2):
                for h in range(8):
                    c = d * 8 + h
                    src = diff if d == 0 else ndiff
                    sy.op("act", lambda e, c=c, src=src: e.activation(out=maskT[:, c, :], in_=src[:], func=AF.Exp, scale=lg[:, c:c + 1]),
                          reads=[cst_r], writes=[cst_r])
                    sy.op("pool", lambda e, c=c, d=d: e.affine_select(out=maskT[:, c, :], in_=maskT[:, c, :], pattern=[[1 if d == 0 else -1, 128]],
                                                                    compare_op=ALU.is_ge, fill=0.0, base=0, channel_multiplier=(-1 if d == 0 else 1)),
                          reads=[cst_r], writes=[cst_r])
                    sy.op("dve", lambda e, c=c: e.tensor_scalar(out=maskT[:, c, :], in0=maskT[:, c, :], scalar1=0.125, scalar2=None, op0=ALU.mult),
                          reads=[cst_r], writes=[cst_r])
                    sy.op("act", lambda e, c=c, d=d: e.activation(out=xiT[:, c, :], in_=(n1 if d == 0 else cn)[:], func=AF.Exp, scale=lg[:, c:c + 1]),
                          reads=[cst_r], writes=[cst_r])
                    sy.op("act", lambda e, c=c, d=d: e.activation(out=zeta[:, c:c + 1], in_=pc[:, 0:1] if d == 0 else pc[:, 1:2], func=AF.Exp, scale=lg[:, c:c + 1]),
                          reads=[cst_r], writes=[cst_r])
                    sy.op("act", lambda e, c=c: e.activation(out=gcc[:, c:c + 1], in_=pc[:, 2:3], func=AF.Exp, scale=lg[:, c:c + 1]),
                          reads=[cst_r], writes=[cst_r])
            sy.op("dve", lambda e: e.tensor_scalar(out=zeta[:], in0=zeta[:], scalar1=0.125, scalar2=None, op0=ALU.mult), reads=[cst_r], writes=[cst_r])
            qT = Ring([self.sb(st, f"r_qT{i}", [64, Lmax], BF16) for i in range(2)], "r_qT")
            kT = Ring([self.sb(st, f"r_kT{i}", [64, Lmax], BF16) for i in range(2)], "r_kT")
            ktm = Ring([self.sb(st, f"r_ktm{i}", [128, NCmax, 64], BF16) for i in range(2)], "r_ktm")
            vtm = Ring([self.sb(st, f"r_vtm{i}", [128, NCmax, 128], BF16) for i in range(2)], "r_vtm")
            gtm = Ring([self.sb(st, f"r_gtm{i}", [128, NCmax, 128]) for i in range(2)], "r_gtm")
            oacc = Ring([self.sb(st, f"r_oacc{i}", [128, NCmax, 128]) for i in range(2)], "r_oacc")
            kz = Ring([self.sb(st, f"r_kz{i}", [128, NCmax, 64], BF16) for i in range(2)], "r_kz")
            qx = Ring([self.sb(st, f"r_qx{i}", [64, 128], BF16) for i in range(3)], "r_qx")
            sTs = Ring([self.sb(st, f"r_sT{i}", [128, 128], BF16) for i in range(3)], "r_sT")
            S = Ring([self.sb(st, f"r_S{i}", [64, 128]) for i in range(2)], "r_S")
            Sb = Ring([self.sb(st, f"r_Sb{i}", [64, 128], BF16) for i in range(3)], "r_Sb")
            ssq = Ring([self.sb(st, f"r_ssq{i}", [128, NCmax, 2]) for i in range(2)], "r_ssq")
            junk = Ring([self.sb(st, f"r_junk{i}", [128, 128]) for i in range(2)], "r_junk")
            oTt = Ring([self.sb(st, f"r_oTt{i}", [128, 128]) for i in range(3)], "r_oTt")
            self.oT_r = Res("oT", multi=True)
            out_r = Res("out_ret", multi=True)
            for (t0, L, g, si) in self.seqs():
                NC_ = L // C
                for h in range(8):
                    q_t, q_r = qT.next(); k_t, k_r = kT.next(); kt_t, kt_r = ktm.next(); v_t, v_r = vtm.next()
                    g_t, g_r = gtm.next(); o_t, o_r = oacc.next()
                    sy.dma("pool", q_t[:, 0:L], qkT[h * 64:(h + 1) * 64, t0:t0 + L], reads=[self.qkT_r], writes=[q_r])
                    sy.dma("pool", k_t[:, 0:L], qkT[512 + h * 64:512 + (h + 1) * 64, t0:t0 + L], reads=[self.qkT_r], writes=[k_r])
                    sy.dma("pool", kt_t[:, 0:NC_, :], kvg[t0:t0 + L, h * 64:(h + 1) * 64].rearrange("(c p) d -> p c d", p=128),
                           reads=[self.kvg_r], writes=[kt_r])
                    sy.dma("pool", v_t[:, 0:NC_, :], kvg[t0:t0 + L, 512 + h * 128:512 + (h + 1) * 128].rearrange("(c p) d -> p c d", p=128),
                           reads=[self.kvg_r], writes=[v_r])
                    sy.dma("sp", g_t[:, 0:NC_, :], kvg[t0:t0 + L, 1536 + h * 128:1536 + (h + 1) * 128].rearrange("(c p) d -> p c d", p=128),
                           reads=[self.kvg_r], writes=[g_r])
                    for d in range(2):
                        c = d * 8 + h
                        kz_t, kz_r = kz.next()
                        sy.op("dve", lambda e, c=c: e.tensor_scalar(out=kz_t[:, 0:NC_, :], in0=kt_t[:, 0:NC_, :], scalar1=zeta[:, c:c + 1], scalar2=None, op0=ALU.mult),
                              reads=[kt_r, cst_r], writes=[kz_r])
                        S_t, S_r = S.next()
                        if g == 0:
                            sy.op("dve", lambda e: e.memset(S_t[:], 0.0), writes=[S_r])
                        else:
                            sy.dma("sp", S_t[:], st_ret[d, h, :, :], writes=[S_r])
                        Sb_t, Sb_r = Sb.next()
                        sy.op("act", lambda e: e.activation(out=Sb_t[:], in_=S_t[:], func=AF.Copy), reads=[S_r], writes=[Sb_r])
                        for ci in (range(NC_) if d == 0 else range(NC_ - 1, -1, -1)):
                            cs = slice(ci * C, (ci + 1) * C)
                            pt, pt_r = self.psum.next()
                            sy.op("pe", lambda e, cs=cs: e.matmul(pt[:, 0:128], lhsT=k_t[:, cs], rhs=q_t[:, cs], start=True, stop=True),
                                  reads=[k_r, q_r], writes=[pt_r])
                            sT_t, sT_r = sTs.next()
                            sy.op("dve", lambda e, c=c: e.tensor_tensor(out=sT_t[:], in0=pt[:, 0:128], in1=maskT[:, c, :], op=ALU.mult),
                                  reads=[pt_r, cst_r], writes=[sT_r])
                            qx_t, qx_r = qx.next()
                            sy.op("pool", lambda e, cs=cs, c=c: e.tensor_tensor(out=qx_t[:], in0=q_t[:, cs], in1=xiT[0:64, c, :], op=ALU.mult),
                                  reads=[q_r, cst_r], writes=[qx_r])
                            po, po_r = self.psum.next()
                            sy.op("pe", lambda e, ci=ci: e.matmul(po[:, 0:128], lhsT=sT_t[:], rhs=v_t[:, ci, :], start=True, stop=False),
                                  reads=[sT_r, v_r], writes=[po_r])
                            sy.op("pe", lambda e: e.matmul(po[:, 0:128], lhsT=qx_t[:], rhs=Sb_t[:], start=False, stop=True),
                                  reads=[qx_r, Sb_r], writes=[po_r])
                            if d == 0:
                                sy.op("act", lambda e, ci=ci: e.activation(out=o_t[:, ci, :], in_=po[:, 0:128], func=AF.Copy), reads=[po_r], writes=[o_r])
                            else:
                                sy.op("dve", lambda e, ci=ci: e.tensor_tensor(out=o_t[:, ci, :], in0=po[:, 0:128], in1=o_t[:, ci, :], op=ALU.add),
                                      reads=[po_r, o_r], writes=[o_r])
                            pS, pS_r = self.psum.next()
                            sy.op("pe", lambda e, ci=ci: e.matmul(pS[0:64, 0:128], lhsT=kz_t[:, ci, :], rhs=v_t[:, ci, :], start=True, stop=True),
                                  reads=[kz_r, v_r], writes=[pS_r])
                            sy.op("dve", lambda e, c=c: e.scalar_tensor_tensor(out=S_t[:], in0=S_t[:], scalar=gcc[0:64, c:c + 1], in1=pS[0:64, 0:128],
                                                                              op0=ALU.mult, op1=ALU.add), reads=[pS_r, S_r, cst_r], writes=[S_r])
                            Sb_t, Sb_r = Sb.next()
                            sy.op("act", lambda e, Sb_t=Sb_t: e.activation(out=Sb_t[:], in_=S_t[:], func=AF.Copy), reads=[S_r], writes=[Sb_r])
                        if g == 0:
                            sy.dma("sp", out_ret[d, si, h, :, :], S_t[:], reads=[S_r], writes=[out_r])
                    sq_t, sq_r = ssq.next()
                    for ci in range(NC_):
                        j_t, j_r = junk.next()
                        sy.op("act", lambda e, ci=ci, j_t=j_t: e.activation(out=j_t[:], in_=o_t[:, ci, :], func=AF.Square, accum_out=sq_t[:, ci, 0:1]),
                              reads=[o_r], writes=[j_r, sq_r])
                    sy.op("dve", lambda e: e.tensor_scalar(out=sq_t[:, 0:NC_, 1:2], in0=sq_t[:, 0:NC_, 0:1], scalar1=1.0 / 128, scalar2=RMS_EPS, op0=ALU.mult, op1=ALU.add),
                          reads=[sq_r], writes=[sq_r])
                    sy.op("act", lambda e: e.activation(out=sq_t[:, 0:NC_, 1:2], in_=sq_t[:, 0:NC_, 1:2], func=AF.Sqrt), reads=[sq_r], writes=[sq_r])
                    sy.op("dve", lambda e: e.reciprocal(out=sq_t[:, 0:NC_, 1:2], in_=sq_t[:, 0:NC_, 1:2]), reads=[sq_r], writes=[sq_r])
                    sy.op("act", lambda e: e.activation(out=g_t[:, 0:NC_, :], in_=g_t[:, 0:NC_, :], func=AF.Silu), reads=[g_r], writes=[g_r])
                    for ci in range(NC_):
                        sy.op("dve", lambda e, ci=ci: e.scalar_tensor_tensor(out=o_t[:, ci, :], in0=o_t[:, ci, :], scalar=sq_t[:, ci, 1:2], in1=g_t[:, ci, :],
                                                                          op0=ALU.mult, op1=ALU.mult), reads=[o_r, sq_r, g_r], writes=[o_r])
                        pt, pt_r = self.psum.next()
                        sy.op("pe", lambda e, ci=ci: e.transpose(pt[:, 0:128], o_t[:, ci, :], self.ident_f[:]), reads=[o_r, self.ident_f_r], writes=[pt_r])
                        ot_t, ot_r = oTt.next()
                        sy.op("act", lambda e, ot_t=ot_t: e.activation(out=ot_t[:], in_=pt[:, 0:128], func=AF.Copy), reads=[pt_r], writes=[ot_r])
                        sy.dma("sp", oT[h * 128:(h + 1) * 128, t0 + ci * C:t0 + (ci + 1) * C], ot_t[:], reads=[ot_r], writes=[self.oT_r])
            sy.barrier()

    def rwkv(self, pbT, oT, P_, st_rwkv, out_rwkv):
        nc, sy, cfg = self.nc, self.sy, self.cfg
        C = 64
        LAM = 0.606531
        SEGmax = min(max(cfg.LP, cfg.LS), 512)
        NCHm = SEGmax // C
        Lmax = max(cfg.LP, cfg.LS)
        NCL = Lmax // C
        ZDT = F32 if cfg.zf32 else BF16
        with contextlib.ExitStack() as st:
            cst_r = Res("w_const")
            convw = self.sb(st, "w_convw", [128, 27, 3]); w0T = self.sb(st, "w_w0T", [128, 2, 8]); a0T = self.sb(st, "w_a0T", [128, 2, 8])
            kkT = self.sb(st, "w_kkT", [128, 8]); kaT = self.sb(st, "w_kaT", [128, 8]); rkT = self.sb(st, "w_rkT", [128, 8])
            lnw = self.sb(st, "w_lnw", [128, 8, 64]); lnb = self.sb(st, "w_lnb", [128, 8, 64])
            wup = self.sb(st, "w_wup", [128, 1024], BF16); aup = self.sb(st, "w_aup", [128, 1024], BF16); gup = self.sb(st, "w_gup", [128, 1024], BF16)
            for t_, n_ in ((convw, "convw"), (w0T, "w0T"), (a0T, "a0T"), (kkT, "k_kT"), (kaT, "k_aT"), (rkT, "r_kT"), (lnw, "lnw_bc"), (lnb, "lnb_bc")):
                sy.dma("sp", t_[:], P_[n_][tuple(slice(None) for _ in t_.shape)], writes=[cst_r])
            for t_, n_ in ((wup, "w_up"), (aup, "a_up"), (gup, "g_up")):
                sy.dma("pool", t_[:], P_[n_][:, :], writes=[cst_r])
            E = self.sb(st, "w_E", [128, 64], BF16)
            sy.op("dve", lambda e: e.tensor_tensor(out=E[:], in0=self.ident_f[:, 0:64], in1=self.ident_f[:, 64:128], op=ALU.add),
                  reads=[self.ident_f_r], writes=[cst_r])
            ones_c = self.sb(st, "w_ones", [128, 2], BF16)
            sy.op("dve", lambda e: e.memset(ones_c[:], 1.0), writes=[cst_r])
            bd1 = self.sb(st, "w_bd1", [128, 128])
            sy.op("dve", lambda e: e.memset(bd1[:], 0.0), writes=[cst_r])
            sy.op("dve", lambda e: e.memset(bd1[0:64, 0:64], 1.0), reads=[cst_r], writes=[cst_r])
            sy.op("dve", lambda e: e.memset(bd1[64:128, 64:128], 1.0), reads=[cst_r], writes=[cst_r])
            mask1 = self.sb(st, "w_mask1", [128, 2, 4, 128]); maskQ = self.sb(st, "w_maskQ", [128, 2, 128])
            sy.op("dve", lambda e: e.memset(mask1[:], 1.0), writes=[cst_r])
            sy.op("dve", lambda e: e.memset(maskQ[:], 1.0), reads=[cst_r], writes=[cst_r])
            for d in range(2):
                sg_ = 1 if d == 0 else -1
                for k4 in range(4):
                    strict = (k4 % 2 == 0)
                    sy.op("pool", lambda e, d=d, k4=k4, strict=strict, sg_=sg_: e.affine_select(
                        out=mask1[:, d, k4, :], in_=mask1[:, d, k4, :], pattern=[[sg_, 128]],
                        compare_op=(ALU.is_gt if strict else ALU.is_ge), fill=0.0, base=0, channel_multiplier=-sg_),
                        reads=[cst_r], writes=[cst_r])
                sy.op("pool", lambda e, d=d, sg_=sg_: e.affine_select(out=maskQ[:, d, :], in_=maskQ[:, d, :], pattern=[[-sg_, 128]],
                                                                    compare_op=ALU.is_gt, fill=0.0, base=0, channel_multiplier=sg_),
                      reads=[cst_r], writes=[cst_r])
            rst = self.sb(st, "w_rst", [128, SEGmax])
            sy.op("dve", lambda e: e.memset(rst[:], 1.0), writes=[cst_r])
            sy.op("dve", lambda e: e.memset(rst[:].rearrange("p (c t) -> p c t", t=C)[:, :, 0:1], 0.0), reads=[cst_r], writes=[cst_r])
            def ring(name, shape, n, dt=F32):
                return Ring([self.sb(st, f"{name}{i}", shape, dt) for i in range(n)], name)
            raw = ring("w_raw", [128, SEGmax + 2], 3)
            f_r, f_kb, f_vb, f_kk = ring("w_r", [128, SEGmax], 1), ring("w_kb", [128, SEGmax], 1), ring("w_vb", [128, SEGmax], 1), ring("w_kk", [128, SEGmax], 1)
            f_t1, f_t2, f_t3 = ring("w_t1", [128, SEGmax], 1), ring("w_t2", [128, SEGmax], 1), ring("w_t3", [128, SEGmax], 1)
            f_a, f_b, f_kd, f_sg, f_cs, f_ex = (ring("w_a", [128, SEGmax], 1), ring("w_b", [128, SEGmax], 1), ring("w_kd", [128, SEGmax], 1),
                                                ring("w_sg", [128, SEGmax], 1), ring("w_cs", [128, SEGmax], 1), ring("w_ex", [128, SEGmax], 1))
            f_e = [ring(f"w_e{i}", [128, SEGmax], 1) for i in range(4)]
            twc = ring("w_twc", [128, SEGmax], 1, BF16); acb = ring("w_acb", [128, SEGmax], 1, BF16)
            WCt = ring("w_WC", [128, NCHm], 2)
            AR = ring("w_AR", [128, NCHm, 2, 128], 1, BF16); BK = ring("w_BK", [128, NCHm, 2, 128], 1, BF16)
            BKh = ring("w_BKh", [128, NCHm, 2, 128], 1, BF16); Vbd = ring("w_Vbd", [128, NCHm, 128], 2, BF16)
            PRd = ring("w_PRd", [128, NCHm, 128], 2, BF16)
            for rg in (AR, BK, BKh, Vbd, PRd):
                for t_, r_ in zip(rg.t, rg.r):
                    sy.op("pool", lambda e, t_=t_: e.memset(t_[:], 0.0), writes=[r_])
            MM = ring("w_MM", [128, NCHm, 4, 128], 2, BF16)
            Zs = ring("w_Zs", [128, NCHm, 128], 2, BF16)
            Vt = ring("w_Vt", [128, NCHm, 64], 2, BF16); Vf = ring("w_Vf", [128, NCL, 64], 1)
            BKtm = ring("w_BKtm", [128, NCHm, 2, 128], 2, BF16)
            Pq = ring("w_Pq", [128, 2, 128], 4, ZDT); Zt = ring("w_Zt", [128, 128], 4, ZDT)
            T = ring("w_T", [128, 64], 2); Tb = ring("w_Tb", [128, 64], 3, BF16)
            RH = ring("w_RH", [128, 64], 3, BF16); Ut = ring("w_Ut", [128, 64], 3, BF16)
            yacc = ring("w_yacc", [128, NCL, 64], 1); bsum = ring("w_bsum", [128, NCL], 1)
            sgc = ring("w_sgc", [128, Lmax // C, 2, C], 1, BF16)
            gn = ring("w_gn", [128, NCL, 4], 1); ysq = ring("w_ysq", [128, NCL, 64], 1)
            ybd = ring("w_ybd", [128, 128], 3, BF16)
            orow = ring("w_orow", [128, Lmax], 1)
            tin = ring("w_tin", [64, 128], 2); tout = ring("w_tout", [64, 128], 2)
            for t_, r_ in zip(ybd.t, ybd.r):
                sy.op("pool", lambda e, t_=t_: e.memset(t_[:], 0.0), writes=[r_])
            out_r = Res("out_rwkv", multi=True)

            def conv(dst, dst_r, blk, t0, L, s0, SEG):
                rw, rw_r = raw.next()
                lo = max(s0 - 1, 0); hi = min(s0 + SEG + 1, L)
                if s0 == 0:
                    sy.op("dve", lambda e: e.memset(rw[:, 0:1], 0.0), writes=[rw_r])
                if s0 + SEG == L:
                    sy.op("dve", lambda e: e.memset(rw[:, SEG + 1:SEG + 2], 0.0), writes=[rw_r])
                sy.dma("sp", rw[:, lo - (s0 - 1):hi - (s0 - 1)], pbT[blk * 128:(blk + 1) * 128, t0 + lo:t0 + hi], reads=[self.pbT_r], writes=[rw_r])
                sy.op("act", lambda e: e.activation(out=dst[:, 0:SEG], in_=rw[:, 0:SEG], func=AF.Copy, scale=convw[:, blk, 0:1]),
                      reads=[rw_r, cst_r], writes=[dst_r])
                sy.op("dve", lambda e: e.scalar_tensor_tensor(out=dst[:, 0:SEG], in0=rw[:, 1:SEG + 1], scalar=convw[:, blk, 1:2], in1=dst[:, 0:SEG],
                                                              op0=ALU.mult, op1=ALU.add), reads=[rw_r, cst_r, dst_r], writes=[dst_r])
                sy.op("dve", lambda e: e.scalar_tensor_tensor(out=dst[:, 0:SEG], in0=rw[:, 2:SEG + 2], scalar=convw[:, blk, 2:3], in1=dst[:, 0:SEG],
                                                              op0=ALU.mult, op1=ALU.add), reads=[rw_r, cst_r, dst_r], writes=[dst_r])

            def bdw(eng, dst, dst_r, k, a, a_r, b, b_r, SEG, NCH):
                for hh in range(2):
                    ps_ = slice(hh * 64, (hh + 1) * 64)
                    o = dst[ps_, 0:NCH, k, ps_] if k is not None else dst[ps_, 0:NCH, ps_]
                    i0 = a[ps_, 0:SEG].rearrange("p (c t) -> p c t", t=C)
                    if b is None:
                        sy.op(eng, lambda e, o=o, i0=i0: e.tensor_copy(out=o, in_=i0), reads=[a_r], writes=[dst_r])
                    else:
                        i1 = b[ps_, 0:SEG].rearrange("p (c t) -> p c t", t=C)
                        sy.op(eng, lambda e, o=o, i0=i0, i1=i1: e.tensor_tensor(out=o, in0=i0, in1=i1, op=ALU.mult), reads=[a_r, b_r], writes=[dst_r])

            for (t0, L, g, si) in self.seqs():
                SEG = min(L, SEGmax); NSEG = L // SEG; NCH = SEG // C; NCS = L // C
                sg_t, sg_r = sgc.next()
                for sgi in range(NSEG):
                    tt, tt_r = f_t1.next()
                    conv(tt, tt_r, 24, t0, L, sgi * SEG, SEG)
                    for dup in range(2):
                        sy.op("act", lambda e, dup=dup, sgi=sgi: e.activation(out=sg_t[:, sgi * NCH:(sgi + 1) * NCH, dup, :],
                                                                           in_=tt[:, 0:SEG].rearrange("p (c t) -> p c t", t=C), func=AF.Sigmoid),
                              reads=[tt_r], writes=[sg_r])
                for cb in range(8):
                    ya, ya_r = yacc.next(); bs, bs_r = bsum.next(); vf, vf_r = Vf.next()
                    for d in range(2):
                        T_t, T_r = T.next()
                        if g == 0:
                            sy.op("dve", lambda e: e.memset(T_t[:], 0.0), writes=[T_r])
                        else:
                            ti, ti_r = tin.next()
                            for hh in range(2):
                                sy.dma("sp", ti[:, hh * 64:(hh + 1) * 64], st_rwkv[d, 2 * cb + hh, :, :], writes=[ti_r])
                            pt, pt_r = self.psum.next()
                            sy.op("pe", lambda e: e.transpose(pt[:, 0:64], ti[:], self.ident_f[0:64, 0:64]), reads=[ti_r, self.ident_f_r], writes=[pt_r])
                            sy.op("dve", lambda e: e.tensor_copy(out=T_t[:], in_=pt[:, 0:64]), reads=[pt_r], writes=[T_r])
                        Tb_t, Tb_r = Tb.next()
                        sy.op("act", lambda e: e.activation(out=Tb_t[:], in_=T_t[:], func=AF.Copy), reads=[T_r], writes=[Tb_r])
                        for sgi in (range(NSEG) if d == 0 else range(NSEG - 1, -1, -1)):
                            s0 = sgi * SEG
                            r_, r_r = f_r.next(); kb, kb_r = f_kb.next(); vb, vb_r = f_vb.next(); kk, kk_r = f_kk.next()
                            t1, t1_r = f_t1.next(); t2, t2_r = f_t2.next(); t3, t3_r = f_t3.next()
                            conv(r_, r_r, cb, t0, L, s0, SEG); conv(kb, kb_r, 8 + cb, t0, L, s0, SEG); conv(vb, vb_r, 16 + cb, t0, L, s0, SEG)
                            conv(t1, t1_r, 25, t0, L, s0, SEG); conv(t2, t2_r, 26, t0, L, s0, SEG)
                            tw, tw_r = twc.next(); ab_, ab_r = acb.next()
                            sy.op("act", lambda e: e.activation(out=tw[:, 0:SEG], in_=t1[:, 0:SEG], func=AF.Tanh), reads=[t1_r], writes=[tw_r])
                            sy.op("dve", lambda e: e.tensor_copy(out=ab_[:, 0:SEG], in_=t2[:, 0:SEG]), reads=[t2_r], writes=[ab_r])
                            sy.op("dve", lambda e: e.tensor_scalar(out=kk[:, 0:SEG], in0=kb[:, 0:SEG], scalar1=kkT[:, cb:cb + 1], scalar2=None, op0=ALU.mult),
                                  reads=[kb_r, cst_r], writes=[kk_r])
                            sy.op("pool", lambda e: e.tensor_tensor(out=t3[:, 0:SEG], in0=kk[:, 0:SEG], in1=kk[:, 0:SEG], op=ALU.mult), reads=[kk_r], writes=[t3_r])
                            pt, pt_r = self.psum.next()
                            sy.op("pe", lambda e: e.matmul(pt[:, 0:SEG], lhsT=bd1[:], rhs=t3[:, 0:SEG], start=True, stop=True), reads=[t3_r, cst_r], writes=[pt_r])
                            sy.op("act", lambda e: e.activation(out=t3[:, 0:SEG], in_=pt[:, 0:SEG], func=AF.Sqrt), reads=[pt_r], writes=[t3_r])
                            sy.op("dve", lambda e: e.tensor_scalar(out=t3[:, 0:SEG], in0=t3[:, 0:SEG], scalar1=1e-12, scalar2=None, op0=ALU.max), reads=[t3_r], writes=[t3_r])
                            sy.op("dve", lambda e: e.reciprocal(out=t3[:, 0:SEG], in_=t3[:, 0:SEG]), reads=[t3_r], writes=[t3_r])
                            sy.op("dve", lambda e: e.tensor_tensor(out=kk[:, 0:SEG], in0=kk[:, 0:SEG], in1=t3[:, 0:SEG], op=ALU.mult), reads=[kk_r, t3_r], writes=[kk_r])
                            a_, a_r = f_a.next(); sg, sg_r2 = f_sg.next()
                            dsl = slice(d * 64, (d + 1) * 64)
                            pt, pt_r = self.psum.next()
                            sy.op("pe", lambda e: e.matmul(pt[:, 0:SEG], lhsT=wup[dsl, cb * 128:(cb + 1) * 128], rhs=tw[dsl, 0:SEG], start=True, stop=True),
                                  reads=[tw_r, cst_r], writes=[pt_r])
                            sy.op("act", lambda e: e.activation(out=sg[:, 0:SEG], in_=pt[:, 0:SEG], func=AF.Sigmoid, bias=w0T[:, d, cb:cb + 1]), reads=[pt_r, cst_r], writes=[sg_r2])
                            pt, pt_r = self.psum.next()
                            sy.op("pe", lambda e: e.matmul(pt[:, 0:SEG], lhsT=aup[dsl, cb * 128:(cb + 1) * 128], rhs=ab_[dsl, 0:SEG], start=True, stop=True),
                                  reads=[ab_r, cst_r], writes=[pt_r])
                            sy.op("act", lambda e: e.activation(out=a_[:, 0:SEG], in_=pt[:, 0:SEG], func=AF.Sigmoid, bias=a0T[:, d, cb:cb + 1]), reads=[pt_r, cst_r], writes=[a_r])
                            b_, b_r = f_b.next(); kd, kd_r = f_kd.next()
                            sy.op("pool", lambda e: e.tensor_tensor(out=b_[:, 0:SEG], in0=kk[:, 0:SEG], in1=a_[:, 0:SEG], op=ALU.mult), reads=[kk_r, a_r], writes=[b_r])
                            sy.op("dve", lambda e: e.tensor_scalar(out=kd[:, 0:SEG], in0=a_[:, 0:SEG], scalar1=-1.0, scalar2=kaT[:, cb:cb + 1], op0=ALU.add, op1=ALU.mult),
                                  reads=[a_r, cst_r], writes=[kd_r])
                            sy.op("dve", lambda e: e.scalar_tensor_tensor(out=kd[:, 0:SEG], in0=kd[:, 0:SEG], scalar=1.0, in1=kb[:, 0:SEG], op0=ALU.add, op1=ALU.mult),
                                  reads=[kd_r, kb_r], writes=[kd_r])
                            cs, cs_r = f_cs.next(); ex, ex_r = f_ex.next()
                            sy.op("dve", lambda e: e.tensor_tensor_scan(out=cs[:, 0:SEG], data0=rst[:, 0:SEG], data1=sg[:, 0:SEG], initial=0.0, op0=ALU.mult, op1=ALU.add),
                                  reads=[sg_r2, cst_r], writes=[cs_r])
                            c3 = cs[:, 0:SEG].rearrange("p (c t) -> p c t", t=C)
                            tot = c3[:, :, C - 1:C].broadcast_to([128, NCH, C])
                            sy.op("dve", lambda e: e.tensor_tensor(out=ex[:, 0:SEG].rearrange("p (c t) -> p c t", t=C), in0=tot, in1=c3, op=ALU.subtract), reads=[cs_r], writes=[ex_r])
                            WC_t, WC_r = WCt.next()
                            sy.op("act", lambda e: e.activation(out=WC_t[:, 0:NCH], in_=c3[:, :, C - 1], func=AF.Exp, scale=-LAM), reads=[cs_r], writes=[WC_r])
                            if d == 1:
                                sy.op("dve", lambda e: e.tensor_tensor(out=t1[:, 0:SEG], in0=cs[:, 0:SEG], in1=sg[:, 0:SEG], op=ALU.subtract), reads=[cs_r, sg_r2, t1_r], writes=[t1_r])
                                sy.op("dve", lambda e: e.tensor_tensor(out=cs[:, 0:SEG], in0=ex[:, 0:SEG], in1=sg[:, 0:SEG], op=ALU.add), reads=[ex_r, sg_r2, t1_r], writes=[cs_r])
                                tmi, tmi_r = t1, t1_r
                                exc, exc_r = ex, ex_r
                            else:
                                sy.op("dve", lambda e: e.tensor_tensor(out=t1[:, 0:SEG], in0=cs[:, 0:SEG], in1=sg[:, 0:SEG], op=ALU.subtract), reads=[cs_r, sg_r2, t1_r], writes=[t1_r])
                                tmi, tmi_r = ex, ex_r
                                exc, exc_r = t1, t1_r
                            eW, eW_r = f_e[0].next(); eWx, eWx_r = f_e[1].next(); eWi, eWi_r = f_e[2].next(); eWC, eWC_r = f_e[3].next()
                            sy.op("act", lambda e: e.activation(out=eW[:, 0:SEG], in_=cs[:, 0:SEG], func=AF.Exp, scale=-LAM), reads=[cs_r], writes=[eW_r])
                            sy.op("act", lambda e: e.activation(out=eWx[:, 0:SEG], in_=exc[:, 0:SEG], func=AF.Exp, scale=-LAM), reads=[exc_r], writes=[eWx_r])
                            sy.op("act", lambda e: e.activation(out=eWi[:, 0:SEG], in_=cs[:, 0:SEG], func=AF.Exp, scale=LAM), reads=[cs_r], writes=[eWi_r])
                            sy.op("act", lambda e: e.activation(out=eWC[:, 0:SEG], in_=tmi[:, 0:SEG], func=AF.Exp, scale=-LAM), reads=[tmi_r], writes=[eWC_r])
                            AR_t, AR_r = AR.next(); BK_t, BK_r = BK.next(); BKh_t, BKh_r = BKh.next(); V_t, V_r = Vbd.next(); PR_t, PR_r = PRd.next()
                            bdw("dve", AR_t, AR_r, 0, kk, kk_r, eWx, eWx_r, SEG, NCH)
                            bdw("pool", AR_t, AR_r, 1, r_, r_r, eW, eW_r, SEG, NCH)
                            bdw("dve", BK_t, BK_r, 0, b_, b_r, eWi, eWi_r, SEG, NCH)
                            bdw("pool", BK_t, BK_r, 1, kd, kd_r, eWi, eWi_r, SEG, NCH)
                            bdw("dve", BKh_t, BKh_r, 0, b_, b_r, eWC, eWC_r, SEG, NCH)
                            bdw("pool", BKh_t, BKh_r, 1, kd, kd_r, eWC, eWC_r, SEG, NCH)
                            bdw("dve", V_t, V_r, None, vb, vb_r, None, None, SEG, NCH)
                            sy.op("dve", lambda e: e.scalar_tensor_tensor(out=t2[:, 0:SEG], in0=r_[:, 0:SEG], scalar=rkT[:, cb:cb + 1], in1=kd[:, 0:SEG], op0=ALU.mult, op1=ALU.mult),
                                  reads=[r_r, kd_r, cst_r, t2_r], writes=[t2_r])
                            bdw("pool", PR_t, PR_r, None, t2, t2_r, None, None, SEG, NCH)
                            MM_t, MM_r = MM.next(); Z_t, Z_r = Zs.next(); Vt_t, Vt_r = Vt.next(); BKtm_t, BKtm_r = BKtm.next()
                            pb_, pb_r = self.psx, self.psx_r
                            for ci in range(NCH):
                                gci = sgi * NCH + ci
                                p1, p1_r = self.psum.next()
                                arflat = AR_t[:, ci, :, :].rearrange("p a b -> p (a b)")
                                sy.op("pe", lambda e, ci=ci, arflat=arflat: e.matmul(p1[:, 0:256], lhsT=BK_t[:, ci, 0, :], rhs=arflat, start=True, stop=True), reads=[BK_r, AR_r], writes=[p1_r])
                                sy.op("pe", lambda e, ci=ci, arflat=arflat: e.matmul(p1[:, 256:512], lhsT=BK_t[:, ci, 1, :], rhs=arflat, start=True, stop=True), reads=[BK_r, AR_r], writes=[p1_r])
                                sy.op("dve", lambda e, ci=ci: e.tensor_tensor(out=MM_t[:, ci, :, :].rearrange("p a b -> p (a b)"), in0=p1[:, :],
                                                                            in1=mask1[:, d, :, :].rearrange("p a b -> p (a b)"), op=ALU.mult), reads=[p1_r, cst_r], writes=[MM_r])
                                p2, p2_r = self.psum.next()
                                sy.op("pe", lambda e, ci=ci: e.matmul(p2[:, 0:128], lhsT=AR_t[:, ci, 0, :], rhs=BK_t[:, ci, 0, :], start=True, stop=True), reads=[BK_r, AR_r], writes=[p2_r])
                                pq, pq_r = Pq.next()
                                sy.op("dve", lambda e, ci=ci, pq=pq: e.tensor_tensor(out=pq[:, 0, :], in0=p1[:, 0:128], in1=mask1[:, d, 0, :], op=ALU.mult), reads=[p1_r, cst_r], writes=[pq_r])
                                sy.op("dve", lambda e, pq=pq: e.tensor_tensor(out=pq[:, 1, :], in0=p2[:, 0:128], in1=maskQ[:, d, :], op=ALU.mult), reads=[p2_r, cst_r], writes=[pq_r])
                                z, z_r = Zt.next()
                                sy.op("pool", lambda e, z=z, pq=pq: e.tensor_tensor(out=z[:], in0=self.ident_f[:], in1=pq[:, 0, :], op=ALU.subtract), reads=[pq_r, self.ident_f_r], writes=[z_r])
                                for lev in range(5):
                                    pp, pp_r = self.psum.next()
                                    sy.op("pe", lambda e, pq=pq: e.matmul(pp[:, 0:128], lhsT=pq[:, 1, :], rhs=pq[:, 0, :], start=True, stop=True), reads=[pq_r], writes=[pp_r])
                                    sy.op("pe", lambda e, pq=pq: e.matmul(pp[:, 128:256], lhsT=pq[:, 0, :], rhs=pq[:, 1, :], start=True, stop=True), reads=[pq_r], writes=[pp_r])
                                    pq, pq_r = Pq.next()
                                    sy.op("act", lambda e, pq=pq, pp=pp: e.activation(out=pq[:].rearrange("p a b -> p (a b)"), in_=pp[:, 0:256], func=AF.Copy), reads=[pp_r], writes=[pq_r])
                                    pz, pz_r = self.psum.next()
                                    sy.op("pe", lambda e, pq=pq, z=z: e.matmul(pz[:, 0:128], lhsT=pq[:, 1, :], rhs=z[:], start=True, stop=True), reads=[pq_r, z_r], writes=[pz_r])
                                    zn, zn_r = Zt.next()
                                    if lev == 4:
                                        sy.op("dve", lambda e, z=z, pz=pz, ci=ci: e.tensor_tensor(out=Z_t[:, ci, :], in0=pz[:, 0:128], in1=z[:], op=ALU.add), reads=[pz_r, z_r], writes=[Z_r])
                                    else:
                                        sy.op("dve", lambda e, z=z, pz=pz, zn=zn: e.tensor_tensor(out=zn[:], in0=pz[:, 0:128], in1=z[:], op=ALU.add), reads=[pz_r, z_r], writes=[zn_r])
                                        z, z_r = zn, zn_r
                                pv, pv_r = self.psum.next()
                                sy.op("pe", lambda e, ci=ci: e.matmul(pv[:, 0:64], lhsT=V_t[:, ci, :], rhs=E[:], start=True, stop=True), reads=[V_r, cst_r], writes=[pv_r])
                                sy.op("act", lambda e, ci=ci: e.activation(out=Vt_t[:, ci, :], in_=pv[:, 0:64], func=AF.Copy), reads=[pv_r], writes=[Vt_r])
                                if d == 0:
                                    sy.op("dve", lambda e, gci=gci: e.tensor_copy(out=vf[:, gci, :], in_=pv[:, 0:64]), reads=[pv_r], writes=[vf_r])
                                pbk, pbk_r = self.psum.next()
                                pbkb = pbk[:].bitcast(BF16)
                                for k2 in range(2):
                                    sy.op("pe", lambda e, ci=ci, k2=k2: e.transpose(pbkb[:, k2 * 128:(k2 + 1) * 128], BKh_t[:, ci, k2, :], self.ident_bf[:]), reads=[BKh_r, self.ident_r], writes=[pbk_r])
                                sy.op("act", lambda e, ci=ci: e.activation(out=BKtm_t[:, ci, :, :].rearrange("p a b -> p (a b)"), in_=pbkb[:, 0:256], func=AF.Copy), reads=[pbk_r], writes=[BKtm_r])
                                sy.op("pe", lambda e, ci=ci: e.matmul(pb_[:, 2 * ci:2 * ci + 2], lhsT=PR_t[:, ci, :], rhs=ones_c[:], start=True, stop=True), reads=[PR_r, cst_r], writes=[pb_r])
                            pbv = pb_[:, 0:2 * NCH].rearrange("p (c two) -> p c two", two=2)[:, :, 0]
                            bsl = bs[:, sgi * NCH:(sgi + 1) * NCH]
                            if d == 0:
                                sy.op("dve", lambda e: e.tensor_copy(out=bsl, in_=pbv), reads=[pb_r], writes=[bs_r])
                            else:
                                sy.op("dve", lambda e: e.tensor_tensor(out=bsl, in0=pbv, in1=bsl, op=ALU.add), reads=[pb_r, bs_r], writes=[bs_r])
                            for ci in (range(NCH) if d == 0 else range(NCH - 1, -1, -1)):
                                gci = sgi * NCH + ci
                                pr, pr_r = self.psum.next()
                                sy.op("pe", lambda e, ci=ci, Tb_t=Tb_t: e.matmul(pr[:, 0:64], lhsT=AR_t[:, ci, 0, :], rhs=Tb_t[:], start=True, stop=False), reads=[AR_r, Tb_r], writes=[pr_r])
                                sy.op("pe", lambda e, ci=ci: e.matmul(pr[:, 0:64], lhsT=MM_t[:, ci, 2, :], rhs=Vt_t[:, ci, :], start=False, stop=True), reads=[MM_r, Vt_r], writes=[pr_r])
                                rh, rh_r = RH.next()
                                sy.op("dve", lambda e, rh=rh, pr=pr: e.tensor_scalar(out=rh[:], in0=pr[:, 0:64], scalar1=-1.0, scalar2=None, op0=ALU.mult), reads=[pr_r], writes=[rh_r])
                                pu, pu_r = self.psum.next()
                                sy.op("pe", lambda e, ci=ci, rh=rh: e.matmul(pu[:, 0:64], lhsT=Z_t[:, ci, :], rhs=rh[:], start=True, stop=True), reads=[Z_r, rh_r], writes=[pu_r])
                                u, u_r = Ut.next()
                                sy.op("act", lambda e, u=u, pu=pu: e.activation(out=u[:], in_=pu[:, 0:64], func=AF.Copy), reads=[pu_r], writes=[u_r])
                                py, py_r = self.psum.next()
                                sy.op("pe", lambda e, ci=ci, Tb_t=Tb_t: e.matmul(py[:, 0:64], lhsT=AR_t[:, ci, 1, :], rhs=Tb_t[:], start=True, stop=False), reads=[AR_r, Tb_r], writes=[py_r])
                                sy.op("pe", lambda e, ci=ci, u=u: e.matmul(py[:, 0:64], lhsT=MM_t[:, ci, 1, :], rhs=u[:], start=False, stop=False), reads=[MM_r, u_r], writes=[py_r])
                                sy.op("pe", lambda e, ci=ci: e.matmul(py[:, 0:64], lhsT=MM_t[:, ci, 3, :], rhs=Vt_t[:, ci, :], start=False, stop=True), reads=[MM_r, Vt_r], writes=[py_r])
                                if d == 0:
                                    sy.op("act", lambda e, gci=gci, py=py: e.activation(out=ya[:, gci, :], in_=py[:, 0:64], func=AF.Copy), reads=[py_r], writes=[ya_r])
                                else:
                                    sy.op("dve", lambda e, gci=gci, py=py: e.tensor_tensor(out=ya[:, gci, :], in0=py[:, 0:64], in1=ya[:, gci, :], op=ALU.add), reads=[py_r, ya_r], writes=[ya_r])
                                pT, pT_r = self.psum.next()
                                sy.op("pe", lambda e, ci=ci, u=u: e.matmul(pT[:, 0:64], lhsT=BKtm_t[:, ci, 0, :], rhs=u[:], start=True, stop=False), reads=[BKtm_r, u_r], writes=[pT_r])
                                sy.op("pe", lambda e, ci=ci: e.matmul(pT[:, 0:64], lhsT=BKtm_t[:, ci, 1, :], rhs=Vt_t[:, ci, :], start=False, stop=True), reads=[BKtm_r, Vt_r], writes=[pT_r])
                                sy.op("dve", lambda e, ci=ci, pT=pT: e.scalar_tensor_tensor(out=T_t[:], in0=T_t[:], scalar=WC_t[:, ci:ci + 1], in1=pT[:, 0:64], op0=ALU.mult, op1=ALU.add),
                                      reads=[pT_r, T_r, WC_r], writes=[T_r])
                                Tb_t, Tb_r = Tb.next()
                                sy.op("act", lambda e, Tb_t=Tb_t: e.activation(out=Tb_t[:], in_=T_t[:], func=AF.Copy), reads=[T_r], writes=[Tb_r])
                        if g == 0:
                            pt, pt_r = self.psum.next()
                            sy.op("pe", lambda e: e.transpose(pt[0:64, 0:128], T_t[:], self.ident_f[:]), reads=[T_r, self.ident_f_r], writes=[pt_r])
                            to, to_r = tout.next()
                            sy.op("dve", lambda e: e.tensor_copy(out=to[:], in_=pt[0:64, 0:128]), reads=[pt_r], writes=[to_r])
                            for hh in range(2):
                                sy.dma("sp", out_rwkv[d, si, 2 * cb + hh, :, :], to[:, hh * 64:(hh + 1) * 64], reads=[to_r], writes=[out_r])
                    gn_t, gn_r = gn.next(); yq, yq_r = ysq.next(); orw, orw_r = orow.next()
                    yv = ya[:, 0:NCS, :]
                    sy.op("dve", lambda e: e.tensor_reduce(out=gn_t[:, 0:NCS, 0], in_=yv, axis=AX.X, op=ALU.add), reads=[ya_r], writes=[gn_r])
                    sy.op("pool", lambda e: e.tensor_tensor(out=yq[:, 0:NCS, :], in0=yv, in1=yv, op=ALU.mult), reads=[ya_r], writes=[yq_r])
                    sy.op("dve", lambda e: e.tensor_reduce(out=gn_t[:, 0:NCS, 1], in_=yq[:, 0:NCS, :], axis=AX.X, op=ALU.add), reads=[yq_r, gn_r], writes=[gn_r])
                    sy.op("dve", lambda e: e.tensor_scalar(out=gn_t[:, 0:NCS, 0], in0=gn_t[:, 0:NCS, 0], scalar1=1.0 / 64, scalar2=None, op0=ALU.mult), reads=[gn_r], writes=[gn_r])
                    sy.op("dve", lambda e: e.tensor_tensor(out=gn_t[:, 0:NCS, 2], in0=gn_t[:, 0:NCS, 0], in1=gn_t[:, 0:NCS, 0], op=ALU.mult), reads=[gn_r], writes=[gn_r])
                    sy.op("dve", lambda e: e.scalar_tensor_tensor(out=gn_t[:, 0:NCS, 1], in0=gn_t[:, 0:NCS, 1], scalar=1.0 / 64, in1=gn_t[:, 0:NCS, 2], op0=ALU.mult, op1=ALU.subtract),
                          reads=[gn_r], writes=[gn_r])
                    sy.op("dve", lambda e: e.tensor_scalar(out=gn_t[:, 0:NCS, 1], in0=gn_t[:, 0:NCS, 1], scalar1=64e-5, scalar2=None, op0=ALU.add), reads=[gn_r], writes=[gn_r])
                    sy.op("act", lambda e: e.activation(out=gn_t[:, 0:NCS, 1], in_=gn_t[:, 0:NCS, 1], func=AF.Sqrt), reads=[gn_r], writes=[gn_r])
                    sy.op("dve", lambda e: e.reciprocal(out=gn_t[:, 0:NCS, 1], in_=gn_t[:, 0:NCS, 1]), reads=[gn_r], writes=[gn_r])
                    bc = lambda col: gn_t[:, 0:NCS, col:col + 1].broadcast_to([128, NCS, 64])
                    sy.op("dve", lambda e: e.tensor_tensor(out=yv, in0=yv, in1=bc(0), op=ALU.subtract), reads=[ya_r, gn_r], writes=[ya_r])
                    sy.op("dve", lambda e: e.tensor_tensor(out=yv, in0=yv, in1=bc(1), op=ALU.mult), reads=[ya_r, gn_r], writes=[ya_r])
                    sy.op("dve", lambda e: e.tensor_tensor(out=yv, in0=yv, in1=lnw[:, cb:cb + 1, :].broadcast_to([128, NCS, 64]), op=ALU.mult), reads=[ya_r, cst_r], writes=[ya_r])
                    sy.op("dve", lambda e: e.tensor_tensor(out=yv, in0=yv, in1=lnb[:, cb:cb + 1, :].broadcast_to([128, NCS, 64]), op=ALU.add), reads=[ya_r, cst_r], writes=[ya_r])
                    sy.op("pool", lambda e: e.tensor_tensor(out=yq[:, 0:NCS, :], in0=vf[:, 0:NCS, :], in1=bs[:, 0:NCS].unsqueeze(2).broadcast_to([128, NCS, 64]), op=ALU.mult),
                          reads=[vf_r, bs_r, yq_r], writes=[yq_r])
                    sy.op("dve", lambda e: e.tensor_tensor(out=yv, in0=yv, in1=yq[:, 0:NCS, :], op=ALU.add), reads=[ya_r, yq_r], writes=[ya_r])
                    for gci in range(NCS):
                        pg, pg_r = self.psum.next()
                        sy.op("pe", lambda e, gci=gci: e.matmul(pg[:, 0:128], lhsT=sg_t[:, gci, :, :].rearrange("p a b -> p (a b)"), rhs=gup[:, cb * 128:(cb + 1) * 128], start=True, stop=True),
                              reads=[sg_r, cst_r], writes=[pg_r])
                        yb, yb_r = ybd.next()
                        for hh in range(2):
                            ps_ = slice(hh * 64, (hh + 1) * 64)
                            sy.op("dve", lambda e, gci=gci, ps_=ps_, yb=yb, pg=pg: e.tensor_tensor(out=yb[ps_, ps_], in0=pg[ps_, ps_], in1=ya[ps_, gci, :], op=ALU.mult),
                                  reads=[pg_r, ya_r], writes=[yb_r])
                        po, po_r = self.psum.next()
                        sy.op("pe", lambda e, yb=yb: e.matmul(po[:, 0:64], lhsT=yb[:], rhs=E[:], start=True, stop=True), reads=[yb_r, cst_r], writes=[po_r])
                        sy.op("act", lambda e, gci=gci, po=po: e.activation(out=orw[:, gci * C:(gci + 1) * C], in_=po[:, 0:64], func=AF.Copy), reads=[po_r], writes=[orw_r])
                    sy.dma("sp", oT[1024 + cb * 128:1024 + (cb + 1) * 128, t0:t0 + L], orw[:, 0:L], reads=[orw_r], writes=[self.oT_r])
            sy.barrier()

    def gate_bc_build(self, gbc, gbc_r, l, g, which, tmp, tmp_r, ones, ones_r):
        sy = self.sy
        for kc in range(KC):
            sy.op("dve", lambda e, kc=kc: e.tensor_scalar(out=tmp[:], in0=self.ident_f[:], scalar1=self.gate[:, l, g, which, kc:kc + 1], scalar2=None, op0=ALU.mult),
                  reads=[self.ident_f_r, self.gate_r, tmp_r], writes=[tmp_r])
            if self.cfg.stages in (4.121, 0.54):
                continue
            pt, pt_r = self.psum.next()
            sy.op("pe", lambda e: e.matmul(pt[:, 0:128], lhsT=ones[:], rhs=tmp[:], start=True, stop=True), reads=[tmp_r, ones_r], writes=[pt_r])
            if self.cfg.stages in (4.122, 0.55):
                continue
            sy.op("dve", lambda e, kc=kc: e.tensor_copy(out=gbc[:, kc * 128:(kc + 1) * 128], in_=pt[:, 0:128]), reads=[pt_r], writes=[gbc_r])

    def dense_residual(self, st, lT, lT_r, KCx, W, T, t0, x_src, x_dst, x_dst_r, x_src_r, l, g, which, yscr, B_):
        sy = self.sy
        ntl = T // 128
        ssq, ssq_r = B_["ssq"].next()
        yscr_r = B_["yscr_r"]
        for nch in range(4):
            wts = []
            for k0 in range(0, KCx, 22):
                kk_ = min(22, KCx - k0)
                wt, wt_r = B_["wbig"].next()
                src = W[k0 * 128:(k0 + kk_) * 128, nch * 512:(nch + 1) * 512].rearrange("(kc p) n -> p kc n", p=128)
                sy.dma("pool", wt[:, 0:kk_, :], src, writes=[wt_r])
                wts.append((wt, wt_r, k0, kk_))
            for i in range(ntl):
                pt, pt_r = self.psum.next()
                for (wt, wt_r, k0, kk_) in wts:
                    for kc in range(kk_):
                        sy.op("pe", lambda e, kc=kc, k0=k0, wt=wt, i=i: e.matmul(pt[:], lhsT=lT[:, k0 + kc, i * 128:(i + 1) * 128], rhs=wt[:, kc, :],
                                                                             start=(k0 + kc == 0), stop=(k0 + kc == KCx - 1)), reads=[lT_r, wt_r], writes=[pt_r])
                et, et_r = B_["ev"].next()
                sy.op("dve", lambda e, et=et: e.tensor_copy(out=et[:], in_=pt[:]), reads=[pt_r], writes=[et_r])
                jk, jk_r = B_["junk"].next()
                sy.op("act", lambda e, jk=jk, i=i, nch=nch, et=et: e.activation(out=jk[:], in_=et[:], func=AF.Square, accum_out=ssq[:, i, nch:nch + 1]), reads=[et_r], writes=[jk_r, ssq_r])
                sy.dma("sp", yscr[i * 128:(i + 1) * 128, nch * 512:(nch + 1) * 512], et[:], reads=[et_r], writes=[yscr_r])
        if self.cfg.stages == 4.11:
            return
        gbc, gbc_r = B_["gbc"], B_["gbc_r"]
        self.gate_bc_build(gbc, gbc_r, l, g, which, B_["tmp"], B_["tmp_r"], B_["ones"], B_["ones_r"])
        if self.cfg.stages in (4.12, 4.121, 4.122):
            return
        for i in range(ntl):
            yt, yt_r = self.xring.next(); xt, xt_r = self.xring.next()
            sy.dma("sp", yt[:], yscr[i * 128:(i + 1) * 128, :], reads=[yscr_r], writes=[yt_r])
            sy.dma("act", xt[:], x_src[t0 + i * 128:t0 + (i + 1) * 128, :], reads=[x_src_r] if x_src_r else [], writes=[xt_r])
            ss, ss_r = self.ssring.next()
            sy.op("dve", lambda e, i=i: e.tensor_reduce(out=ss[:, 0:1], in_=ssq[:, i, :], axis=AX.X, op=ALU.add), reads=[ssq_r], writes=[ss_r])
            sy.op("dve", lambda e: e.tensor_scalar(out=ss[:, 1:2], in0=ss[:, 0:1], scalar1=1.0 / D, scalar2=RMS_EPS, op0=ALU.mult, op1=ALU.add), reads=[ss_r], writes=[ss_r])
            sy.op("act", lambda e: e.activation(out=ss[:, 2:3], in_=ss[:, 1:2], func=AF.Sqrt), reads=[ss_r], writes=[ss_r])
            sy.op("dve", lambda e: e.reciprocal(out=ss[:, 3:4], in_=ss[:, 2:3]), reads=[ss_r], writes=[ss_r])
            sy.op("dve", lambda e: e.scalar_tensor_tensor(out=yt[:], in0=yt[:], scalar=ss[:, 3:4], in1=gbc[:], op0=ALU.mult, op1=ALU.mult), reads=[yt_r, ss_r, gbc_r], writes=[yt_r])
            sy.op("pool", lambda e: e.tensor_tensor(out=xt[:], in0=xt[:], in1=yt[:], op=ALU.add), reads=[yt_r, xt_r], writes=[xt_r])
            sy.dma("sp", x_dst[t0 + i * 128:t0 + (i + 1) * 128, :], xt[:], reads=[xt_r], writes=[x_dst_r])

    def dense_bufs(self, st, yscr):
        B_ = {}
        B_["wbig"] = Ring([self.sb(st, f"wbig{i}", [128, 22, 512], BF16) for i in range(4)], "wbig")
        B_["ev"] = Ring([self.sb(st, f"dev{i}", [128, 512]) for i in range(3)], "dev")
        B_["junk"] = Ring([self.sb(st, f"djunk{i}", [128, 512], BF16) for i in range(2)], "djunk")
        B_["ssq"] = Ring([self.sb(st, f"dssq{i}", [128, 4, 4]) for i in range(2)], "dssq")
        B_["gbc"] = self.sb(st, "gbc", [128, D]); B_["gbc_r"] = Res("gbc")
        B_["tmp"] = self.sb(st, "gtmp", [128, 128]); B_["tmp_r"] = Res("gtmp")
        B_["ones"] = self.sb(st, "gones", [128, 128]); B_["ones_r"] = Res("gones")
        self.sy.op("dve", lambda e: e.memset(B_["ones"][:], 1.0), writes=[B_["ones_r"]])
        B_["yscr_r"] = Res("yscr", multi=True)
        self.xring = Ring([self.sb(st, f"xt{i}", [128, D]) for i in range(3)], "xt")
        self.xbring = Ring([self.sb(st, f"xb{i}", [128, D], BF16) for i in range(2)], "xb")
        self.ssring = Ring([self.sb(st, f"ss{i}", [128, 4]) for i in range(4)], "ss")
        return B_

    def mix_out_and_ffn(self, l, oT, W_out, Wg, Wu, Wd, x_in, x_in_r, x_mid, x_out, x_out_final, yscr):
        sy, cfg = self.sy, self.cfg
        sy.q["pool"].ring_limit = 5
        with contextlib.ExitStack() as st:
            B_ = self.dense_bufs(st, yscr)
            TBm = 512
            hT = self.sb(st, "hT2", [128, KC, TBm], BF16); hT_r = Res("hT2")
            actT = self.sb(st, "actT", [128, FC, TBm], BF16); actT_r = Res("actT")
            sil = Ring([self.sb(st, f"sil{i}", [128, 512]) for i in range(2)], "sil")
            x_mid_r = Res("x_mid", multi=True); x_out_r = Res("x_out", multi=True)
            for (t0, T, g) in self.blocks():
                sy.dma("pool", hT[:, :, 0:T], oT[:, t0:t0 + T].rearrange("(kc p) t -> p kc t", p=128), reads=[self.oT_r, hT_r], writes=[hT_r])
                if cfg.stages == 4.05:
                    break
                self.dense_residual(st, hT, hT_r, KC, W_out, T, t0, x_in, x_mid, x_mid_r, x_in_r, l, g, 0, yscr, B_)
                if cfg.stages in (4.1, 4.11, 4.12, 4.121, 4.122):
                    break
                self.make_hT(st, x_mid[t0:t0 + T, :], T, l, g, 1, hT, hT_r, self.ident_bf, self.ident_r, src_r=x_mid_r)
                for fc4 in range(11):
                    wg, wg_r = B_["wbig"].next()
                    sy.dma("pool", wg[:, 0:KC, :], Wg[:, fc4 * 512:(fc4 + 1) * 512].rearrange("(kc p) n -> p kc n", p=128), writes=[wg_r])
                    wu, wu_r = B_["wbig"].next()
                    sy.dma("pool", wu[:, 0:KC, :], Wu[:, fc4 * 512:(fc4 + 1) * 512].rearrange("(kc p) n -> p kc n", p=128), writes=[wu_r])
                    for sub in range(4):
                        fc = fc4 * 4 + sub
                        for tb in range(0, T, 512):
                            tw = min(512, T - tb)
                            pg, pg_r = self.psum.next(); pu, pu_r = self.psum.next()
                            for kc in range(KC):
                                sy.op("pe", lambda e, kc=kc, sub=sub, tb=tb, tw=tw: e.matmul(pg[:, 0:tw], lhsT=wg[:, kc, sub * 128:(sub + 1) * 128], rhs=hT[:, kc, tb:tb + tw],
                                                                                         start=(kc == 0), stop=(kc == KC - 1)), reads=[wg_r, hT_r], writes=[pg_r])
                            for kc in range(KC):
                                sy.op("pe", lambda e, kc=kc, sub=sub, tb=tb, tw=tw: e.matmul(pu[:, 0:tw], lhsT=wu[:, kc, sub * 128:(sub + 1) * 128], rhs=hT[:, kc, tb:tb + tw],
                                                                                         start=(kc == 0), stop=(kc == KC - 1)), reads=[wu_r, hT_r], writes=[pu_r])
                            s_t, s_r = sil.next()
                            sy.op("act", lambda e, tw=tw, s_t=s_t: e.activation(out=s_t[:, 0:tw], in_=pg[:, 0:tw], func=AF.Silu), reads=[pg_r], writes=[s_r])
                            sy.op("dve", lambda e, tw=tw, s_t=s_t, fc=fc, tb=tb: e.tensor_tensor(out=actT[:, fc, tb:tb + tw], in0=pu[:, 0:tw], in1=s_t[:, 0:tw], op=ALU.mult),
                                  reads=[pu_r, s_r], writes=[actT_r])
                if cfg.stages == 4.3:
                    break
                dst = x_out_final if x_out_final is not None else x_out
                self.dense_residual(st, actT, actT_r, FC, Wd, T, t0, x_mid, dst, x_out_r, x_mid_r, l, g, 1, yscr, B_)
            sy.barrier()
        sy.q["pool"].ring_limit = 2
        return x_out_r

    def layer1(self, x1, x1_r, c_w_in, qkn_bc, rope, cache_k, cache_v, QT, KT, Vs, oT, out_k, out_v):
        sy, cfg = self.sy, self.cfg
        QT_r, KT_r, Vs_r = Res("QT", multi=True), Res("KT", multi=True), Res("Vs", multi=True)
        okv_r = Res("okv", multi=True)
        sy.q["pool"].ring_limit = 5
        with contextlib.ExitStack() as st:
            self.setup_dense_bufs(st)
            hT = self.sb(st, "hT3", [128, KC, 512], BF16); hT_r = Res("hT3")
            qn = self.sb(st, "qn_bc", [128, 2, 128]); cst_r = Res("l1c")
            sy.dma("sp", qn[:], qkn_bc[:, :, :], writes=[cst_r])
            qt = Ring([self.sb(st, f"l1q{i}", [128, 4, 128]) for i in range(2)], "l1q")
            sq = Ring([self.sb(st, f"l1sq{i}", [128, 4, 128]) for i in range(2)], "l1sq")
            rs = Ring([self.sb(st, f"l1rs{i}", [128, 8]) for i in range(3)], "l1rs")
            cs_t = Ring([self.sb(st, f"l1cs{i}", [128, 2, 64]) for i in range(2)], "l1cs")
            rt = Ring([self.sb(st, f"l1rt{i}", [128, 4, 2, 2, 32]) for i in range(2)], "l1rt")
            r2 = Ring([self.sb(st, f"l1r2{i}", [128, 4, 2, 32]) for i in range(4)], "l1r2")
            qb = Ring([self.sb(st, f"l1qb{i}", [128, 4, 128], BF16) for i in range(2)], "l1qb")
            tq = Ring([self.sb(st, f"l1tq{i}", [128, 4, 128]) for i in range(3)], "l1tq")
            for (t0, T, g) in self.blocks():
                self.make_hT(st, x1[t0:t0 + T, :], T, 1, g, 0, hT, hT_r, self.ident_bf, self.ident_r, src_r=x1_r)
                for c6 in range(6):
                    wt, wt_r = self.wload(c_w_in, c6 * 512)
                    for i in range(T // 128):
                        tok = t0 + i * 128
                        pt, pt_r = self.psum.next()
                        for kc in range(KC):
                            sy.op("pe", lambda e, kc=kc, i=i: e.matmul(pt[:], lhsT=hT[:, kc, i * 128:(i + 1) * 128], rhs=wt[:, kc, :], start=(kc == 0), stop=(kc == KC - 1)),
                                  reads=[hT_r, wt_r], writes=[pt_r])
                        q_t, q_r = qt.next()
                        p3 = pt[:].rearrange("p (h d) -> p h d", d=128)
                        if c6 == 5:
                            sy.op("act", lambda e, q_t=q_t: e.activation(out=q_t[:].rearrange("p h d -> p (h d)"), in_=pt[:], func=AF.Copy), reads=[pt_r], writes=[q_r])
                            sy.dma("sp", Vs[tok:tok + 128, :], q_t[:].rearrange("p h d -> p (h d)"), reads=[q_r], writes=[Vs_r])
                            if g == 0:
                                si, tl = tok // cfg.LP, tok % cfg.LP
                                sy.dma("sp", out_v[si, tl:tl + 128, :], q_t[:].rearrange("p h d -> p (h d)"), reads=[q_r], writes=[okv_r])
                            continue
                        s_t, s_r = sq.next(); r_t, r_r = rs.next()
                        sy.op("act", lambda e, s_t=s_t: e.activation(out=s_t[:].rearrange("p h d -> p (h d)"), in_=pt[:], func=AF.Square), reads=[pt_r], writes=[s_r])
                        sy.op("dve", lambda e, s_t=s_t, r_t=r_t: e.tensor_reduce(out=r_t[:, 0:4], in_=s_t[:], axis=AX.X, op=ALU.add), reads=[s_r], writes=[r_r])
                        sy.op("dve", lambda e, r_t=r_t: e.tensor_scalar(out=r_t[:, 0:4], in0=r_t[:, 0:4], scalar1=1.0 / 128, scalar2=RMS_EPS, op0=ALU.mult, op1=ALU.add), reads=[r_r], writes=[r_r])
                        sy.op("act", lambda e, r_t=r_t: e.activation(out=r_t[:, 0:4], in_=r_t[:, 0:4], func=AF.Sqrt), reads=[r_r], writes=[r_r])
                        sy.op("dve", lambda e, r_t=r_t: e.reciprocal(out=r_t[:, 4:8], in_=r_t[:, 0:4]), reads=[r_r], writes=[r_r])
                        sy.op("dve", lambda e, q_t=q_t, r_t=r_t: e.tensor_tensor(out=q_t[:], in0=p3, in1=r_t[:, 4:8].unsqueeze(2).broadcast_to([128, 4, 128]), op=ALU.mult),
                              reads=[pt_r, r_r], writes=[q_r])
                        gi = 0 if c6 < 4 else 1
                        sy.op("pool", lambda e, q_t=q_t, gi=gi: e.tensor_tensor(out=q_t[:], in0=q_t[:], in1=qn[:, gi:gi + 1, :].broadcast_to([128, 4, 128]), op=ALU.mult),
                              reads=[q_r, cst_r], writes=[q_r])
                        if g == 0 and c6 == 4:
                            si, tl = tok // cfg.LP, tok % cfg.LP
                            sy.dma("sp", out_k[si, tl:tl + 128, :], q_t[:].rearrange("p h d -> p (h d)"), reads=[q_r], writes=[okv_r])
                        qb_t, qb_r = qb.next()
                        if g == 1:
                            c_t, c_r = cs_t.next()
                            sy.dma("act", c_t[:], rope[tok - cfg.TP:tok - cfg.TP + 128, :, :], writes=[c_r])
                            q5 = q_t[:].rearrange("p h (a f e) -> p h a f e", a=2, f=2)
                            cosb = c_t[:, 0, :].rearrange("p (a e) -> p a e", a=2).unsqueeze(1).broadcast_to([128, 4, 2, 32])
                            sinb = c_t[:, 1, :].rearrange("p (a e) -> p a e", a=2).unsqueeze(1).broadcast_to([128, 4, 2, 32])
                            x1v, x2v = q5[:, :, :, 0, :], q5[:, :, :, 1, :]
                            a1, a1_r = r2.next(); a2, a2_r = r2.next(); a3, a3_r = r2.next(); a4, a4_r = r2.next()
                            sy.op("dve", lambda e, a1=a1: e.tensor_tensor(out=a1[:], in0=x1v, in1=cosb, op=ALU.mult), reads=[q_r, c_r], writes=[a1_r])
                            sy.op("pool", lambda e, a2=a2: e.tensor_tensor(out=a2[:], in0=x2v, in1=sinb, op=ALU.mult), reads=[q_r, c_r], writes=[a2_r])
                            sy.op("dve", lambda e, a3=a3: e.tensor_tensor(out=a3[:], in0=x2v, in1=cosb, op=ALU.mult), reads=[q_r, c_r], writes=[a3_r])
                            sy.op("pool", lambda e, a4=a4: e.tensor_tensor(out=a4[:], in0=x1v, in1=sinb, op=ALU.mult), reads=[q_r, c_r], writes=[a4_r])
                            qb5 = qb_t[:].rearrange("p h (a f e) -> p h a f e", a=2, f=2)
                            sy.op("dve", lambda e, a1=a1, a2=a2: e.tensor_tensor(out=qb5[:, :, :, 0, :], in0=a1[:], in1=a2[:], op=ALU.subtract), reads=[a1_r, a2_r], writes=[qb_r])
                            sy.op("dve", lambda e, a3=a3, a4=a4: e.tensor_tensor(out=qb5[:, :, :, 1, :], in0=a3[:], in1=a4[:], op=ALU.add), reads=[a3_r, a4_r], writes=[qb_r])
                        else:
                            sy.op("dve", lambda e, qb_t=qb_t, q_t=q_t: e.tensor_copy(out=qb_t[:], in_=q_t[:]), reads=[q_r], writes=[qb_r])
                        pT, pT_r = self.psum.next()
                        pTb = pT[:].bitcast(BF16)
                        for hh in range(4):
                            sy.op("pe", lambda e, hh=hh, qb_t=qb_t: e.transpose(pTb[:, hh * 128:(hh + 1) * 128], qb_t[:, hh, :], self.ident_bf[:]), reads=[qb_r, self.ident_r], writes=[pT_r])
                        tq_t, tq_r = tq.next()
                        sy.op("act", lambda e, tq_t=tq_t: e.activation(out=tq_t[:].rearrange("p h d -> p (h d)"), in_=pTb[:, 0:512], func=AF.Copy), reads=[pT_r], writes=[tq_r])
                        if c6 < 4:
                            sy.dma("sp", QT[c6 * 4:(c6 + 1) * 4, :, tok:tok + 128].rearrange("h d t -> d h t"), tq_t[:], reads=[tq_r], writes=[QT_r])
                        else:
                            sy.dma("sp", KT[:, :, tok:tok + 128].rearrange("h d t -> d h t"), tq_t[:], reads=[tq_r], writes=[KT_r])
            sy.barrier()
        sy.q["pool"].ring_limit = 2
        self.psum = self.psum_small
        with contextlib.ExitStack() as st:
            Lmax = max(cfg.LP, cfg.LS + cfg.PL)
            NKT = Lmax // 128
            KTs = self.sb(st, "a_KT", [128, Lmax], BF16); KTs_r = Res("a_KT")
            Va = self.sb(st, "a_Va", [128, NKT, 130], BF16); Va_r = Res("a_Va")
            ck = Ring([self.sb(st, f"a_ck{i}", [128, 128], BF16) for i in range(2)], "a_ck")
            Qc = Ring([self.sb(st, f"a_Qc{i}", [128, 4, 128], BF16) for i in range(2)], "a_Qc")
            Pt = Ring([self.sb(st, f"a_Pt{i}", [128, 512], BF16) for i in range(3)], "a_Pt")
            ob = Ring([self.sb(st, f"a_ob{i}", [128, 4, 128]) for i in range(2)], "a_ob")
            rc = Ring([self.sb(st, f"a_rc{i}", [128, 4]) for i in range(2)], "a_rc")
            ot = Ring([self.sb(st, f"a_ot{i}", [128, 4, 128]) for i in range(2)], "a_ot")
            sy.op("dve", lambda e: e.memset(Va[:], 1.0), writes=[Va_r])
            for (t0, L, g, si) in self.seqs():
                PLs = cfg.PL if g == 1 else 0
                nkt = (L + PLs) // 128
                for hk in range(4):
                    if g == 1:
                        for i in range(PLs // 128):
                            c_t, c_r = ck.next()
                            sy.dma("pool", c_t[:], cache_k[i * 128:(i + 1) * 128, hk, :], writes=[c_r])
                            pT, pT_r = self.psum.next()
                            pTb = pT[:].bitcast(BF16)
                            sy.op("pe", lambda e, c_t=c_t: e.transpose(pTb[:, 0:128], c_t[:], self.ident_bf[:]), reads=[c_r, self.ident_r], writes=[pT_r])
                            sy.op("act", lambda e, i=i: e.activation(out=KTs[:, i * 128:(i + 1) * 128], in_=pTb[:, 0:128], func=AF.Copy), reads=[pT_r], writes=[KTs_r])
                        sy.dma("pool", Va[:, 0:PLs // 128, 0:128], cache_v[:, hk, :].rearrange("(c p) d -> p c d", p=128), writes=[Va_r])
                    sy.dma("pool", KTs[:, PLs:PLs + L], KT[hk, :, t0:t0 + L], reads=[KT_r], writes=[KTs_r])
                    sy.dma("pool", Va[:, PLs // 128:nkt, 0:128], Vs[t0:t0 + L, hk * 128:(hk + 1) * 128].rearrange("(c p) d -> p c d", p=128), reads=[Vs_r], writes=[Va_r])
                    for qi in range(L // 128):
                        Q_t, Q_r = Qc.next()
                        sy.dma("pool", Q_t[:], QT[hk * 4:(hk + 1) * 4, :, t0 + qi * 128:t0 + (qi + 1) * 128].rearrange("h d t -> d h t"), reads=[QT_r], writes=[Q_r])
                        accs = [(self.psa[i], self.psa_r[i]) for i in range(4)]
                        for kt in range(nkt):
                            pS, pS_r = self.psum.next()
                            sy.op("pe", lambda e, kt=kt, Q_t=Q_t: e.matmul(pS[:], lhsT=KTs[:, kt * 128:(kt + 1) * 128], rhs=Q_t[:].rearrange("p h d -> p (h d)"), start=True, stop=True),
                                  reads=[KTs_r, Q_r], writes=[pS_r])
                            P_t, P_r = Pt.next()
                            sy.op("act", lambda e, P_t=P_t: e.activation(out=P_t[:], in_=pS[:], func=AF.Exp, scale=128 ** -0.5), reads=[pS_r], writes=[P_r])
                            for hq in range(4):
                                acc, acc_r = accs[hq]
                                sy.op("pe", lambda e, hq=hq, kt=kt, P_t=P_t, acc=acc: e.matmul(acc[:, 0:129], lhsT=P_t[:, hq * 128:(hq + 1) * 128], rhs=Va[:, kt, 0:129],
                                                                                           start=(kt == 0), stop=(kt == nkt - 1)), reads=[P_r, Va_r], writes=[acc_r])
                        o_t, o_r = ob.next(); r_t, r_r = rc.next()
                        for hq in range(4):
                            acc, acc_r = accs[hq]
                            c0 = 0
                            sy.op("dve", lambda e, hq=hq, acc=acc, c0=c0, r_t=r_t: e.reciprocal(out=r_t[:, hq:hq + 1], in_=acc[:, c0 + 128:c0 + 129]), reads=[acc_r, r_r], writes=[r_r])
                            sy.op("dve", lambda e, hq=hq, acc=acc, c0=c0, r_t=r_t, o_t=o_t: e.tensor_scalar(out=o_t[:, hq, :], in0=acc[:, c0:c0 + 128], scalar1=r_t[:, hq:hq + 1], scalar2=None, op0=ALU.mult),
                                  reads=[acc_r, r_r], writes=[o_r])
                        pT, pT_r = self.psum.next()
                        for hq in range(4):
                            sy.op("pe", lambda e, hq=hq, o_t=o_t: e.transpose(pT[:, hq * 128:(hq + 1) * 128], o_t[:, hq, :], self.ident_f[:]), reads=[o_r, self.ident_f_r], writes=[pT_r])
                        ot_t, ot_r = ot.next()
                        sy.op("act", lambda e, ot_t=ot_t: e.activation(out=ot_t[:].rearrange("p h d -> p (h d)"), in_=pT[:], func=AF.Copy), reads=[pT_r], writes=[ot_r])
                        sy.dma("sp", oT[hk * 512:(hk + 1) * 512, t0 + qi * 128:t0 + (qi + 1) * 128].rearrange("(h d) t -> d h t", d=128), ot_t[:], reads=[ot_r], writes=[self.oT_r])
            sy.barrier()
        self.psum = self.psum_big

    def setup_dense_bufs(self, st):
        self.xring = Ring([self.sb(st, f"xt{i}", [128, D]) for i in range(2)], "xt")
        self.xbring = Ring([self.sb(st, f"xb{i}", [128, D], BF16) for i in range(2)], "xb")
        self.ssring = Ring([self.sb(st, f"ss{i}", [128, 4]) for i in range(4)], "ss")
        self.wring = Ring([self.sb(st, f"wt{i}", [128, KC, 512], BF16) for i in range(4)], "wt")


def rope_tables(LS):
    t = np.arange(LS)
    row = (t // 64).astype(np.float32); col = (t % 64).astype(np.float32)
    freqs = np.power(np.float32(10000.0), -np.arange(0, 64, 2, dtype=np.float32) / np.float32(64)).astype(np.float32)
    ang = np.concatenate([row[:, None] * freqs[None], col[:, None] * freqs[None]], axis=1).astype(np.float32)
    return np.ascontiguousarray(np.stack([np.cos(ang), np.sin(ang)], axis=1).astype(np.float32))


def build_inputs(cfg, core, I):
    NP, LP = cfg.NP, cfg.LP
    nb = I["x_sample"].shape[0]
    b = core % nb
    xp = I["x_prompt"][core * NP:(core + 1) * NP].reshape(NP * LP, D)
    xs = I["x_sample"][b]
    m = {}
    m["x_all"] = np.ascontiguousarray(np.concatenate([xp, xs], axis=0))
    ct = np.stack([I["c_ctx"].reshape(KC, 128).T, I["c"][b].reshape(KC, 128).T], axis=-1)
    m["condT"] = np.ascontiguousarray(ct)
    m["ada_w"] = I["ada_w"]
    m["ada_bT"] = np.ascontiguousarray(I["ada_b"].reshape(2, 96, 128).transpose(0, 2, 1))
    m["norm_gT"] = np.ascontiguousarray(I["norm_g"].reshape(2, 4, KC, 128).transpose(0, 3, 1, 2))
    m["ab_w_in"] = I["ab_w_in"][0]
    m["ret_ld_bc"] = np.ascontiguousarray(np.broadcast_to(I["ret_log_decay"][0].reshape(1, 16), (128, 16)))
    m["convw"] = np.ascontiguousarray(I["rwkv_conv_w"][0].T.reshape(27, 128, 3).transpose(1, 0, 2))
    f8 = lambda v: np.ascontiguousarray(np.asarray(v).reshape(8, 128).T)
    m["w0T"] = np.ascontiguousarray(np.stack([f8(I["rwkv_w0"][0, d]) for d in range(2)], axis=1))
    m["a0T"] = np.ascontiguousarray(np.stack([f8(I["rwkv_a0"][0, d]) for d in range(2)], axis=1))
    m["k_kT"] = f8(I["rwkv_k_k"][0]); m["k_aT"] = f8(I["rwkv_k_a"][0]); m["r_kT"] = f8(I["rwkv_r_k"][0])
    pairbc = lambda v: np.ascontiguousarray(np.broadcast_to(np.asarray(v).reshape(8, 2, 1, 64), (8, 2, 64, 64)).transpose(1, 2, 0, 3).reshape(128, 8, 64))
    m["lnw_bc"] = pairbc(I["rwkv_ln_w"][0]); m["lnb_bc"] = pairbc(I["rwkv_ln_b"][0])
    m["w_up"] = np.ascontiguousarray(I["rwkv_w_up"][0].reshape(128, 1024)); m["a_up"] = np.ascontiguousarray(I["rwkv_a_up"][0].reshape(128, 1024))
    m["g_up"] = I["rwkv_g_up"][0]
    m["st_rwkv"] = np.ascontiguousarray(np.stack([I["state_rwkv_fwd"][b, 0], I["state_rwkv_bwd"][b, 0]]))
    m["ab_w_out"] = I["ab_w_out"][0]; m["c_w_in"] = I["c_w_in"][0]; m["c_w_out"] = I["c_w_out"][0]
    m["ffn_w_gate"] = I["ffn_w_gate"]; m["ffn_w_up"] = I["ffn_w_up"]; m["ffn_w_down"] = I["ffn_w_down"]
    m["qkn_bc"] = np.ascontiguousarray(np.broadcast_to(np.stack([I["c_q_norm"][0], I["c_k_norm"][0]])[None], (128, 2, 128)))
    m["rope"] = rope_tables(cfg.LS)
    m["cache_k"] = I["cache_k"][b, 0]; m["cache_v"] = I["cache_v"][b, 0]
    m["st_ret"] = np.ascontiguousarray(np.stack([I["state_ret_fwd"][b, 0], I["state_ret_bwd"][b, 0]]))
    return m


def run(I, cfg, ncores=8):
    b = B(cfg)
    nc = b.build()
    in_maps = []
    for core in range(ncores):
        m = build_inputs(cfg, core, I)
        in_maps.append({k: v for k, v in m.items() if k in b.ins})
    res = run_bass_kernel_spmd(nc, in_maps, core_ids=list(range(ncores)))
    return res.results


def kernel(**I):
    cfg = Cfg()
    I = {k: np.asarray(v) for k, v in I.items()}
    r = run(I, cfg)
    NP, LP, TP = cfg.NP, cfg.LP, cfg.TP
    nb = I["x_sample"].shape[0]
    y_prompt = np.concatenate([r[c]["y_all"][:TP].reshape(NP, LP, D) for c in range(8)], 0)
    y_sample = np.stack([r[c]["y_all"][TP:] for c in range(nb)], 0)
    cat = lambda name, d: np.concatenate([r[c][name][d] for c in range(8)], 0)[:, None]
    new_k = np.concatenate([r[c]["out_k"].reshape(NP, LP, 4, 128) for c in range(8)], 0)[:, None]
    new_v = np.concatenate([r[c]["out_v"].reshape(NP, LP, 4, 128) for c in range(8)], 0)[:, None]
    outs = (y_prompt, y_sample, cat("out_ret", 0), cat("out_ret", 1), cat("out_rwkv", 0), cat("out_rwkv", 1), new_k, new_v)
    return tuple(np.ascontiguousarray(o, dtype=np.float32) for o in outs)
```
